# Optimizing a Trainium2 kernel written in Bass

```python
import jax, jax.numpy as jnp
from jax import lax
import numpy as np

D_MODEL = 1024
BATCH = 16
SEQ = 2048
DEPTH = 4

CTX_LEN = 256
GRID_W = 64
NA_HEADS = 8
NA_HEAD_DIM = 64
NA_WIN_R = 8
NA_WIN_C = 16
NA_WIDTH = NA_HEADS * NA_HEAD_DIM
RPB_R = 2 * NA_WIN_R - 1
RPB_C = 2 * NA_WIN_C - 1
MLA_HEADS = 8
MLA_Q_RANK = 256
MLA_KV_RANK = 128
MLA_NOPE_DIM = 64
MLA_ROPE_DIM = 32
MLA_V_DIM = 64
MLA_WIDTH = MLA_HEADS * MLA_V_DIM
ATTN_BLOCK = 128
ROPE_BASE = 10000.0
CONV_DIM = 512
CONV_WIDTH = 31
N_BRANCHES = 3
N_EXPERTS = 16
EXPERT_DIM = 1024
EC_CAPACITY_FACTOR = 2
LN_EPS = 1e-5
RMS_EPS = 1e-6
DEEPNORM_ALPHA = (2 * DEPTH) ** 0.25
DEEPNORM_BETA = (8 * DEPTH) ** -0.25
PROJ_SIZES = (NA_WIDTH, NA_WIDTH, NA_WIDTH, MLA_Q_RANK, MLA_KV_RANK, MLA_ROPE_DIM, 2 * CONV_DIM, N_BRANCHES * D_MODEL)
IN_COLS = sum(PROJ_SIZES)

kernel_name = "hybrid_na_mla_conformer_ecmoe_dit"


def _layer_norm(x, g, b):
    xf = x.astype(jnp.float32)
    mu = jnp.mean(xf, axis=-1, keepdims=True)
    var = jnp.mean(jnp.square(xf - mu), axis=-1, keepdims=True)
    return ((xf - mu) * lax.rsqrt(var + LN_EPS) * g + b).astype(x.dtype)


def _rms_norm(x, g):
    xf = x.astype(jnp.float32)
    return (xf * lax.rsqrt(jnp.mean(jnp.square(xf), axis=-1, keepdims=True) + RMS_EPS) * g).astype(x.dtype)


def _modulate(x, shift, scale):
    return x * (1 + scale) + shift


def _axial_angles(n_tokens):
    t = jnp.arange(n_tokens)
    row = (t // GRID_W).astype(jnp.float32)
    col = (t % GRID_W).astype(jnp.float32)
    n_freq = MLA_ROPE_DIM // 4
    inv_freq = ROPE_BASE ** (-jnp.arange(n_freq, dtype=jnp.float32) / n_freq)
    return row[:, None] * inv_freq, col[:, None] * inv_freq


def _rotate(x, ang):
    x1, x2 = jnp.split(x, 2, axis=-1)
    cos = jnp.cos(ang).astype(x.dtype)
    sin = jnp.sin(ang).astype(x.dtype)
    return jnp.concatenate([x1 * cos - x2 * sin, x2 * cos + x1 * sin], axis=-1)


def _rope_2d(x, ang_row, ang_col):
    xr, xc = jnp.split(x, 2, axis=-1)
    return jnp.concatenate([_rotate(xr, ang_row), _rotate(xc, ang_col)], axis=-1)


def _split_proj(p):
    offsets = np.cumsum(PROJ_SIZES)[:-1].tolist()
    return jnp.split(p, offsets, axis=-1)


def _heads(t, n_heads):
    return t.reshape(t.shape[0], t.shape[1], n_heads, t.shape[2] // n_heads)


def _dense_attention(q, k, v, scale):
    s = jnp.einsum('bqhd,bkhd->bhqk', q, k).astype(jnp.float32) * scale
    p = jax.nn.softmax(s, axis=-1).astype(v.dtype)
    o = jnp.einsum('bhqk,bkhd->bqhd', p, v)
    return o.reshape(o.shape[0], o.shape[1], -1)


def _blocked_attention(q, k, v, scale):
    B, S, H, Dq = q.shape
    nb = S // ATTN_BLOCK
    qb = q.reshape(B, nb, ATTN_BLOCK, H, Dq).transpose(1, 0, 2, 3, 4)

    def block(q_blk):
        s = jnp.einsum('bqhd,bkhd->bhqk', q_blk, k).astype(jnp.float32) * scale
        p = jax.nn.softmax(s, axis=-1).astype(v.dtype)
        return jnp.einsum('bhqk,bkhd->bqhd', p, v)

    o = lax.map(block, qb)
    return o.transpose(1, 0, 2, 3, 4).reshape(B, S, -1)


def _neighbourhood_attention(q, k, v, k_ctx, v_ctx, rpb):
    B, S, H, Dh = q.shape
    rows = S // GRID_W
    win_r = min(NA_WIN_R, rows)
    scale = Dh ** -0.5
    qg = q.reshape(B, rows, GRID_W, H, Dh)
    kg = k.reshape(B, rows, GRID_W, H, Dh)
    vg = v.reshape(B, rows, GRID_W, H, Dh)
    col = jnp.arange(GRID_W)
    c_start = jnp.clip(col - NA_WIN_C // 2, 0, GRID_W - NA_WIN_C)
    col_mask = (col[None, :] >= c_start[:, None]) & (col[None, :] < c_start[:, None] + NA_WIN_C)
    dc_idx = jnp.clip(col[None, :] - col[:, None], -(NA_WIN_C - 1), NA_WIN_C - 1) + NA_WIN_C - 1
    n_win = win_r * GRID_W

    def row_block(r):
        r_start = jnp.clip(r - win_r // 2, 0, rows - win_r)
        q_r = lax.dynamic_index_in_dim(qg, r, axis=1, keepdims=False)
        k_band = lax.dynamic_slice_in_dim(kg, r_start, win_r, axis=1)
        v_band = lax.dynamic_slice_in_dim(vg, r_start, win_r, axis=1)
        dr_idx = r_start + jnp.arange(win_r) - r + NA_WIN_R - 1
        bias = rpb[:, dr_idx[None, :, None], dc_idx[:, None, :]]
        s_win = jnp.einsum('bqhd,brkhd->bhqrk', q_r, k_band).astype(jnp.float32) * scale + bias
        s_win = jnp.where(col_mask[:, None, :], s_win, -jnp.inf).reshape(B, H, GRID_W, n_win)
        s_ctx = jnp.einsum('bqhd,bkhd->bhqk', q_r, k_ctx).astype(jnp.float32) * scale
        p = jax.nn.softmax(jnp.concatenate([s_win, s_ctx], axis=-1), axis=-1).astype(v.dtype)
        p_win = p[..., :n_win].reshape(B, H, GRID_W, win_r, GRID_W)
        p_ctx = p[..., n_win:]
        return (jnp.einsum('bhqrk,brkhd->bqhd', p_win, v_band)
                + jnp.einsum('bhqk,bkhd->bqhd', p_ctx, v_ctx))

    o = lax.map(row_block, jnp.arange(rows))
    return o.transpose(1, 0, 2, 3, 4).reshape(B, S, H * Dh)


def _mla_queries(c_q, g_q, w_uq, ang_row, ang_col):
    B, N, _ = c_q.shape
    q = (_rms_norm(c_q, g_q) @ w_uq).reshape(B, N, MLA_HEADS, MLA_NOPE_DIM + MLA_ROPE_DIM)
    if ang_row is None:
        return q
    q_rope = _rope_2d(q[..., MLA_NOPE_DIM:], ang_row[:, None, :], ang_col[:, None, :])
    return jnp.concatenate([q[..., :MLA_NOPE_DIM], q_rope], axis=-1)


def _mla_keys_values(c_kv, k_rope, g_kv, w_ukv, ang_row, ang_col):
    B, N, _ = c_kv.shape
    kv = (_rms_norm(c_kv, g_kv) @ w_ukv).reshape(B, N, MLA_HEADS, MLA_NOPE_DIM + MLA_V_DIM)
    k_nope, v = kv[..., :MLA_NOPE_DIM], kv[..., MLA_NOPE_DIM:]
    k_rope = k_rope[:, :, None, :]
    if ang_row is not None:
        k_rope = _rope_2d(k_rope, ang_row[:, None, :], ang_col[:, None, :])
    k = jnp.concatenate([k_nope, jnp.broadcast_to(k_rope, (B, N, MLA_HEADS, MLA_ROPE_DIM))], axis=-1)
    return k, v


def _conformer_conv(u2, w_dw, b_dw, g_cn, b_cn, w_pw2):
    a, gate = jnp.split(u2, 2, axis=-1)
    u = a * jax.nn.sigmoid(gate)
    u = lax.conv_general_dilated(u, w_dw[:, None, :], window_strides=(1,),
                                 padding=[(CONV_WIDTH // 2, CONV_WIDTH // 2)],
                                 dimension_numbers=('NWC', 'WIO', 'NWC'),
                                 feature_group_count=u.shape[-1]) + b_dw
    u = jax.nn.silu(_layer_norm(u, g_cn, b_cn))
    return u @ w_pw2


def _merge(y_a, y_b, y_c, gate_logits, w_oa, w_ob, w_out):
    g = jax.nn.sigmoid(gate_logits.reshape(gate_logits.shape[:-1] + (N_BRANCHES, D_MODEL)))
    m = g[..., 0, :] * (y_a @ w_oa) + g[..., 1, :] * (y_b @ w_ob) + g[..., 2, :] * y_c
    return m @ w_out


def _mixer(h_lat, h_ctx, lp, ang_row, ang_col, with_ctx_out):
    qa, ka, va, cq, ckv, krope, conv_in, gates = _split_proj(h_lat @ lp['w_in'])
    qa_c, ka_c, va_c, cq_c, ckv_c, krope_c, conv_in_c, gates_c = _split_proj(h_ctx @ lp['w_in'])
    ka_c, va_c = _heads(ka_c, NA_HEADS), _heads(va_c, NA_HEADS)
    k_bc, v_bc = _mla_keys_values(ckv_c, krope_c, lp['g_kv'], lp['w_ukv'], None, None)
    mla_scale = (MLA_NOPE_DIM + MLA_ROPE_DIM) ** -0.5
    y_a = _neighbourhood_attention(_heads(qa, NA_HEADS), _heads(ka, NA_HEADS), _heads(va, NA_HEADS),
                                   ka_c, va_c, lp['rpb'])
    q_b = _mla_queries(cq, lp['g_q'], lp['w_uq'], ang_row, ang_col)
    k_b, v_b = _mla_keys_values(ckv, krope, lp['g_kv'], lp['w_ukv'], ang_row, ang_col)
    y_b = _blocked_attention(q_b, jnp.concatenate([k_bc, k_b], axis=1), jnp.concatenate([v_bc, v_b], axis=1), mla_scale)
    y_c = _conformer_conv(conv_in, lp['w_dw'], lp['b_dw'], lp['g_cn'], lp['b_cn'], lp['w_pw2'])
    y_lat = _merge(y_a, y_b, y_c, gates, lp['w_oa'], lp['w_ob'], lp['w_out'])
    if not with_ctx_out:
        return y_lat, None
    y_a_c = _dense_attention(_heads(qa_c, NA_HEADS), ka_c, va_c, NA_HEAD_DIM ** -0.5)
    q_bc = _mla_queries(cq_c, lp['g_q'], lp['w_uq'], None, None)
    y_b_c = _dense_attention(q_bc, k_bc, v_bc, mla_scale)
    y_c_c = _conformer_conv(conv_in_c, lp['w_dw'], lp['b_dw'], lp['g_cn'], lp['b_cn'], lp['w_pw2'])
    y_ctx = _merge(y_a_c, y_b_c, y_c_c, gates_c, lp['w_oa'], lp['w_ob'], lp['w_out'])
    return y_lat, y_ctx


def _expert_choice_moe(h, w_router, w_gate, w_up, w_down):
    B, N, _ = h.shape
    cap = EC_CAPACITY_FACTOR * N // N_EXPERTS
    aff = jax.nn.softmax((h @ w_router).astype(jnp.float32), axis=-1)
    top_aff, top_idx = lax.top_k(aff.transpose(0, 2, 1), cap)
    b_idx = jnp.arange(B)[:, None, None]
    xg = h[b_idx, top_idx]
    hid = jax.nn.silu(jnp.einsum('becd,edf->becf', xg, w_gate)) * jnp.einsum('becd,edf->becf', xg, w_up)
    y = jnp.einsum('becf,efd->becd', hid, w_down) * top_aff[..., None].astype(h.dtype)
    return jnp.zeros_like(h).at[b_idx, top_idx].add(y)


def setup_inputs(seed: int = 0) -> dict:
    key = jax.random.key(seed)
    ks = iter(jax.random.split(key, 32))

    def nrm(shape, scale):
        return jax.random.normal(next(ks), shape, jnp.float32) * scale

    def gain(shape):
        return 1.0 + nrm(shape, 0.01)

    L, D = DEPTH, D_MODEL
    return {
        "x": nrm((BATCH, SEQ, D), 1.0),
        "c": nrm((BATCH, D), 1.0),
        "ctx": nrm((BATCH, CTX_LEN, D), 1.0),
        "c_ctx": nrm((D,), 1.0),
        "w_ada": nrm((L, D, 6 * D), 0.5 * D ** -0.5),
        "b_ada": nrm((L, 6 * D), 0.01),
        "w_in": nrm((L, D, IN_COLS), D ** -0.5),
        "g_q": gain((L, MLA_Q_RANK)),
        "w_uq": nrm((L, MLA_Q_RANK, MLA_HEADS * (MLA_NOPE_DIM + MLA_ROPE_DIM)), MLA_Q_RANK ** -0.5),
        "g_kv": gain((L, MLA_KV_RANK)),
        "w_ukv": nrm((L, MLA_KV_RANK, MLA_HEADS * (MLA_NOPE_DIM + MLA_V_DIM)), MLA_KV_RANK ** -0.5),
        "rpb": nrm((L, NA_HEADS, RPB_R, RPB_C), 0.1),
        "w_dw": nrm((L, CONV_WIDTH, CONV_DIM), CONV_WIDTH ** -0.5),
        "b_dw": nrm((L, CONV_DIM), 0.01),
        "g_cn": gain((L, CONV_DIM)),
        "b_cn": nrm((L, CONV_DIM), 0.01),
        "w_pw2": nrm((L, CONV_DIM, D), CONV_DIM ** -0.5),
        "w_oa": nrm((L, NA_WIDTH, D), NA_WIDTH ** -0.5),
        "w_ob": nrm((L, MLA_WIDTH, D), MLA_WIDTH ** -0.5),
        "w_out": nrm((L, D, D), DEEPNORM_BETA * D ** -0.5),
        "ln1_g": gain((L, D)),
        "ln1_b": nrm((L, D), 0.01),
        "w_router": nrm((L, D, N_EXPERTS), D ** -0.5),
        "w_gate": nrm((L, N_EXPERTS, D, EXPERT_DIM), D ** -0.5),
        "w_up": nrm((L, N_EXPERTS, D, EXPERT_DIM), D ** -0.5),
        "w_down": nrm((L, N_EXPERTS, EXPERT_DIM, D), DEEPNORM_BETA * EXPERT_DIM ** -0.5),
        "ln2_g": gain((L, D)),
        "ln2_b": nrm((L, D), 0.01),
    }


def reference(x, c, ctx, c_ctx, w_ada, b_ada, w_in, g_q, w_uq, g_kv, w_ukv, rpb, w_dw, b_dw, g_cn, b_cn,
              w_pw2, w_oa, w_ob, w_out, ln1_g, ln1_b, w_router, w_gate, w_up, w_down, ln2_g, ln2_b):
    ang_row, ang_col = _axial_angles(x.shape[1])
    for l in range(DEPTH):
        with_ctx_out = l < DEPTH - 1
        lp = dict(w_in=w_in[l], g_q=g_q[l], w_uq=w_uq[l], g_kv=g_kv[l], w_ukv=w_ukv[l], rpb=rpb[l],
                  w_dw=w_dw[l], b_dw=b_dw[l], g_cn=g_cn[l], b_cn=b_cn[l], w_pw2=w_pw2[l],
                  w_oa=w_oa[l], w_ob=w_ob[l], w_out=w_out[l])
        sh1, sc1, g1, sh2, sc2, g2 = jnp.split((jax.nn.silu(c) @ w_ada[l] + b_ada[l])[:, None, :], 6, axis=-1)
        sh1c, sc1c, g1c, sh2c, sc2c, g2c = jnp.split(jax.nn.silu(c_ctx) @ w_ada[l] + b_ada[l], 6, axis=-1)
        y_lat, y_ctx = _mixer(_modulate(x, sh1, sc1), _modulate(ctx, sh1c, sc1c), lp, ang_row, ang_col, with_ctx_out)
        x = _layer_norm(DEEPNORM_ALPHA * x + g1 * y_lat, ln1_g[l], ln1_b[l])
        x = _layer_norm(DEEPNORM_ALPHA * x + g2 * _expert_choice_moe(_modulate(x, sh2, sc2), w_router[l], w_gate[l], w_up[l], w_down[l]),
                        ln2_g[l], ln2_b[l])
        if with_ctx_out:
            ctx = _layer_norm(DEEPNORM_ALPHA * ctx + g1c * y_ctx, ln1_g[l], ln1_b[l])
            ctx = _layer_norm(DEEPNORM_ALPHA * ctx + g2c * _expert_choice_moe(_modulate(ctx, sh2c, sc2c), w_router[l], w_gate[l], w_up[l], w_down[l]),
                              ln2_g[l], ln2_b[l])
    return x
```

```python
import numpy as np
from contextlib import ExitStack
import concourse.bass as bass
import concourse.mybir as mybir
from concourse.bass_utils import run_bass_kernel_spmd

F32 = mybir.dt.float32
BF16 = mybir.dt.bfloat16
I32 = mybir.dt.int32
U32 = mybir.dt.uint32
AF = mybir.ActivationFunctionType
ALU = mybir.AluOpType

NCORES = 8
NS = 2
D = 1024
T = 2048
C = 256
TT = C + T
L = 4
NE = 16
IN_COLS = 6048
ALPHA = float((2 * L) ** 0.25)
LN_EPS = 1e-5
RMS_EPS = 1e-6
NEG = -30000.0
BLOCKS = [(0, 256), (256, 512), (768, 512), (1280, 512), (1792, 512)]
V_BADA = 0
V_LN1G = 48
V_LN1B = 56
V_LN2G = 64
V_LN2B = 72
V_GQ = 80
V_GKV = 82
V_BDW = 83
V_GCN = 87
V_BCN = 91
V_WDW = 95
NV = 95 + 124


class Buf:
    __slots__ = ("name", "w", "r")

    def __init__(self, name=""):
        self.name = name
        self.w = None
        self.r = {}


class Tile:
    def __init__(self, t, name):
        self.t = t
        self.b = Buf(name)


class KB:
    NDMA = 32
    NPDMA = 16

    def __init__(self, nc, es):
        self.nc = nc
        self.engs = {"pe": nc.tensor, "act": nc.scalar, "dve": nc.vector, "pool": nc.gpsimd, "sp": nc.sync}
        self.sems = {}
        for e in ("pe", "act", "dve", "pool"):
            self.sems[e] = es.enter_context(nc.semaphore("s_" + e))
        self.cnt = {e: 0 for e in self.sems}
        self.dsem = [es.enter_context(nc.semaphore("s_dma%d" % i)) for i in range(self.NDMA + self.NPDMA)]
        self.duse = [0] * (self.NDMA + self.NPDMA)
        self.dnext = 0
        self.pnext = 0
        self.waited = {}
        self.dbufs = {}
        self.ninst = 0

    def _sem(self, key):
        if isinstance(key, tuple):
            return self.dsem[key[1]]
        return self.sems[key]

    def _wait(self, F, deps):
        eng = self.engs[F]
        for key, val in deps.items():
            if val <= 0:
                continue
            if F == "pe" and key == "pe":
                continue
            if self.waited.get((F, key), 0) >= val:
                continue
            eng.wait_ge(self._sem(key), val)
            self.waited[(F, key)] = val
            self.ninst += 1

    def _collect(self, reads, writes):
        deps = {}

        def add(k, v):
            if deps.get(k, 0) < v:
                deps[k] = v

        for b in reads:
            if b.w is not None:
                add(*b.w)
        for b in writes:
            if b.w is not None:
                add(*b.w)
            for k, v in b.r.items():
                add(k, v)
        return deps

    def _mark(self, tok, reads, writes):
        k, v = tok
        for b in reads:
            if b.r.get(k, 0) < v:
                b.r[k] = v
        for b in writes:
            b.w = tok
            b.r = {}

    def op(self, F, fn, reads=(), writes=()):
        self._wait(F, self._collect(reads, writes))
        inst = fn(self.engs[F])
        self.cnt[F] += 1
        inst.then_inc(self.sems[F], 1)
        self._mark((F, self.cnt[F]), reads, writes)
        self.ninst += 1
        return inst

    def dma(self, Q, fn, reads=(), writes=()):
        if Q == "pool":
            i = self.NDMA + self.pnext
            self.pnext = (self.pnext + 1) % self.NPDMA
        else:
            i = self.dnext
            self.dnext = (self.dnext + 1) % self.NDMA
        key = ("d", i)
        deps = self._collect(reads, writes)
        prev = 16 * self.duse[i]
        if prev and deps.get(key, 0) < prev:
            deps[key] = prev
        self._wait(Q, deps)
        inst = fn(self.engs[Q])
        self.duse[i] += 1
        inst.then_inc(self.dsem[i], 16)
        self._mark((key, 16 * self.duse[i]), reads, writes)
        self.ninst += 1
        return inst

    def barrier(self):
        deps = {e: self.cnt[e] for e in self.cnt}
        for i in range(self.NDMA + self.NPDMA):
            if self.duse[i]:
                deps[("d", i)] = 16 * self.duse[i]
        for F in self.engs:
            self._wait(F, dict(deps))

    def dall(self, *prefix):
        return [b for key, b in self.dbufs.items() if key[:len(prefix)] == prefix]

    def dbuf(self, *key):
        b = self.dbufs.get(key)
        if b is None:
            b = Buf(str(key))
            self.dbufs[key] = b
        return b


class Ctx:
    pass


PHASE_LOG = []


def _plog(k, name):
    PHASE_LOG.append((name, dict(k.cnt)))


_UID = [0]


def _sb(nc, es, name, shape, dtype):
    _UID[0] += 1
    return Tile(es.enter_context(nc.sbuf_tensor("sb%d_%s" % (_UID[0], name), list(shape), dtype)), name)


class WStream:
    def __init__(self, k, g, units, pf=2, cast=True):
        self.k, self.g, self.units, self.pf, self.cast = k, g, units, pf, cast
        self.issued = 0
        self.slots = {}

    def _issue(self, i):
        k, g = self.k, self.g
        src, shape = self.units[i]
        st = g.wst[g.wst_i % len(g.wst)]
        g.wst_i += 1
        n = 1
        for d in shape[1:]:
            n *= d
        stv = st.t[:, 0:n]
        if len(shape) == 3:
            stv = stv.rearrange("p (a b) -> p a b", a=shape[1])
        k.dma("sp", lambda e: e.dma_start(out=stv, in_=src), reads=(), writes=[st.b])
        if not self.cast:
            self.slots[i] = (stv, st.b)
            return
        bf = g.wbf[g.wbf_i % len(g.wbf)]
        g.wbf_i += 1
        bfv = bf.t[:, 0:n]
        if len(shape) == 3:
            bfv = bfv.rearrange("p (a b) -> p a b", a=shape[1])
        ce = g.cast_engs[g.cast_i % len(g.cast_engs)]
        g.cast_i += 1
        if ce == "act":
            k.op("act", lambda e: e.copy(out=bfv, in_=stv), reads=[st.b], writes=[bf.b])
        else:
            k.op(ce, lambda e: e.tensor_copy(out=bfv, in_=stv), reads=[st.b], writes=[bf.b])
        self.slots[i] = (bfv, bf.b)

    def get(self, i):
        while self.issued < min(len(self.units), i + 1 + self.pf):
            self._issue(self.issued)
            self.issued += 1
        r = self.slots[i]
        if i - 1 in self.slots and i - 1 >= 0:
            pass
        return r


def mm(k, ps, out_ap, pairs, reads):
    n = len(pairs)
    for i, (l, r) in enumerate(pairs):
        k.op("pe", lambda e, l=l, r=r, i=i: e.matmul(out_ap, lhsT=l, rhs=r, start=(i == 0), stop=(i == n - 1)),
             reads=reads, writes=[ps.b])


def col_stats(k, g, zs, zb, N, nfeat, eps, mean_needed, sq, mean_t, rstd_t, z16=None, ps1=None, ps2=None, off1=0, off2=0):
    nch = len(zs)
    ps1 = g.ps[6] if ps1 is None else ps1
    ps2 = g.ps[7] if ps2 is None else ps2
    for c, z in enumerate(zs):
        k.op("act", lambda e, c=c, z=z: e.activation(out=sq.t[:, c, 0:N], in_=z, func=AF.Square), reads=[zb], writes=[sq.b])
    if mean_needed:
        for c, z in enumerate(zs):
            k.op("act", lambda e, c=c, z=z: e.copy(out=z16.t[:, c, 0:N], in_=z), reads=[zb], writes=[z16.b])
        mm(k, ps1, ps1.t[:, off1:off1 + N], [(g.ones_b.t[:, :], z16.t[:, c, 0:N]) for c in range(nch)], [z16.b, g.ones_b.b])
    mm(k, ps2, ps2.t[:, off2:off2 + N], [(g.ones_b.t[:, :], sq.t[:, c, 0:N]) for c in range(nch)], [sq.b, g.ones_b.b])
    inv = 1.0 / nfeat
    if mean_needed:
        k.op("act", lambda e: e.activation(out=mean_t.t[:, 0:N], in_=ps1.t[:, off1:off1 + N], func=AF.Copy, scale=inv),
             reads=[ps1.b], writes=[mean_t.b])
        k.op("dve", lambda e: e.tensor_tensor(out=rstd_t.t[:, 0:N], in0=mean_t.t[:, 0:N], in1=mean_t.t[:, 0:N], op=ALU.mult),
             reads=[mean_t.b], writes=[rstd_t.b])
        k.op("dve", lambda e: e.scalar_tensor_tensor(out=rstd_t.t[:, 0:N], in0=ps2.t[:, off2:off2 + N], scalar=inv, in1=rstd_t.t[:, 0:N],
                                                     op0=ALU.mult, op1=ALU.subtract),
             reads=[ps2.b, rstd_t.b], writes=[rstd_t.b])
        k.op("act", lambda e: e.activation(out=rstd_t.t[:, 0:N], in_=rstd_t.t[:, 0:N], func=AF.Sqrt, bias=g.eps_ln.t[:, 0:1], scale=1.0),
             reads=[rstd_t.b, g.eps_ln.b], writes=[rstd_t.b])
    else:
        k.op("act", lambda e: e.activation(out=rstd_t.t[:, 0:N], in_=ps2.t[:, off2:off2 + N], func=AF.Sqrt, bias=g.eps_rms.t[:, 0:1], scale=inv),
             reads=[ps2.b, g.eps_rms.b], writes=[rstd_t.b])
    k.op("dve", lambda e: e.reciprocal(out=rstd_t.t[:, 0:N], in_=rstd_t.t[:, 0:N]), reads=[rstd_t.b], writes=[rstd_t.b])


class AttJob:
    def __init__(self, k, g, N, q_ap, q_bufs, chunks, scale, hp, out_ap, out_buf, sbanks, obank, dbank, pts, rec):
        self.k, self.g, self.N, self.q_ap, self.q_bufs, self.chunks = k, g, N, q_ap, q_bufs, chunks
        self.scale, self.hp, self.out_ap, self.out_buf = scale, hp, out_ap, out_buf
        self.sbanks, self.obank, self.dbank, self.pts, self.rec = sbanks, obank, dbank, pts, rec
        G = max(1, 512 // N)
        self.groups = [(ci, chunks[ci:ci + G]) for ci in range(0, len(chunks), G)]
        self.ngroups = len(self.groups)
        self.sb = {}

    def qk(self, gi):
        k, g, N = self.k, self.g, self.N
        ci, grp = self.groups[gi]
        sb_ = self.sbanks[g.sb_i % len(self.sbanks)]
        g.sb_i += 1
        self.sb[gi] = sb_
        for j, ch in enumerate(grp):
            o = sb_.t[:, j * N:(j + 1) * N]
            has_b = ch.get("bias") is not None
            k.op("pe", lambda e, o=o, ch=ch, has_b=has_b: e.matmul(o, lhsT=ch["kT"], rhs=self.q_ap, start=True, stop=not has_b),
                 reads=list(ch["bufs"]) + list(self.q_bufs), writes=[sb_.b])
            if has_b:
                bl, br, bb = ch["bias"]
                k.op("pe", lambda e, o=o, bl=bl, br=br: e.matmul(o, lhsT=bl, rhs=br, start=False, stop=True), reads=bb, writes=[sb_.b])

    def rest(self, gi):
        k, g, N = self.k, self.g, self.N
        ci, grp = self.groups[gi]
        nchunks = len(self.chunks)
        sb_ = self.sb.pop(gi)
        pt = self.pts[g.pt_i % len(self.pts)]
        g.pt_i += 1
        w = len(grp) * N
        k.op("act", lambda e: e.activation(out=pt.t[:, 0:w], in_=sb_.t[:, 0:w], func=AF.Exp, scale=self.scale), reads=[sb_.b], writes=[pt.b])
        for j, ch in enumerate(grp):
            first, last = (ci + j == 0), (ci + j == nchunks - 1)
            p_ap = pt.t[:, j * N:(j + 1) * N]
            k.op("pe", lambda e, ch=ch, p_ap=p_ap, first=first, last=last: e.matmul(self.obank.t[:, 0:N], lhsT=ch["v"], rhs=p_ap, start=first, stop=last),
                 reads=list(ch["bufs"]) + [pt.b], writes=[self.obank.b])

    def fin(self):
        k, N, rec = self.k, self.N, self.rec
        r0 = self.hp * 64
        d0 = 64 - r0
        if N >= 256:
            k.op("act", lambda e: e.activation(out=rec.t[r0:r0 + 64, 0:N], in_=self.obank.t[d0:d0 + 64, 0:N], func=AF.Ln), reads=[self.obank.b], writes=[rec.b])
            k.op("act", lambda e: e.activation(out=rec.t[r0:r0 + 64, 0:N], in_=rec.t[r0:r0 + 64, 0:N], func=AF.Exp, scale=-1.0), reads=[rec.b], writes=[rec.b])
        else:
            k.op("dve", lambda e: e.reciprocal(out=rec.t[r0:r0 + 64, 0:N], in_=self.obank.t[d0:d0 + 64, 0:N]), reads=[self.obank.b], writes=[rec.b])
        k.op("dve", lambda e: e.tensor_tensor(out=self.out_ap, in0=self.obank.t[r0:r0 + 64, 0:N], in1=rec.t[r0:r0 + 64, 0:N], op=ALU.mult),
             reads=[self.obank.b, rec.b], writes=[self.out_buf])


def run_attention(jobs, side=None):
    flat = [(job, gi) for job in jobs for gi in range(job.ngroups)]
    if not flat:
        return
    LA = 2
    issued = 0
    pending = None
    for i, (job, gi) in enumerate(flat):
        while issued < min(len(flat), i + 1 + LA):
            flat[issued][0].qk(flat[issued][1])
            issued += 1
        job.rest(gi)
        if pending is not None:
            pending.fin()
            pending = None
        if side is not None:
            side()
        if gi == job.ngroups - 1:
            pending = job
    if pending is not None:
        pending.fin()


def attention(k, g, N, q_ap, q_bufs, chunks, scale, hp, out_ap, out_buf, sbanks, obank, dbank, pts, rec):
    return AttJob(k, g, N, q_ap, q_bufs, chunks, scale, hp, out_ap, out_buf, sbanks, obank, dbank, pts, rec)


def build(n_layers=L, dbg=(), stop_after=None):
    nc = bass.Bass("TRN2", target_bir_lowering=False)
    g = Ctx()

    def din(name, shape, dt=F32):
        return nc.dram_tensor(name, list(shape), dt, kind="ExternalInput").ap()

    def dscr(name, shape, dt):
        kind = "ExternalOutput" if name in dbg else "Internal"
        return nc.dram_tensor(name, list(shape), dt, kind=kind).ap()

    xT_in = din("xT", [NS, D, TT])
    cT_in = din("cT", [128, 8, 3])
    vecs_in = din("vecs", [L, 128, NV])
    ident_in = din("ident", [128, 128])
    ropeC_in = din("ropeC", [32, TT])
    ropeS_in = din("ropeS", [32, TT])
    biasT_rep = din("biasT", [L, 128, 7680])
    w_ada = din("w_ada", [L, D, 6 * D])
    w_in = din("w_in", [L, D, IN_COLS])
    w_kr = din("w_kr", [L, D, 192])
    w_uq = din("w_uq", [L, 256, 768])
    w_uqs = din("w_uq_sw", [L, 256, 768])
    w_ukv = din("w_ukv", [L, 128, 1024])
    w_pw2 = din("w_pw2", [L, 512, D])
    w_oa = din("w_oa", [L, 512, D])
    w_ob = din("w_ob", [L, 512, D])
    w_out = din("w_out", [L, D, D])
    w_router = din("w_router", [L, D, NE])
    w_gate = din("w_gate", [L, NE, D, D])
    w_up = din("w_up", [L, NE, D, D])
    w_down = din("w_down", [L, NE, D, D])
    outT = nc.dram_tensor("outT", [NS, D, T], F32, kind="ExternalOutput").ap()

    xres = dscr("xres", [NS, D, TT], F32)
    x1res = dscr("x1res", [NS, D, TT], F32)
    qaT_d = dscr("qaT_d", [NS, 512, TT], BF16)
    kaT_d = dscr("kaT_d", [NS, 512, TT], BF16)
    vae_d = dscr("vae_d", [NS, 18, 128, 768], BF16)
    vao_d = dscr("vao_d", [NS, 15, 128, 768], BF16)
    cq_d = dscr("cq_d", [NS, 256, TT], F32)
    ckv_d = dscr("ckv_d", [NS, 128, TT], F32)
    kr_d = dscr("kr_d", [NS, 2, 32, TT], F32)
    u_d = dscr("u_d", [NS, 512, TT], F32)
    gT_d = dscr("gT_d", [NS, 3072, TT], BF16)
    qbT_d = dscr("qbT_d", [NS, 8, 96, TT], BF16)
    kbT_d = dscr("kbT_d", [NS, 8, 96, TT], BF16)
    vb_d = dscr("vb_d", [NS, 18, 128, 768], BF16)
    yaT_d = dscr("yaT_d", [NS, 512, TT], BF16)
    ybT_d = dscr("ybT_d", [NS, 512, TT], BF16)
    ucT_d = dscr("ucT_d", [NS, 512, TT], BF16)
    h2tok_d = dscr("h2tok_d", [NS * TT, D], BF16)
    afftok_d = dscr("afftok_d", [NS * TT, NE], F32)
    moe_d = dscr("moe_d", [NS * TT, D], F32)
    mod_d = dscr("mod_d", [L, 128, 48, 3], F32)
    affT_d = dscr("affT_d", [NS, NE, TT], F32)

    with ExitStack() as es:
        k = KB(nc, es)
        g.k = k
        g.ps = [Tile(es.enter_context(nc.psum_tensor("ps%d" % i, [128, 512], F32)), "ps%d" % i) for i in range(8)]
        g.ident_f = _sb(nc, es, "ident_f", [128, 128], F32)
        g.ident_b = _sb(nc, es, "ident_b", [128, 128], BF16)
        g.ones_f = _sb(nc, es, "ones_f", [128, 128], F32)
        g.ones_b = _sb(nc, es, "ones_b", [128, 128], BF16)
        g.eps_ln = _sb(nc, es, "eps_ln", [128, 1], F32)
        g.eps_rms = _sb(nc, es, "eps_rms", [128, 1], F32)
        g.vecs = _sb(nc, es, "vecs", [128, L, NV], F32)
        g.mod = _sb(nc, es, "mod", [128, L, 48, 3], F32)
        g.wst_i = g.wbf_i = g.cast_i = 0
        g.cast_engs = ["act", "pool"]
        g.sb_i = g.pt_i = 0

        k.dma("sp", lambda e: e.dma_start(out=g.ident_f.t[:, :], in_=ident_in[:, :]), writes=[g.ident_f.b])
        k.op("dve", lambda e: e.tensor_copy(out=g.ident_b.t[:, :], in_=g.ident_f.t[:, :]), reads=[g.ident_f.b], writes=[g.ident_b.b])
        k.op("dve", lambda e: e.memset(g.ones_f.t[:, :], 1.0), writes=[g.ones_f.b])
        k.op("dve", lambda e: e.memset(g.ones_b.t[:, :], 1.0), writes=[g.ones_b.b])
        k.op("dve", lambda e: e.memset(g.eps_ln.t[:, :], LN_EPS), writes=[g.eps_ln.b])
        k.op("dve", lambda e: e.memset(g.eps_rms.t[:, :], RMS_EPS), writes=[g.eps_rms.b])
        for l in range(L):
            k.dma("sp", lambda e, l=l: e.dma_start(out=g.vecs.t[:, l, :], in_=vecs_in[l, :, :]), writes=[g.vecs.b])

        with ExitStack() as ph:
            ring(nc, g, ph, 3, 0)
            cT = _sb(nc, ph, "cT", [128, 8, 3], F32)
            k.dma("sp", lambda e: e.dma_start(out=cT.t[:, :, :], in_=cT_in[:, :, :]), writes=[cT.b])
            k.op("act", lambda e: e.activation(out=cT.t[:, :, :], in_=cT.t[:, :, :], func=AF.Silu), reads=[cT.b], writes=[cT.b])
            astg = _sb(nc, ph, "astg", [3, 6 * D], F32)
            for l in range(n_layers):
                units = [(w_ada[l, :, gi * 512:(gi + 1) * 512].rearrange("(kc p) n -> p kc n", p=128), (128, 8, 512)) for gi in range(12)]
                ws = WStream(k, g, units, pf=2, cast=False)
                for gi in range(12):
                    wv, wb = ws.get(gi)
                    ps = g.ps[gi % 4]
                    mm(k, ps, ps.t[0:3, 0:512], [(cT.t[:, kc, :], wv[:, kc, :]) for kc in range(8)], [wb, cT.b])
                    k.op("act", lambda e, ps=ps, gi=gi: e.copy(out=astg.t[:, gi * 512:(gi + 1) * 512], in_=ps.t[0:3, 0:512]), reads=[ps.b], writes=[astg.b])
                pt_ = g.ps[4 + (l % 2)]
                for oc in range(48):
                    k.op("pe", lambda e, oc=oc, pt_=pt_: e.transpose(pt_.t[:, oc * 3:(oc + 1) * 3], astg.t[0:3, oc * 128:(oc + 1) * 128], g.ident_f.t[0:3, 0:3]),
                         reads=[astg.b, g.ident_f.b], writes=[pt_.b])
                p3 = pt_.t[:, 0:144].rearrange("p (o j) -> p o j", j=3)
                for j in range(3):
                    k.op("dve", lambda e, j=j, l=l, p3=p3: e.tensor_tensor(out=g.mod.t[:, l, :, j], in0=p3[:, :, j], in1=g.vecs.t[:, l, V_BADA:V_BADA + 48], op=ALU.add),
                         reads=[pt_.b, g.vecs.b], writes=[g.mod.b])
                for r in (1, 4):
                    k.op("dve", lambda e, r=r, l=l: e.tensor_scalar(out=g.mod.t[:, l, r * 8:(r + 1) * 8, :], in0=g.mod.t[:, l, r * 8:(r + 1) * 8, :],
                                                                    scalar1=1.0, scalar2=None, op0=ALU.add), reads=[g.mod.b], writes=[g.mod.b])
            mm(k, g.ps[7], g.ps[7].t[0:64, 0:8], [(g.ones_b.t[:, 0:64], g.ones_b.t[:, 0:8])], [g.ones_b.b])
            if "mod_d" in dbg:
                for l in range(n_layers):
                    k.dma("sp", lambda e, l=l: e.dma_start(out=mod_d[l], in_=g.mod.t[:, l, :, :]), reads=[g.mod.b], writes=[k.dbuf("mod_d")])
            k.barrier()

        def modv(l, role, c, j):
            return g.mod.t[:, l, role * 8 + c, j:j + 1]

        if stop_after != "ada":
            for l in range(n_layers):
                layer(nc, k, g, l, locals())
        k.barrier()
    return nc


def ring(nc, g, ph, nst, nbf, cast_engs=("act", "pool", "dve")):
    g.cast_engs = list(cast_engs)
    g.wst = [_sb(nc, ph, "wst%d" % i, [128, 4096], F32) for i in range(nst)]
    g.wbf = [_sb(nc, ph, "wbf%d" % i, [128, 4096], BF16) for i in range(nbf)]


def load_cast(k, g, src, shape, dst_view, dst_buf, eng=None):
    st = g.wst[g.wst_i % len(g.wst)]
    g.wst_i += 1
    n = 1
    for d in shape[1:]:
        n *= d
    stv = st.t[0:shape[0], 0:n]
    if len(shape) == 3:
        stv = stv.rearrange("p (a b) -> p a b", a=shape[1])
    k.dma("sp", lambda e: e.dma_start(out=stv, in_=src), writes=[st.b])
    if eng is None:
        eng = g.cast_engs[g.cast_i % len(g.cast_engs)]
        g.cast_i += 1
    if eng == "act":
        k.op("act", lambda e: e.copy(out=dst_view, in_=stv), reads=[st.b], writes=[dst_buf])
    else:
        k.op(eng, lambda e: e.tensor_copy(out=dst_view, in_=stv), reads=[st.b], writes=[dst_buf])


def phase_proj(nc, k, g, l, s, dr):
    X = dr["xT_in"] if l == 0 else dr["xres"]
    w_in = dr["w_in"]
    with ExitStack() as ph:
        ring(nc, g, ph, 3, 4)
        hT = _sb(nc, ph, "hT", [128, 8, TT], BF16)
        hTb = [Buf("hT%d" % i) for i in range(len(BLOCKS))]
        xin = [_sb(nc, ph, "xin%d" % i, [128, 8, 512], F32) for i in range(1)]
        stg_b = [_sb(nc, ph, "stgb%d" % i, [128, 4, 512], BF16) for i in range(2)]
        stg_f = [_sb(nc, ph, "stgf%d" % i, [128, 4, 512], F32) for i in range(2)]
        sig = [_sb(nc, ph, "sig%d" % i, [128, 512], F32) for i in range(2)]
        vst = [_sb(nc, ph, "vst%d" % i, [128, 768], BF16) for i in range(2)]
        for v_ in vst:
            k.op("pool", lambda e, v_=v_: e.memset(v_.t[:, :], 1.0), writes=[v_.b])
        for bi, (c0, N) in enumerate(BLOCKS):
            xt = xin[0]
            j = 2 if bi == 0 else s
            k.dma("sp", lambda e, xt=xt, c0=c0, N=N: e.dma_start(out=xt.t[:, :, 0:N], in_=X[s, :, c0:c0 + N].rearrange("(kc p) n -> p kc n", p=128)),
                  reads=[k.dbuf("x", s, bi)], writes=[xt.b])
            for kc in range(8):
                if bi % 2 == 0:
                    k.op("dve", lambda e, xt=xt, kc=kc, c0=c0, N=N, j=j: e.tensor_scalar(
                        out=hT.t[:, kc, c0:c0 + N], in0=xt.t[:, kc, 0:N], scalar1=g.mod.t[:, l, 8 + kc, j:j + 1],
                        scalar2=g.mod.t[:, l, kc, j:j + 1], op0=ALU.mult, op1=ALU.add), reads=[xt.b, g.mod.b], writes=[hTb[bi]])
                else:
                    k.op("act", lambda e, xt=xt, kc=kc, c0=c0, N=N, j=j: e.activation(
                        out=hT.t[:, kc, c0:c0 + N], in_=xt.t[:, kc, 0:N], func=AF.Identity, scale=g.mod.t[:, l, 8 + kc, j:j + 1],
                        bias=g.mod.t[:, l, kc, j:j + 1]), reads=[xt.b, g.mod.b], writes=[hTb[bi]])

        def wu(c0, n):
            return (w_in[l, :, c0:c0 + n].rearrange("(kc p) n -> p kc n", p=128), (128, 8, n))

        units = [wu(0, 512), wu(512, 512), wu(1024, 512), wu(1536, 256), wu(1792, 128),
                 (dr["w_kr"][l, :, :].rearrange("(kc p) n -> p kc n", p=128), (128, 8, 192)),
                 wu(1952, 512), wu(2464, 512)] + [wu(2976 + 512 * i, 512) for i in range(6)]
        ws = WStream(k, g, units, pf=2)
        cnt = [0]

        def nextps():
            p = g.ps[cnt[0] % 6]
            cnt[0] += 1
            return p

        def fm_tile(wv, wb, mcol, mw, bi):
            c0, N = BLOCKS[bi]
            ps = nextps()
            mm(k, ps, ps.t[0:mw, 0:N], [(wv[:, kc, mcol:mcol + mw], hT.t[:, kc, c0:c0 + N]) for kc in range(8)], [wb, hTb[bi]])
            return ps

        it = [0]
        for u, (dst, name, scl) in enumerate([(dr["qaT_d"], "qaT", 0.125), (dr["kaT_d"], "kaT", 1.0)]):
            wv, wb = ws.get(u)
            for bi, (c0, N) in enumerate(BLOCKS):
                st = stg_b[it[0] % 2]
                it[0] += 1
                for mt in range(4):
                    ps = fm_tile(wv, wb, mt * 128, 128, bi)
                    k.op("dve", lambda e, ps=ps, st=st, mt=mt, N=N, scl=scl: e.tensor_scalar(
                        out=st.t[:, mt, 0:N], in0=ps.t[:, 0:N], scalar1=scl, scalar2=None, op0=ALU.mult), reads=[ps.b], writes=[st.b])
                k.dma("sp", lambda e, st=st, dst=dst, c0=c0, N=N: e.dma_start(
                    out=dst[s, :, c0:c0 + N].rearrange("(m p) n -> p m n", p=128), in_=st.t[:, :, 0:N]),
                    reads=[st.b], writes=[k.dbuf(name, s, bi)])
        wv, wb = ws.get(2)

        def blocks_of(t0, t1):
            return [hTb[bi] for bi, (c0, N) in enumerate(BLOCKS) if t0 < c0 + N and t1 > c0]

        for kind, ntile, base, dst in (("e", 18, 0, dr["vae_d"]), ("o", 15, C + 64, dr["vao_d"])):
            for j in range(ntile):
                t0 = base + 128 * j
                ps = nextps()
                mm(k, ps, ps.t[:, 0:512], [(hT.t[:, kc, t0:t0 + 128], wv[:, kc, 0:512]) for kc in range(8)], [wb] + blocks_of(t0, t0 + 128))
                st = vst[it[0] % 2]
                it[0] += 1
                v3 = st.t[:, :].rearrange("p (q c) -> p q c", c=192)
                p3 = ps.t[:, 0:512].rearrange("p (q c) -> p q c", c=128)
                k.op("act", lambda e, v3=v3, p3=p3: e.copy(out=v3[:, :, 0:64], in_=p3[:, :, 0:64]), reads=[ps.b], writes=[st.b])
                k.op("act", lambda e, v3=v3, p3=p3: e.copy(out=v3[:, :, 128:192], in_=p3[:, :, 64:128]), reads=[ps.b], writes=[st.b])
                k.dma("sp", lambda e, st=st, dst=dst, j=j: e.dma_start(out=dst[s, j, :, :], in_=st.t[:, :]),
                      reads=[st.b], writes=[k.dbuf("va" + kind, s, j)])
        for u, nm, dst, nmt in ((3, "cq", dr["cq_d"], 2), (4, "ckv", dr["ckv_d"], 1)):
            wv, wb = ws.get(u)
            for bi, (c0, N) in enumerate(BLOCKS):
                st = stg_f[it[0] % 2]
                it[0] += 1
                for mt in range(nmt):
                    ps = fm_tile(wv, wb, mt * 128, 128, bi)
                    k.op("dve", lambda e, ps=ps, st=st, mt=mt, N=N: e.tensor_copy(out=st.t[:, mt, 0:N], in_=ps.t[:, 0:N]), reads=[ps.b], writes=[st.b])
                k.dma("sp", lambda e, st=st, dst=dst, c0=c0, N=N, nmt=nmt: e.dma_start(
                    out=dst[s, :, c0:c0 + N].rearrange("(m p) n -> p m n", p=128), in_=st.t[:, 0:nmt, 0:N]),
                    reads=[st.b], writes=[k.dbuf(nm, s, bi)])
        wv, wb = ws.get(5)
        for bi, (c0, N) in enumerate(BLOCKS):
            st = stg_f[it[0] % 2]
            it[0] += 1
            for mt in range(2):
                ps = fm_tile(wv, wb, mt * 96, 96, bi)
                k.op("dve", lambda e, ps=ps, st=st, mt=mt, N=N: e.tensor_copy(out=st.t[64:96, mt, 0:N], in_=ps.t[64:96, 0:N]), reads=[ps.b], writes=[st.b])
            for mt in range(2):
                k.dma("sp", lambda e, st=st, c0=c0, N=N, mt=mt: e.dma_start(out=dr["kr_d"][s, mt, :, c0:c0 + N], in_=st.t[64:96, mt, 0:N]),
                      reads=[st.b], writes=[k.dbuf("kr", s, bi, mt)])
        wva, wba = ws.get(6)
        wvg, wbg = ws.get(7)
        for bi, (c0, N) in enumerate(BLOCKS):
            st = stg_f[it[0] % 2]
            it[0] += 1
            for mt in range(4):
                psa = fm_tile(wva, wba, mt * 128, 128, bi)
                psg = fm_tile(wvg, wbg, mt * 128, 128, bi)
                sg = sig[mt % 2]
                k.op("act", lambda e, psg=psg, sg=sg, N=N: e.activation(out=sg.t[:, 0:N], in_=psg.t[:, 0:N], func=AF.Sigmoid), reads=[psg.b], writes=[sg.b])
                k.op("dve", lambda e, psa=psa, sg=sg, st=st, mt=mt, N=N: e.tensor_tensor(out=st.t[:, mt, 0:N], in0=psa.t[:, 0:N], in1=sg.t[:, 0:N], op=ALU.mult),
                     reads=[psa.b, sg.b], writes=[st.b])
            k.dma("sp", lambda e, st=st, c0=c0, N=N: e.dma_start(out=dr["u_d"][s, :, c0:c0 + N].rearrange("(m p) n -> p m n", p=128), in_=st.t[:, :, 0:N]),
                  reads=[st.b], writes=[k.dbuf("u", s, bi)])
        for gu in range(6):
            wv, wb = ws.get(8 + gu)
            for bi, (c0, N) in enumerate(BLOCKS):
                st = stg_b[it[0] % 2]
                it[0] += 1
                for mt in range(4):
                    ps = fm_tile(wv, wb, mt * 128, 128, bi)
                    k.op("act", lambda e, ps=ps, st=st, mt=mt, N=N: e.activation(out=st.t[:, mt, 0:N], in_=ps.t[:, 0:N], func=AF.Sigmoid), reads=[ps.b], writes=[st.b])
                k.dma("sp", lambda e, st=st, c0=c0, N=N, gu=gu: e.dma_start(
                    out=dr["gT_d"][s, gu * 512:(gu + 1) * 512, c0:c0 + N].rearrange("(m p) n -> p m n", p=128), in_=st.t[:, :, 0:N]),
                    reads=[st.b], writes=[k.dbuf("gT", s, bi, gu)])
        k.barrier()


def phase_mla_prep(nc, k, g, l, s, dr):
    with ExitStack() as ph:
        ring(nc, g, ph, 2, 0)
        Wuq = _sb(nc, ph, "Wuq", [128, 2, 768], BF16)
        Wuqs = _sb(nc, ph, "Wuqs", [128, 2, 768], BF16)
        Wukv = _sb(nc, ph, "Wukv", [128, 1024], BF16)
        load_cast(k, g, dr["w_uq"][l].rearrange("(kc p) n -> p kc n", p=128), (128, 2, 768), Wuq.t[:, :, :], Wuq.b)
        load_cast(k, g, dr["w_uqs"][l].rearrange("(kc p) n -> p kc n", p=128), (128, 2, 768), Wuqs.t[:, :, :], Wuqs.b)
        load_cast(k, g, dr["w_ukv"][l], (128, 1024), Wukv.t[:, :], Wukv.b)
        Wukv3 = Wukv.t[:, :].rearrange("p (h c) -> p h c", h=8)
        cq = [_sb(nc, ph, "cq%d" % i, [128, 2, 512], F32) for i in range(2)]
        ckv = [_sb(nc, ph, "ckv%d" % i, [128, 512], F32) for i in range(2)]
        krt = [_sb(nc, ph, "krt%d" % i, [96, 2, 512], F32) for i in range(2)]
        sq = _sb(nc, ph, "sq", [128, 2, 512], BF16)
        rstd = _sb(nc, ph, "rstd", [128, 512], F32)
        t1 = _sb(nc, ph, "t1", [96, 512], F32)
        t2 = _sb(nc, ph, "t2", [96, 512], F32)
        qbs = [_sb(nc, ph, "qbs%d" % i, [96, 8, 512], BF16) for i in range(2)]
        kbs = [_sb(nc, ph, "kbs%d" % i, [96, 8, 512], BF16) for i in range(2)]
        vbs = [_sb(nc, ph, "vbs%d" % i, [128, 768], BF16) for i in range(2)]
        for v_ in vbs:
            k.op("pool", lambda e, v_=v_: e.memset(v_.t[:, :], 1.0), writes=[v_.b])
        vi = 0
        rCs = [_sb(nc, ph, "rC%d" % i, [96, 512], F32) for i in range(2)]
        rSs = [_sb(nc, ph, "rS%d" % i, [96, 512], F32) for i in range(2)]
        cqns = [_sb(nc, ph, "cqn%d" % i, [128, 2, 512], BF16) for i in range(2)]
        ckvns = [_sb(nc, ph, "ckvn%d" % i, [128, 512], BF16) for i in range(2)]
        kros = [_sb(nc, ph, "kro%d" % i, [96, 512], BF16) for i in range(2)]
        ta1 = _sb(nc, ph, "ta1", [96, 512], F32)
        ta2 = _sb(nc, ph, "ta2", [96, 512], F32)
        vi_ = [0]

        def stage_a(bi):
            c0, N = BLOCKS[bi]
            cqn, ckvn, kro = cqns[bi % 2], ckvns[bi % 2], kros[bi % 2]
            t1, t2 = ta1, ta2
            if True:
                cqt, ckvt, krtt = cq[bi % 2], ckv[bi % 2], krt[bi % 2]
                qb, kb = qbs[bi % 2], kbs[bi % 2]
                rC, rS = rCs[bi % 2], rSs[bi % 2]
                k.dma("sp", lambda e: e.dma_start(out=rC.t[64:96, 0:N], in_=dr["ropeC_in"][:, c0:c0 + N]), writes=[rC.b])
                k.dma("sp", lambda e: e.dma_start(out=rS.t[64:96, 0:N], in_=dr["ropeS_in"][:, c0:c0 + N]), writes=[rS.b])
                k.dma("sp", lambda e: e.dma_start(out=cqt.t[:, :, 0:N], in_=dr["cq_d"][s, :, c0:c0 + N].rearrange("(m p) n -> p m n", p=128)),
                      reads=[k.dbuf("cq", s, bi)], writes=[cqt.b])
                k.dma("sp", lambda e: e.dma_start(out=ckvt.t[:, 0:N], in_=dr["ckv_d"][s, :, c0:c0 + N]), reads=[k.dbuf("ckv", s, bi)], writes=[ckvt.b])
                for mt in range(2):
                    k.dma("sp", lambda e, mt=mt: e.dma_start(out=krtt.t[64:96, mt, 0:N], in_=dr["kr_d"][s, mt, :, c0:c0 + N]),
                          reads=[k.dbuf("kr", s, bi, mt)], writes=[krtt.b])
                col_stats(k, g, [cqt.t[:, 0, 0:N], cqt.t[:, 1, 0:N]], cqt.b, N, 256, RMS_EPS, False, sq, None, rstd)
                for c in range(2):
                    k.op("dve", lambda e, c=c: e.scalar_tensor_tensor(out=cqn.t[:, c, 0:N], in0=cqt.t[:, c, 0:N], scalar=g.vecs.t[:, l, V_GQ + c:V_GQ + c + 1],
                                                                      in1=rstd.t[:, 0:N], op0=ALU.mult, op1=ALU.mult), reads=[cqt.b, rstd.b, g.vecs.b], writes=[cqn.b])
                col_stats(k, g, [ckvt.t[:, 0:N]], ckvt.b, N, 128, RMS_EPS, False, sq, None, rstd)
                k.op("dve", lambda e: e.scalar_tensor_tensor(out=ckvn.t[:, 0:N], in0=ckvt.t[:, 0:N], scalar=g.vecs.t[:, l, V_GKV:V_GKV + 1],
                                                             in1=rstd.t[:, 0:N], op0=ALU.mult, op1=ALU.mult), reads=[ckvt.b, rstd.b, g.vecs.b], writes=[ckvn.b])
                k.op("dve", lambda e: e.tensor_tensor(out=t1.t[64:96, 0:N], in0=krtt.t[64:96, 0, 0:N], in1=rC.t[64:96, 0:N], op=ALU.mult),
                     reads=[krtt.b, rC.b], writes=[t1.b])
                k.op("dve", lambda e: e.tensor_tensor(out=t2.t[64:96, 0:N], in0=krtt.t[64:96, 1, 0:N], in1=rS.t[64:96, 0:N], op=ALU.mult),
                     reads=[krtt.b, rS.b], writes=[t2.b])
                k.op("dve", lambda e: e.tensor_tensor(out=kro.t[64:96, 0:N], in0=t1.t[64:96, 0:N], in1=t2.t[64:96, 0:N], op=ALU.add),
                     reads=[t1.b, t2.b], writes=[kro.b])

        def stage_b(bi):
            c0, N = BLOCKS[bi]
            cqn, ckvn, kro = cqns[bi % 2], ckvns[bi % 2], kros[bi % 2]
            qb, kb = qbs[bi % 2], kbs[bi % 2]
            rC, rS = rCs[bi % 2], rSs[bi % 2]
            vi = vi_[0]
            if True:
                need_q = not (bi == 0 and l == L - 1)
                for h in range(8):
                    ps = g.ps[h % 2]
                    mm(k, ps, ps.t[0:64, 0:N], [(Wukv3[:, h, 0:64], ckvn.t[:, 0:N])], [Wukv.b, ckvn.b])
                    k.op("act", lambda e, ps=ps, h=h: e.copy(out=kb.t[0:64, h, 0:N], in_=ps.t[0:64, 0:N]), reads=[ps.b], writes=[kb.b])
                    k.op("pool", lambda e, h=h: e.tensor_copy(out=kb.t[64:96, h, 0:N], in_=kro.t[64:96, 0:N]), reads=[kro.b], writes=[kb.b])
                    if need_q:
                        pa = g.ps[2 + (h % 2)]
                        pb = g.ps[4 + (h % 2)]
                        mm(k, pa, pa.t[0:96, 0:N], [(Wuq.t[:, kc, h * 96:(h + 1) * 96], cqn.t[:, kc, 0:N]) for kc in range(2)], [Wuq.b, cqn.b])
                        mm(k, pb, pb.t[0:96, 0:N], [(Wuqs.t[:, kc, h * 96:(h + 1) * 96], cqn.t[:, kc, 0:N]) for kc in range(2)], [Wuqs.b, cqn.b])
                        k.op("act", lambda e, pa=pa, h=h: e.copy(out=qb.t[0:64, h, 0:N], in_=pa.t[0:64, 0:N]), reads=[pa.b], writes=[qb.b])
                        k.op("dve", lambda e, pa=pa: e.tensor_tensor(out=t1.t[64:96, 0:N], in0=pa.t[64:96, 0:N], in1=rC.t[64:96, 0:N], op=ALU.mult),
                             reads=[pa.b, rC.b], writes=[t1.b])
                        k.op("dve", lambda e, pb=pb: e.tensor_tensor(out=t2.t[64:96, 0:N], in0=pb.t[64:96, 0:N], in1=rS.t[64:96, 0:N], op=ALU.mult),
                             reads=[pb.b, rS.b], writes=[t2.b])
                        k.op("dve", lambda e, h=h: e.tensor_tensor(out=qb.t[64:96, h, 0:N], in0=t1.t[64:96, 0:N], in1=t2.t[64:96, 0:N], op=ALU.add),
                             reads=[t1.b, t2.b], writes=[qb.b])
                k.dma("sp", lambda e: e.dma_start(out=dr["kbT_d"][s, :, :, c0:c0 + N].rearrange("h r n -> r h n"), in_=kb.t[:, :, 0:N]),
                      reads=[kb.b], writes=[k.dbuf("kbT", s, bi)])
                if need_q:
                    k.dma("sp", lambda e: e.dma_start(out=dr["qbT_d"][s, :, :, c0:c0 + N].rearrange("h r n -> r h n"), in_=qb.t[:, :, 0:N]),
                          reads=[qb.b], writes=[k.dbuf("qbT", s, bi)])
                for tk in range(N // 128):
                    ps = g.ps[tk % 2]
                    vt = vbs[vi % 2]
                    vi += 1
                    mm(k, ps, ps.t[:, 0:512].rearrange("p (h c) -> p h c", h=8), [(ckvn.t[:, tk * 128:(tk + 1) * 128], Wukv3[:, :, 64:128])], [Wukv.b, ckvn.b])
                    v3 = vt.t[:, :].rearrange("p (q c) -> p q c", c=192)
                    p3 = ps.t[:, 0:512].rearrange("p (q c) -> p q c", c=128)
                    k.op("act", lambda e, v3=v3, p3=p3: e.copy(out=v3[:, :, 0:64], in_=p3[:, :, 0:64]), reads=[ps.b], writes=[vt.b])
                    k.op("act", lambda e, v3=v3, p3=p3: e.copy(out=v3[:, :, 128:192], in_=p3[:, :, 64:128]), reads=[ps.b], writes=[vt.b])
                    j = (c0 + tk * 128) // 128
                    k.dma("sp", lambda e, vt=vt, j=j: e.dma_start(out=dr["vb_d"][s, j, :, :], in_=vt.t[:, :]), reads=[vt.b], writes=[k.dbuf("vb", s, j)])

            vi_[0] = vi

        stage_a(0)
        for bi in range(len(BLOCKS)):
            if bi + 1 < len(BLOCKS):
                stage_a(bi + 1)
            stage_b(bi)
        k.barrier()


def phase_na(nc, k, g, l, s, dr):
    with ExitStack() as ph:
        ring(nc, g, ph, 1, 0)
        kaT = _sb(nc, ph, "kaT", [128, 4, TT], BF16)
        qaT = _sb(nc, ph, "qaT", [128, 4, TT], BF16)
        Ve = _sb(nc, ph, "Ve", [128, 18, 768], BF16)
        Vo = _sb(nc, ph, "Vo", [128, 15, 768], BF16)
        yaT = _sb(nc, ph, "yaT", [128, 4, TT], BF16)
        pts = [_sb(nc, ph, "pt%d" % i, [128, 512], BF16) for i in range(4)]
        recs = [_sb(nc, ph, "rec%d" % i, [128, 256], F32) for i in range(2)]
        g.biasb = _sb(nc, ph, "biasb", [128, 7680], BF16)
        for hf in range(2):
            load_cast(k, g, dr["biasT_rep"][l, :, hf * 3840:(hf + 1) * 3840], (128, 3840), g.biasb.t[:, hf * 3840:(hf + 1) * 3840], g.biasb.b)
        k.dma("sp", lambda e: e.dma_start(out=kaT.t[:, :, :], in_=dr["kaT_d"][s].rearrange("(m p) n -> p m n", p=128)), reads=k.dall("kaT", s), writes=[kaT.b])
        k.dma("sp", lambda e: e.dma_start(out=qaT.t[:, :, :], in_=dr["qaT_d"][s].rearrange("(m p) n -> p m n", p=128)), reads=k.dall("qaT", s), writes=[qaT.b])
        k.dma("sp", lambda e: e.dma_start(out=Ve.t[:, :, :], in_=dr["vae_d"][s].rearrange("j p c -> p j c")), reads=k.dall("vae", s), writes=[Ve.b])
        k.dma("sp", lambda e: e.dma_start(out=Vo.t[:, :, :], in_=dr["vao_d"][s].rearrange("j p c -> p j c")), reads=k.dall("vao", s), writes=[Vo.b])
        sbanks = [g.ps[0], g.ps[1], g.ps[6], g.ps[7]]
        obs = [(g.ps[2], None), (g.ps[3], None), (g.ps[4], None), (g.ps[5], None)]
        it = 0
        jobs = []
        for r in range(32):
            rs = min(max(r - 4, 0), 24)
            for h in range(8):
                pair, hp = h // 2, h % 2
                rows = slice(hp * 64, hp * 64 + 64)
                q_ap = qaT.t[rows, pair, C + 64 * r:C + 64 * r + 64]
                chunks = []
                for c in range(4):
                    tok0 = C + 64 * (rs + 2 * c)
                    if rs % 2 == 0:
                        v = Ve.t[:, 2 + rs // 2 + c, pair * 192 + hp * 64:pair * 192 + hp * 64 + 128]
                        vb_ = Ve.b
                    else:
                        v = Vo.t[:, (rs - 1) // 2 + c, pair * 192 + hp * 64:pair * 192 + hp * 64 + 128]
                        vb_ = Vo.b
                    d0 = rs + 2 * c - r + 7
                    col0 = (h * 15 + d0) * 64
                    chunks.append(dict(kT=kaT.t[rows, pair, tok0:tok0 + 128], v=v, bufs=[kaT.b, vb_],
                                       bias=(g.biasb.t[rows, col0:col0 + 128], g.ident_b.t[rows, hp * 64:hp * 64 + 64], [g.biasb.b, g.ident_b.b])))
                for c in range(2):
                    chunks.append(dict(kT=kaT.t[rows, pair, 128 * c:128 * c + 128], v=Ve.t[:, c, pair * 192 + hp * 64:pair * 192 + hp * 64 + 128], bufs=[kaT.b, Ve.b], bias=None))
                ob, db = obs[it % 4]
                jobs.append(attention(k, g, 64, q_ap, [qaT.b], chunks, 1.0, hp, yaT.t[rows, pair, C + 64 * r:C + 64 * r + 64], yaT.b,
                                      sbanks, ob, db, pts, recs[it % 2]))
                it += 1
        if l < L - 1:
            for h in range(8):
                pair, hp = h // 2, h % 2
                rows = slice(hp * 64, hp * 64 + 64)
                chunks = [dict(kT=kaT.t[rows, pair, 128 * c:128 * c + 128], v=Ve.t[:, c, pair * 192 + hp * 64:pair * 192 + hp * 64 + 128], bufs=[kaT.b, Ve.b], bias=None) for c in range(2)]
                ob, db = obs[it % 4]
                jobs.append(attention(k, g, 256, qaT.t[rows, pair, 0:256], [qaT.b], chunks, 1.0, hp, yaT.t[rows, pair, 0:256], yaT.b,
                                      sbanks, ob, db, pts, recs[it % 2]))
                it += 1
        run_attention(jobs)
        c0 = 0 if l < L - 1 else C
        k.dma("sp", lambda e: e.dma_start(out=dr["yaT_d"][s, :, c0:TT].rearrange("(m p) n -> p m n", p=128), in_=yaT.t[:, :, c0:TT]),
              reads=[yaT.b], writes=[k.dbuf("yaT", s)])
        k.barrier()


def phase_mla(nc, k, g, l, s, dr):
    with ExitStack() as ph:
        kbT = _sb(nc, ph, "kbT", [96, 8, TT], BF16)
        vb = _sb(nc, ph, "vb", [128, 18, 768], BF16)
        qbs = [_sb(nc, ph, "qb%d" % i, [96, 8, 512], BF16) for i in range(2)]
        ybs = [_sb(nc, ph, "yb%d" % i, [128, 4, 512], BF16) for i in range(2)]
        pts = [_sb(nc, ph, "pt%d" % i, [128, 512], BF16) for i in range(4)]
        recs = [_sb(nc, ph, "rec%d" % i, [128, 512], F32) for i in range(2)]
        cv = conv_alloc(nc, ph)
        taps = conv_taps(k, g, l, s, dr, cv, C, T)
        for b0 in range(0, T, 256):
            taps.append(lambda b0=b0: conv_tail(k, g, l, s, dr, cv, C, T, blocks=[(b0, 256)], bank=g.ps[7]))
        if l < L - 1:
            taps += conv_taps(k, g, l, s, dr, cv, 0, C, uT=cv.uTc, acc=cv.accc)
            taps.append(lambda: conv_tail(k, g, l, s, dr, cv, 0, C, acc=cv.accc, blocks=[(0, 256)], bank=g.ps[7]))
        tap_i = [0]
        calls = [0]

        def emit_taps(n):
            for _ in range(n):
                if tap_i[0] < len(taps):
                    taps[tap_i[0]]()
                    tap_i[0] += 1

        def side():
            calls[0] += 1
            if calls[0] % 2 == 0:
                emit_taps(1)

        emit_taps(4)
        k.dma("sp", lambda e: e.dma_start(out=kbT.t[:, :, :], in_=dr["kbT_d"][s].rearrange("h r n -> r h n")), reads=k.dall("kbT", s), writes=[kbT.b])
        k.dma("sp", lambda e: e.dma_start(out=vb.t[:, :, :], in_=dr["vb_d"][s].rearrange("j p c -> p j c")), reads=k.dall("vb", s), writes=[vb.b])
        sbanks = [g.ps[0], g.ps[1], g.ps[6]]
        obs = [(g.ps[2], None), (g.ps[3], None), (g.ps[4], None), (g.ps[5], None)]
        scale = float(96 ** -0.5)
        it = 0
        for bi, (c0, N) in enumerate(BLOCKS):
            if bi == 0 and l == L - 1:
                continue
            qb, yb = qbs[bi % 2], ybs[bi % 2]
            k.dma("sp", lambda e: e.dma_start(out=qb.t[:, :, 0:N], in_=dr["qbT_d"][s, :, :, c0:c0 + N].rearrange("h r n -> r h n")),
                  reads=[k.dbuf("qbT", s, bi)], writes=[qb.b])
            nk = 2 if bi == 0 else 18
            jobs = []
            for h in range(8):
                pair, hp = h // 2, h % 2
                rows = slice(hp * 64, hp * 64 + 64)
                chunks = [dict(kT=kbT.t[0:96, h, 128 * j:128 * j + 128], v=vb.t[:, j, pair * 192 + hp * 64:pair * 192 + hp * 64 + 128], bufs=[kbT.b, vb.b], bias=None) for j in range(nk)]
                ob, db = obs[it % 4]
                jobs.append(attention(k, g, N, qb.t[0:96, h, 0:N], [qb.b], chunks, scale, hp, yb.t[rows, pair, 0:N], yb.b, sbanks, ob, db, pts, recs[it % 2]))
                it += 1
            run_attention(jobs, side)
            k.dma("sp", lambda e: e.dma_start(out=dr["ybT_d"][s, :, c0:c0 + N].rearrange("(m p) n -> p m n", p=128), in_=yb.t[:, :, 0:N]),
                  reads=[yb.b], writes=[k.dbuf("ybT", s, bi)])
        emit_taps(len(taps))
        k.barrier()


class ConvState:
    pass


def conv_alloc(nc, ph):
    cv = ConvState()
    cv.uT = _sb(nc, ph, "uT", [128, 4, T + 30], F32)
    cv.acc = _sb(nc, ph, "acc", [128, 4, T], F32)
    cv.sq = _sb(nc, ph, "sq", [128, 4, 512], BF16)
    cv.z16 = _sb(nc, ph, "z16", [128, 4, 512], BF16)
    cv.mean = _sb(nc, ph, "mean", [128, 512], F32)
    cv.rstd = _sb(nc, ph, "rstd", [128, 512], F32)
    cv.tmp = _sb(nc, ph, "tmp", [128, 512], F32)
    cv.ucs = [_sb(nc, ph, "ucs%d" % i, [128, 4, 512], BF16) for i in range(2)]
    cv.uTc = _sb(nc, ph, "uTc", [128, 4, C + 30], F32)
    cv.accc = _sb(nc, ph, "accc", [128, 4, C], F32)
    cv.it = 0
    return cv


def conv_taps(k, g, l, s, dr, cv, c0, Ls, uT=None, acc=None):
    uT = cv.uT if uT is None else uT
    acc = cv.acc if acc is None else acc
    ops = []
    ops.append(lambda: k.op("pool", lambda e: e.memset(uT.t[:, :, 0:15], 0.0), writes=[uT.b]))
    ops.append(lambda: k.op("pool", lambda e: e.memset(uT.t[:, :, 15 + Ls:30 + Ls], 0.0), writes=[uT.b]))
    ops.append(lambda: k.dma("sp", lambda e: e.dma_start(out=uT.t[:, :, 15:15 + Ls], in_=dr["u_d"][s, :, c0:c0 + Ls].rearrange("(m p) n -> p m n", p=128)),
                             reads=k.dall("u", s), writes=[uT.b]))
    for ch in range(4):
        wcol = V_WDW + ch * 31
        ops.append(lambda ch=ch, wcol=wcol: k.op("dve", lambda e: e.tensor_scalar(
            out=acc.t[:, ch, 0:Ls], in0=uT.t[:, ch, 0:Ls], scalar1=g.vecs.t[:, l, wcol:wcol + 1],
            scalar2=g.vecs.t[:, l, V_BDW + ch:V_BDW + ch + 1], op0=ALU.mult, op1=ALU.add), reads=[uT.b, g.vecs.b], writes=[acc.b]))
        for kk in range(1, 31):
            ops.append(lambda ch=ch, wcol=wcol, kk=kk: k.op("dve", lambda e: e.scalar_tensor_tensor(
                out=acc.t[:, ch, 0:Ls], in0=uT.t[:, ch, kk:kk + Ls], scalar=g.vecs.t[:, l, wcol + kk:wcol + kk + 1], in1=acc.t[:, ch, 0:Ls],
                op0=ALU.mult, op1=ALU.add), reads=[uT.b, g.vecs.b, acc.b], writes=[acc.b]))
    return ops


def conv_tail(k, g, l, s, dr, cv, c0, Ls, acc=None, blocks=None, bank=None):
    sq, z16, mean, rstd, tmp = cv.sq, cv.z16, cv.mean, cv.rstd, cv.tmp
    acc = cv.acc if acc is None else acc
    if blocks is None:
        blocks = [(b0, min(512, Ls - b0)) for b0 in range(0, Ls, 512)]
    for b0, N in blocks:
        if bank is None:
            col_stats(k, g, [acc.t[:, ch, b0:b0 + N] for ch in range(4)], acc.b, N, 512, LN_EPS, True, sq, mean, rstd, z16)
        else:
            col_stats(k, g, [acc.t[:, ch, b0:b0 + N] for ch in range(4)], acc.b, N, 512, LN_EPS, True, sq, mean, rstd, z16,
                      ps1=bank, ps2=bank, off1=0, off2=256)
        uc = cv.ucs[cv.it % 2]
        cv.it += 1
        for ch in range(4):
            k.op("dve", lambda e, ch=ch: e.tensor_tensor(out=tmp.t[:, 0:N], in0=acc.t[:, ch, b0:b0 + N], in1=mean.t[:, 0:N], op=ALU.subtract),
                 reads=[acc.b, mean.b], writes=[tmp.b])
            k.op("dve", lambda e, ch=ch: e.scalar_tensor_tensor(out=tmp.t[:, 0:N], in0=tmp.t[:, 0:N], scalar=g.vecs.t[:, l, V_GCN + ch:V_GCN + ch + 1],
                                                                 in1=rstd.t[:, 0:N], op0=ALU.mult, op1=ALU.mult), reads=[tmp.b, rstd.b, g.vecs.b], writes=[tmp.b])
            k.op("act", lambda e, ch=ch, uc=uc: e.activation(out=uc.t[:, ch, 0:N], in_=tmp.t[:, 0:N], func=AF.Silu,
                                                             bias=g.vecs.t[:, l, V_BCN + ch:V_BCN + ch + 1], scale=1.0), reads=[tmp.b, g.vecs.b], writes=[uc.b])
        k.dma("sp", lambda e, uc=uc, b0=b0, N=N: e.dma_start(out=dr["ucT_d"][s, :, c0 + b0:c0 + b0 + N].rearrange("(m p) n -> p m n", p=128), in_=uc.t[:, :, 0:N]),
              reads=[uc.b], writes=[k.dbuf("ucT", s, c0 + b0)])


MBLK = [(c0, 256) for c0 in range(0, TT, 256)]


def phase_merge(nc, k, g, l, s, dr):
    X = dr["xT_in"] if l == 0 else dr["xres"]
    NM = 512
    with ExitStack() as ph:
        ring(nc, g, ph, 2, 0)
        Woa = _sb(nc, ph, "Woa", [128, 4, 1024], BF16)
        Wob = _sb(nc, ph, "Wob", [128, 4, 1024], BF16)
        Wpw = _sb(nc, ph, "Wpw", [128, 4, 1024], BF16)
        Wout = _sb(nc, ph, "Wout", [128, 8, 1024], BF16)
        Wrf = _sb(nc, ph, "Wrf", [128, 8, 16], F32)
        Wrh = _sb(nc, ph, "Wrh", [128, 8, 16], BF16)
        Wrl = _sb(nc, ph, "Wrl", [128, 8, 16], BF16)
        for W, src in ((Woa, dr["w_oa"]), (Wob, dr["w_ob"]), (Wpw, dr["w_pw2"])):
            load_cast(k, g, src[l].rearrange("(kc p) n -> p kc n", p=128), (128, 4, 1024), W.t[:, :, :], W.b)
        for hf in range(2):
            load_cast(k, g, dr["w_out"][l, :, hf * 512:(hf + 1) * 512].rearrange("(kc p) n -> p kc n", p=128), (128, 8, 512),
                      Wout.t[:, :, hf * 512:(hf + 1) * 512], Wout.b)
        k.dma("sp", lambda e: e.dma_start(out=Wrf.t[:, :, :], in_=dr["w_router"][l].rearrange("(kc p) n -> p kc n", p=128)), writes=[Wrf.b])
        k.op("dve", lambda e: e.tensor_copy(out=Wrh.t[:, :, :], in_=Wrf.t[:, :, :]), reads=[Wrf.b], writes=[Wrh.b])
        k.op("dve", lambda e: e.tensor_tensor(out=Wrl.t[:, :, :], in0=Wrf.t[:, :, :], in1=Wrh.t[:, :, :], op=ALU.subtract), reads=[Wrf.b, Wrh.b], writes=[Wrl.b])
        yas = [_sb(nc, ph, "ya%d" % i, [128, 4, NM], BF16) for i in range(2)]
        ybs_ = [_sb(nc, ph, "yb%d" % i, [128, 4, NM], BF16) for i in range(2)]
        ucs_ = [_sb(nc, ph, "uc%d" % i, [128, 4, NM], BF16) for i in range(2)]
        gts = [_sb(nc, ph, "gt%d" % i, [128, 3, NM], BF16) for i in range(3)]
        ms_ = [_sb(nc, ph, "m%d" % i, [128, 8, NM], BF16) for i in range(2)]
        mfs = [_sb(nc, ph, "mf%d" % i, [128, NM], F32) for i in range(2)]
        t2s = [_sb(nc, ph, "t2_%d" % i, [128, NM], F32) for i in range(2)]
        tts = [_sb(nc, ph, "tt%d" % i, [128, NM], F32) for i in range(2)]
        z = _sb(nc, ph, "z", [128, 8, NM], F32)
        sq = _sb(nc, ph, "sq", [128, 8, NM], BF16)
        z16 = _sb(nc, ph, "z16", [128, 8, NM], BF16)
        mean = _sb(nc, ph, "mean", [128, NM], F32)
        rstd = _sb(nc, ph, "rstd", [128, NM], F32)
        h2fs = [_sb(nc, ph, "h2f%d" % i, [128, NM], F32) for i in range(2)]
        h2b = _sb(nc, ph, "h2b", [128, 8, NM], BF16)
        h2l = _sb(nc, ph, "h2l", [128, 8, NM], BF16)
        h2s = [_sb(nc, ph, "h2s%d" % i, [128, 1024], BF16) for i in range(2)]
        Et = _sb(nc, ph, "Et", [16, NM], F32)
        afs = [_sb(nc, ph, "afs%d" % i, [128, 16], F32) for i in range(4)]
        ssum = _sb(nc, ph, "ssum", [128, 4], F32)
        affs = _sb(nc, ph, "affs", [16, NM], F32)
        cnt = [0]
        gti = [0]
        tti = [0]
        hi_ = [0]

        def nextps():
            p = g.ps[cnt[0] % 4]
            cnt[0] += 1
            return p

        blocks = [(bi, c0, N) for bi, (c0, N) in enumerate(BLOCKS) if not (bi == 0 and l == L - 1)]

        def stage1(ix):
            bi, c0, N = blocks[ix]
            ya, yb, uc, m = yas[ix % 2], ybs_[ix % 2], ucs_[ix % 2], ms_[ix % 2]
            for T_, nm in ((ya, "yaT"), (yb, "ybT"), (uc, "ucT")):
                k.dma("sp", lambda e, T_=T_, nm=nm: e.dma_start(out=T_.t[:, :, 0:N], in_=dr[nm + "_d"][s, :, c0:c0 + N].rearrange("(m p) n -> p m n", p=128)),
                      reads=k.dall(nm, s), writes=[T_.b])
            for oc in range(8):
                osl = slice(oc * 128, (oc + 1) * 128)
                gt = gts[gti[0] % 3]
                gti[0] += 1
                k.dma("sp", lambda e, gt=gt, oc=oc: e.dma_start(
                    out=gt.t[:, :, 0:N], in_=dr["gT_d"][s, :, c0:c0 + N].rearrange("(b m p) n -> b m p n", b=3, p=128)[:, oc].rearrange("b p n -> p b n")),
                    reads=k.dall("gT", s), writes=[gt.b])
                pa = nextps()
                mm(k, pa, pa.t[:, 0:N], [(Woa.t[:, kc, osl], ya.t[:, kc, 0:N]) for kc in range(4)], [Woa.b, ya.b])
                pb = nextps()
                mm(k, pb, pb.t[:, 0:N], [(Wob.t[:, kc, osl], yb.t[:, kc, 0:N]) for kc in range(4)], [Wob.b, yb.b])
                pc = nextps()
                mm(k, pc, pc.t[:, 0:N], [(Wpw.t[:, kc, osl], uc.t[:, kc, 0:N]) for kc in range(4)], [Wpw.b, uc.b])
                mf, t2 = mfs[oc % 2], t2s[oc % 2]
                t1 = tts[tti[0] % 2]
                tti[0] += 1
                k.op("dve", lambda e, pa=pa, gt=gt, mf=mf: e.tensor_tensor(out=mf.t[:, 0:N], in0=pa.t[:, 0:N], in1=gt.t[:, 0, 0:N], op=ALU.mult), reads=[pa.b, gt.b], writes=[mf.b])
                k.op("dve", lambda e, pb=pb, gt=gt, t1=t1: e.tensor_tensor(out=t1.t[:, 0:N], in0=pb.t[:, 0:N], in1=gt.t[:, 1, 0:N], op=ALU.mult), reads=[pb.b, gt.b], writes=[t1.b])
                k.op("dve", lambda e, mf=mf, t1=t1: e.tensor_tensor(out=mf.t[:, 0:N], in0=mf.t[:, 0:N], in1=t1.t[:, 0:N], op=ALU.add), reads=[mf.b, t1.b], writes=[mf.b])
                k.op("dve", lambda e, pc=pc, gt=gt, t2=t2: e.tensor_tensor(out=t2.t[:, 0:N], in0=pc.t[:, 0:N], in1=gt.t[:, 2, 0:N], op=ALU.mult), reads=[pc.b, gt.b], writes=[t2.b])
                k.op("pool", lambda e, oc=oc, m=m, mf=mf, t2=t2: e.tensor_tensor(out=m.t[:, oc, 0:N], in0=mf.t[:, 0:N], in1=t2.t[:, 0:N], op=ALU.add), reads=[mf.b, t2.b], writes=[m.b])

        def stage2(ix):
            bi, c0, N = blocks[ix]
            m = ms_[ix % 2]
            j = 2 if bi == 0 else s
            k.dma("sp", lambda e: e.dma_start(out=z.t[:, :, 0:N], in_=X[s, :, c0:c0 + N].rearrange("(kc p) n -> p kc n", p=128)),
                  reads=k.dall("x", s), writes=[z.b])
            for oc in range(8):
                osl = slice(oc * 128, (oc + 1) * 128)
                py = nextps()
                mm(k, py, py.t[:, 0:N], [(Wout.t[:, kc, osl], m.t[:, kc, 0:N]) for kc in range(8)], [Wout.b, m.b])
                t1 = tts[tti[0] % 2]
                tti[0] += 1
                k.op("act", lambda e, py=py, t1=t1, oc=oc: e.activation(out=t1.t[:, 0:N], in_=py.t[:, 0:N], func=AF.Copy, scale=g.mod.t[:, l, 16 + oc, j:j + 1]),
                     reads=[py.b, g.mod.b], writes=[t1.b])
                k.op("dve", lambda e, t1=t1, oc=oc: e.scalar_tensor_tensor(out=z.t[:, oc, 0:N], in0=z.t[:, oc, 0:N], scalar=ALPHA, in1=t1.t[:, 0:N], op0=ALU.mult, op1=ALU.add),
                     reads=[z.b, t1.b], writes=[z.b])

        def stage3(ix):
            bi, c0, N = blocks[ix]
            j = 2 if bi == 0 else s
            col_stats(k, g, [z.t[:, oc, 0:N] for oc in range(8)], z.b, N, 1024, LN_EPS, True, sq, mean, rstd, z16)
            for oc in range(8):
                t1 = tts[tti[0] % 2]
                tti[0] += 1
                k.op("pool", lambda e, t1=t1, oc=oc: e.tensor_tensor(out=t1.t[:, 0:N], in0=z.t[:, oc, 0:N], in1=mean.t[:, 0:N], op=ALU.subtract), reads=[z.b, mean.b], writes=[t1.b])
                k.op("dve", lambda e, t1=t1, oc=oc: e.scalar_tensor_tensor(out=t1.t[:, 0:N], in0=t1.t[:, 0:N], scalar=g.vecs.t[:, l, V_LN1G + oc:V_LN1G + oc + 1], in1=rstd.t[:, 0:N],
                                                                          op0=ALU.mult, op1=ALU.mult), reads=[t1.b, rstd.b, g.vecs.b], writes=[t1.b])
                k.op("act", lambda e, t1=t1, oc=oc: e.activation(out=z.t[:, oc, 0:N], in_=t1.t[:, 0:N], func=AF.Identity, bias=g.vecs.t[:, l, V_LN1B + oc:V_LN1B + oc + 1], scale=1.0),
                     reads=[t1.b, g.vecs.b], writes=[z.b])
                h2f = h2fs[oc % 2]
                k.op("act", lambda e, oc=oc, h2f=h2f: e.activation(out=h2f.t[:, 0:N], in_=z.t[:, oc, 0:N], func=AF.Identity, scale=g.mod.t[:, l, 32 + oc, j:j + 1],
                                                                   bias=g.mod.t[:, l, 24 + oc, j:j + 1]), reads=[z.b, g.mod.b], writes=[h2f.b])
                k.op("act", lambda e, oc=oc, h2f=h2f: e.copy(out=h2b.t[:, oc, 0:N], in_=h2f.t[:, 0:N]), reads=[h2f.b], writes=[h2b.b])
                k.op("pool", lambda e, oc=oc, h2f=h2f: e.tensor_tensor(out=h2l.t[:, oc, 0:N], in0=h2f.t[:, 0:N], in1=h2b.t[:, oc, 0:N], op=ALU.subtract),
                     reads=[h2f.b, h2b.b], writes=[h2l.b])
            k.dma("sp", lambda e: e.dma_start(out=dr["x1res"][s, :, c0:c0 + N].rearrange("(kc p) n -> p kc n", p=128), in_=z.t[:, :, 0:N]),
                  reads=[z.b], writes=[k.dbuf("x1", s, bi)])

        def stage4(ix):
            bi, c0, N = blocks[ix]
            ntk = N // 128
            pr = g.ps[4]
            pairs = []
            for kc in range(8):
                pairs += [(Wrh.t[:, kc, :], h2b.t[:, kc, 0:N]), (Wrl.t[:, kc, :], h2b.t[:, kc, 0:N]), (Wrh.t[:, kc, :], h2l.t[:, kc, 0:N])]
            mm(k, pr, pr.t[0:16, 0:N], pairs, [Wrh.b, Wrl.b, h2b.b, h2l.b])
            k.op("act", lambda e: e.activation(out=Et.t[:, 0:N], in_=pr.t[0:16, 0:N], func=AF.Exp), reads=[pr.b], writes=[Et.b])
            pq = g.ps[5]
            for tk in range(ntk):
                af = afs[tk]
                k.op("pe", lambda e, tk=tk: e.transpose(pq.t[:, tk * 16:tk * 16 + 16], Et.t[0:16, tk * 128:(tk + 1) * 128], g.ident_f.t[0:16, 0:16]),
                     reads=[Et.b, g.ident_f.b], writes=[pq.b])
                k.op("dve", lambda e, tk=tk: e.tensor_reduce(out=ssum.t[:, tk:tk + 1], in_=pq.t[:, tk * 16:tk * 16 + 16], op=ALU.add, axis=mybir.AxisListType.X),
                     reads=[pq.b], writes=[ssum.b])
                k.op("dve", lambda e, tk=tk: e.reciprocal(out=ssum.t[:, tk:tk + 1], in_=ssum.t[:, tk:tk + 1]), reads=[ssum.b], writes=[ssum.b])
                k.op("dve", lambda e, tk=tk, af=af: e.tensor_scalar(out=af.t[:, :], in0=pq.t[:, tk * 16:tk * 16 + 16], scalar1=ssum.t[:, tk:tk + 1], scalar2=None, op0=ALU.mult),
                     reads=[pq.b, ssum.b], writes=[af.b])
                r0 = s * TT + c0 + tk * 128
                k.dma("sp", lambda e, af=af, r0=r0: e.dma_start(out=dr["afftok_d"][r0:r0 + 128, :], in_=af.t[:, :]), reads=[af.b], writes=[k.dbuf("afftok", s, bi, tk)])
                k.op("pe", lambda e, tk=tk, af=af: e.transpose(pr.t[0:16, tk * 128:(tk + 1) * 128], af.t[:, :], g.ident_f.t[:, :]),
                     reads=[af.b, g.ident_f.b], writes=[pr.b])
            k.op("dve", lambda e: e.tensor_copy(out=affs.t[:, 0:N], in_=pr.t[0:16, 0:N]), reads=[pr.b], writes=[affs.b])
            k.dma("sp", lambda e: e.dma_start(out=dr["affT_d"][s, :, c0:c0 + N], in_=affs.t[:, 0:N]), reads=[affs.b], writes=[k.dbuf("affT", s, bi)])
            for tk in range(ntk):
                pt_ = g.ps[4 + (tk % 2)]
                ptb = pt_.t[:, :].bitcast(BF16)
                hs = h2s[hi_[0] % 2]
                hi_[0] += 1
                for kc in range(8):
                    k.op("pe", lambda e, kc=kc, tk=tk, ptb=ptb: e.transpose(ptb[:, kc * 128:(kc + 1) * 128], h2b.t[:, kc, tk * 128:(tk + 1) * 128], g.ident_b.t[:, :]),
                         reads=[h2b.b, g.ident_b.b], writes=[pt_.b])
                k.op("act", lambda e, ptb=ptb, hs=hs: e.copy(out=hs.t[:, :], in_=ptb[:, 0:1024]), reads=[pt_.b], writes=[hs.b])
                r0 = s * TT + c0 + tk * 128
                k.dma("sp", lambda e, hs=hs, r0=r0: e.dma_start(out=dr["h2tok_d"][r0:r0 + 128, :], in_=hs.t[:, :]), reads=[hs.b], writes=[k.dbuf("h2tok", s, bi, tk)])

        stage1(0)
        for ix in range(len(blocks)):
            stage2(ix)
            if ix + 1 < len(blocks):
                stage1(ix + 1)
            stage3(ix)
            stage4(ix)
        k.barrier()


def phase_moe(nc, k, g, l, dr):
    with_ctx = l < L - 1
    nch = 5 if with_ctx else 4
    moe_d, h2tok_d, afftok_d = dr["moe_d"], dr["h2tok_d"], dr["afftok_d"]
    with ExitStack() as ph:
        ring(nc, g, ph, 3, 4, cast_engs=("dve", "act", "dve"))
        gl = _sb(nc, ph, "gidx_l", [128, 2, 32], I32)
        gc = _sb(nc, ph, "gidx_c", [64, 16], I32)
        with ExitStack() as ph2:
            zt = _sb(nc, ph2, "zt", [128, 2048], F32)
            k.op("pool", lambda e: e.memset(zt.t[:, :], 0.0), writes=[zt.b])
            moe_v = moe_d.rearrange("(i p a) d -> i p (a d)", p=128, a=2)
            for i in range(18):
                k.dma("sp", lambda e, i=i: e.dma_start(out=moe_v[i], in_=zt.t[:, :]), reads=[zt.b], writes=[k.dbuf("moez", i)])
            affT = _sb(nc, ph2, "affT", [32, TT], F32)
            for s in range(NS):
                k.dma("sp", lambda e, s=s: e.dma_start(out=affT.t[16 * s:16 * s + 16, :], in_=dr["affT_d"][s, :, :]), reads=k.dall("affT", s), writes=[affT.b])
            wk = _sb(nc, ph2, "wk", [32, T], F32)
            mx = _sb(nc, ph2, "mx", [32, 8], F32)
            idxu = _sb(nc, ph2, "idxu", [32, 256], U32)
            idxf = _sb(nc, ph2, "idxf", [32, 256], F32)
            offl = _sb(nc, ph2, "offl", [32, 1], F32)
            offc = _sb(nc, ph2, "offc", [32, 1], F32)
            k.op("dve", lambda e: e.memset(offl.t[:, :], float(TT + C)), writes=[offl.b])
            k.op("dve", lambda e: e.memset(offl.t[0:16, :], float(C)), writes=[offl.b])
            k.op("dve", lambda e: e.memset(offc.t[:, :], float(TT)), writes=[offc.b])
            k.op("dve", lambda e: e.memset(offc.t[0:16, :], 0.0), writes=[offc.b])
            pst = g.ps[7]

            def topk(n_tok, n_it):
                for it in range(n_it):
                    k.op("dve", lambda e: e.max(out=mx.t[:, :], in_=wk.t[:, 0:n_tok]), reads=[wk.b], writes=[mx.b])
                    k.op("dve", lambda e, it=it: e.max_index(out=idxu.t[:, it * 8:(it + 1) * 8], in_max=mx.t[:, :], in_values=wk.t[:, 0:n_tok]),
                         reads=[wk.b, mx.b], writes=[idxu.b])
                    k.op("dve", lambda e: e.match_replace(out=wk.t[:, 0:n_tok], in_to_replace=mx.t[:, :], in_values=wk.t[:, 0:n_tok], imm_value=-1.0),
                         reads=[wk.b, mx.b], writes=[wk.b])

            k.op("dve", lambda e: e.tensor_copy(out=wk.t[:, 0:T], in_=affT.t[:, C:TT]), reads=[affT.b], writes=[wk.b])
            topk(T, 32)
            k.op("dve", lambda e: e.tensor_scalar(out=idxf.t[:, :], in0=idxu.t[:, :], scalar1=offl.t[:, 0:1], scalar2=None, op0=ALU.add),
                 reads=[idxu.b, offl.b], writes=[idxf.b])
            for ch in range(2):
                k.op("pe", lambda e, ch=ch: e.transpose(pst.t[:, ch * 32:(ch + 1) * 32], idxf.t[0:32, ch * 128:(ch + 1) * 128], g.ident_f.t[0:32, 0:32]),
                     reads=[idxf.b, g.ident_f.b], writes=[pst.b])
                k.op("dve", lambda e, ch=ch: e.tensor_copy(out=gl.t[:, ch, :], in_=pst.t[:, ch * 32:(ch + 1) * 32]), reads=[pst.b], writes=[gl.b])
            if with_ctx:
                k.op("dve", lambda e: e.tensor_copy(out=wk.t[:, 0:C], in_=affT.t[:, 0:C]), reads=[affT.b], writes=[wk.b])
                topk(C, 4)
                k.op("dve", lambda e: e.tensor_scalar(out=idxf.t[:, 0:32], in0=idxu.t[:, 0:32], scalar1=offc.t[:, 0:1], scalar2=None, op0=ALU.add),
                     reads=[idxu.b, offc.b], writes=[idxf.b])
                k.op("pe", lambda e: e.transpose(pst.t[0:32, 64:96], idxf.t[0:32, 0:32], g.ident_f.t[0:32, 0:32]), reads=[idxf.b, g.ident_f.b], writes=[pst.b])
                k.op("dve", lambda e: e.tensor_copy(out=gc.t[0:32, :], in_=pst.t[0:32, 64:80]), reads=[pst.b], writes=[gc.b])
                k.op("dve", lambda e: e.tensor_copy(out=gc.t[32:64, :], in_=pst.t[0:32, 80:96]), reads=[pst.b], writes=[gc.b])
            k.barrier()
        units = []
        for ex in range(NE):
            for W, c0 in ((dr["w_gate"], 0), (dr["w_up"], 0), (dr["w_gate"], 512), (dr["w_up"], 512), (dr["w_down"], 0), (dr["w_down"], 512)):
                units.append((W[l, ex, :, c0:c0 + 512].rearrange("(kc p) n -> p kc n", p=128), (128, 8, 512)))
        ws = WStream(k, g, units, pf=2)
        xg = [[_sb(nc, ph, "xg%d_%d" % (b, i), [128, 1024], BF16) for i in range(nch)] for b in range(2)]
        ag = [[_sb(nc, ph, "ag%d_%d" % (b, i), [128, 16], F32) for i in range(nch)] for b in range(2)]
        xgT = _sb(nc, ph, "xgT", [128, 8, 576], BF16)
        hid = _sb(nc, ph, "hid", [128, 8, 576], BF16)
        sgs = [_sb(nc, ph, "sg%d" % i, [128, 576], F32) for i in range(2)]
        ysb = [_sb(nc, ph, "ysb%d" % i, [128, 1024], F32) for i in range(nch)]
        h2reads = k.dall("h2tok")
        afreads = k.dall("afftok")
        zreads = k.dall("moez")
        ti = [0]

        def idx_ap(ch, ex):
            if ch < 4:
                s_, c_ = ch // 2, ch % 2
                return gl.t[:, c_, s_ * 16 + ex:s_ * 16 + ex + 1], gl.b, 128
            return gc.t[0:64, ex:ex + 1], gc.b, 64

        def gathers(ex):
            b = ex % 2
            for ch in range(nch):
                ia, ib, P = idx_ap(ch, ex)
                k.dma("pool", lambda e, ch=ch, P=P, ia=ia: e.indirect_dma_start(
                    out=xg[b][ch].t[0:P, :], out_offset=None, in_=h2tok_d[:, :], in_offset=bass.IndirectOffsetOnAxis(ap=ia, axis=0)),
                    reads=h2reads + [ib], writes=[xg[b][ch].b])
                k.dma("pool", lambda e, ch=ch, P=P, ia=ia: e.indirect_dma_start(
                    out=ag[b][ch].t[0:P, :], out_offset=None, in_=afftok_d[:, :], in_offset=bass.IndirectOffsetOnAxis(ap=ia, axis=0)),
                    reads=afreads + [ib], writes=[ag[b][ch].b])

        def transposes(ex):
            b = ex % 2
            for ch in range(nch):
                P = 128 if ch < 4 else 64
                pt_ = g.ps[6 + (ti[0] % 2)]
                ti[0] += 1
                ptb = pt_.t[:, :].bitcast(BF16)
                for kc in range(8):
                    k.op("pe", lambda e, kc=kc, ch=ch, P=P, ptb=ptb: e.transpose(ptb[:, kc * 128:kc * 128 + P], xg[b][ch].t[0:P, kc * 128:(kc + 1) * 128], g.ident_b.t[0:P, 0:P]),
                         reads=[xg[b][ch].b, g.ident_b.b], writes=[pt_.b])
                k.op("act", lambda e, ch=ch, P=P, ptb=ptb: e.copy(out=xgT.t[:, :, ch * 128:ch * 128 + P], in_=ptb[:, 0:1024].rearrange("p (a b) -> p a b", a=8)[:, :, 0:P]),
                     reads=[pt_.b], writes=[xgT.b])

        gathers(0)
        transposes(0)
        for ex in range(NE):
            b = ex % 2
            if ex + 1 < NE:
                gathers(ex + 1)
            for half in range(2):
                wg, wgb = ws.get(ex * 6 + 2 * half)
                wu, wub = ws.get(ex * 6 + 2 * half + 1)
                for jj in range(4):
                    j = half * 4 + jj
                    b0 = 3 * (j % 2)
                    pg, pu, pc = g.ps[b0], g.ps[b0 + 1], g.ps[b0 + 2]
                    csl = slice(jj * 128, (jj + 1) * 128)
                    mm(k, pg, pg.t[:, 0:512], [(wg[:, kc, csl], xgT.t[:, kc, 0:512]) for kc in range(8)], [wgb, xgT.b])
                    mm(k, pu, pu.t[:, 0:512], [(wu[:, kc, csl], xgT.t[:, kc, 0:512]) for kc in range(8)], [wub, xgT.b])
                    sg = sgs[j % 2]
                    k.op("act", lambda e, pg=pg, sg=sg: e.activation(out=sg.t[:, 0:512], in_=pg.t[:, 0:512], func=AF.Silu), reads=[pg.b], writes=[sg.b])
                    if with_ctx:
                        mm(k, pc, pc.t[:, 0:64], [(wg[:, kc, csl], xgT.t[:, kc, 512:576]) for kc in range(8)], [wgb, xgT.b])
                        mm(k, pc, pc.t[:, 64:128], [(wu[:, kc, csl], xgT.t[:, kc, 512:576]) for kc in range(8)], [wub, xgT.b])
                        k.op("act", lambda e, pc=pc, sg=sg: e.activation(out=sg.t[:, 512:576], in_=pc.t[:, 0:64], func=AF.Silu), reads=[pc.b], writes=[sg.b])
                    k.op("dve", lambda e, pu=pu, sg=sg, j=j: e.tensor_tensor(out=hid.t[:, j, 0:512], in0=pu.t[:, 0:512], in1=sg.t[:, 0:512], op=ALU.mult),
                         reads=[pu.b, sg.b], writes=[hid.b])
                    if with_ctx:
                        k.op("dve", lambda e, pc=pc, sg=sg, j=j: e.tensor_tensor(out=hid.t[:, j, 512:576], in0=pc.t[:, 64:128], in1=sg.t[:, 512:576], op=ALU.mult),
                             reads=[pc.b, sg.b], writes=[hid.b])
            if ex + 1 < NE:
                transposes(ex + 1)
            wdl, wdlb = ws.get(ex * 6 + 4)
            wdh, wdhb = ws.get(ex * 6 + 5)
            prev = k.dall("moeacc", ex - 1) if ex > 0 else []
            for ch in range(nch):
                ia, ib, P = idx_ap(ch, ex)
                for oh, (wd, wdb) in enumerate(((wdl, wdlb), (wdh, wdhb))):
                    py = g.ps[6 + (ti[0] % 2)]
                    ti[0] += 1
                    mm(k, py, py.t[0:P, 0:512], [(hid.t[:, kc, ch * 128:ch * 128 + P], wd[:, kc, 0:512]) for kc in range(8)], [hid.b, wdb])
                    k.op("act", lambda e, py=py, ch=ch, P=P, oh=oh, ex=ex, b=b: e.activation(out=ysb[ch].t[0:P, oh * 512:(oh + 1) * 512], in_=py.t[0:P, 0:512], func=AF.Copy,
                                                                                           scale=ag[b][ch].t[0:P, ex:ex + 1]), reads=[py.b, ag[b][ch].b], writes=[ysb[ch].b])
                k.dma("pool", lambda e, ch=ch, P=P, ia=ia: e.indirect_dma_start(
                    out=moe_d[:, :], out_offset=bass.IndirectOffsetOnAxis(ap=ia, axis=0), in_=ysb[ch].t[0:P, :], in_offset=None, compute_op=ALU.add),
                    reads=[ysb[ch].b, ib] + zreads + prev, writes=[k.dbuf("moeacc", ex, ch)])
        k.barrier()


def phase_ln2(nc, k, g, l, s, dr):
    last = (l == L - 1)
    with ExitStack() as ph:
        zs = [_sb(nc, ph, "z%d" % i, [128, 8, 512], F32) for i in range(2)]
        sq = _sb(nc, ph, "sq", [128, 8, 512], BF16)
        z16 = _sb(nc, ph, "z16", [128, 8, 512], BF16)
        mean = _sb(nc, ph, "mean", [128, 512], F32)
        rstd = _sb(nc, ph, "rstd", [128, 512], F32)
        tts = [_sb(nc, ph, "tt%d" % i, [128, 512], F32) for i in range(2)]
        mrows = [[_sb(nc, ph, "mrow%d_%d" % (b, i), [128, 1024], F32) for i in range(4)] for b in range(2)]
        tti = [0]
        blocks = [(bi, c0, N) for bi, (c0, N) in enumerate(BLOCKS) if not (bi == 0 and last)]

        def stage_t(ix):
            bi, c0, N = blocks[ix]
            z, mrow = zs[ix % 2], mrows[ix % 2]
            j = 2 if bi == 0 else s
            k.dma("sp", lambda e: e.dma_start(out=z.t[:, :, 0:N], in_=dr["x1res"][s, :, c0:c0 + N].rearrange("(kc p) n -> p kc n", p=128)),
                  reads=k.dall("x1", s), writes=[z.b])
            ntk = N // 128
            for tk in range(ntk):
                r0 = s * TT + c0 + tk * 128
                k.dma("sp", lambda e, tk=tk, r0=r0: e.dma_start(out=mrow[tk].t[:, :], in_=dr["moe_d"][r0:r0 + 128, :]),
                      reads=k.dall("moez") + k.dall("moeacc"), writes=[mrow[tk].b])
            for oc in range(8):
                pm = g.ps[oc % 4]
                for tk in range(ntk):
                    k.op("pe", lambda e, tk=tk, oc=oc, pm=pm: e.transpose(pm.t[:, tk * 128:(tk + 1) * 128], mrow[tk].t[:, oc * 128:(oc + 1) * 128], g.ident_f.t[:, :]),
                         reads=[mrow[tk].b, g.ident_f.b], writes=[pm.b])
                t1 = tts[tti[0] % 2]
                tti[0] += 1
                k.op("act", lambda e, pm=pm, t1=t1, oc=oc: e.activation(out=t1.t[:, 0:N], in_=pm.t[:, 0:N], func=AF.Copy, scale=g.mod.t[:, l, 40 + oc, j:j + 1]),
                     reads=[pm.b, g.mod.b], writes=[t1.b])
                k.op("dve", lambda e, t1=t1, oc=oc: e.scalar_tensor_tensor(out=z.t[:, oc, 0:N], in0=z.t[:, oc, 0:N], scalar=ALPHA, in1=t1.t[:, 0:N], op0=ALU.mult, op1=ALU.add),
                     reads=[z.b, t1.b], writes=[z.b])

        def stage_n(ix):
            bi, c0, N = blocks[ix]
            z = zs[ix % 2]
            col_stats(k, g, [z.t[:, oc, 0:N] for oc in range(8)], z.b, N, 1024, LN_EPS, True, sq, mean, rstd, z16)
            for oc in range(8):
                t1 = tts[tti[0] % 2]
                tti[0] += 1
                k.op("pool", lambda e, t1=t1, oc=oc: e.tensor_tensor(out=t1.t[:, 0:N], in0=z.t[:, oc, 0:N], in1=mean.t[:, 0:N], op=ALU.subtract), reads=[z.b, mean.b], writes=[t1.b])
                k.op("dve", lambda e, t1=t1, oc=oc: e.scalar_tensor_tensor(out=t1.t[:, 0:N], in0=t1.t[:, 0:N], scalar=g.vecs.t[:, l, V_LN2G + oc:V_LN2G + oc + 1], in1=rstd.t[:, 0:N],
                                                                          op0=ALU.mult, op1=ALU.mult), reads=[t1.b, rstd.b, g.vecs.b], writes=[t1.b])
                k.op("act", lambda e, t1=t1, oc=oc: e.activation(out=z.t[:, oc, 0:N], in_=t1.t[:, 0:N], func=AF.Identity, bias=g.vecs.t[:, l, V_LN2B + oc:V_LN2B + oc + 1], scale=1.0),
                     reads=[t1.b, g.vecs.b], writes=[z.b])
            if last:
                k.dma("sp", lambda e: e.dma_start(out=dr["outT"][s, :, c0 - C:c0 - C + N].rearrange("(kc p) n -> p kc n", p=128), in_=z.t[:, :, 0:N]),
                      reads=[z.b], writes=[k.dbuf("out", s, bi)])
            else:
                k.dma("sp", lambda e: e.dma_start(out=dr["xres"][s, :, c0:c0 + N].rearrange("(kc p) n -> p kc n", p=128), in_=z.t[:, :, 0:N]),
                      reads=[z.b], writes=[k.dbuf("x", s, bi)])

        stage_t(0)
        for ix in range(len(blocks)):
            if ix + 1 < len(blocks):
                stage_t(ix + 1)
            stage_n(ix)
        k.barrier()


def layer(nc, k, g, l, dr):
    stop = dr.get("stop_after")
    for s in range(NS):
        _plog(k, "L%d s%d proj" % (l, s))
        phase_proj(nc, k, g, l, s, dr)
        if stop == "proj":
            continue
        _plog(k, "L%d s%d mla_prep" % (l, s))
        phase_mla_prep(nc, k, g, l, s, dr)
        if stop == "mla_prep":
            return
        _plog(k, "L%d s%d na" % (l, s))
        phase_na(nc, k, g, l, s, dr)
        if stop == "na":
            return
        _plog(k, "L%d s%d mla" % (l, s))
        phase_mla(nc, k, g, l, s, dr)
        if stop == "mla":
            return
    if stop in ("proj", "mix"):
        return
    for s in range(NS):
        _plog(k, "L%d s%d merge" % (l, s))
        phase_merge(nc, k, g, l, s, dr)
    if stop == "merge":
        return
    _plog(k, "L%d moe" % l)
    phase_moe(nc, k, g, l, dr)
    if stop == "moe":
        return
    for s in range(NS):
        _plog(k, "L%d s%d ln2" % (l, s))
        phase_ln2(nc, k, g, l, s, dr)
    _plog(k, "L%d end" % l)


def _prep_inputs(inputs):
    f = lambda a: np.ascontiguousarray(np.asarray(a, dtype=np.float32))
    x, c, ctx, c_ctx = f(inputs["x"]), f(inputs["c"]), f(inputs["ctx"]), f(inputs["c_ctx"])
    shared = {}
    for nm in ("w_ada", "w_in", "w_uq", "w_ukv", "w_pw2", "w_oa", "w_ob", "w_out", "w_router", "w_gate", "w_up", "w_down"):
        shared[nm] = f(inputs[nm])
    w_in = shared["w_in"]
    kr = w_in[:, :, 1920:1952]
    sw = np.concatenate([np.arange(8, 16), np.arange(0, 8), np.arange(24, 32), np.arange(16, 24)])
    w_kr = np.zeros((L, D, 192), np.float32)
    w_kr[:, :, 64:96] = kr
    w_kr[:, :, 160:192] = kr[:, :, sw]
    shared["w_kr"] = w_kr
    wq = shared["w_uq"].reshape(L, 256, 8, 96)
    wqs = wq.copy()
    wqs[:, :, :, 64:96] = wq[:, :, :, 64:96][:, :, :, sw]
    shared["w_uq_sw"] = np.ascontiguousarray(wqs.reshape(L, 256, 768))
    vecs = np.zeros((L, 128, NV), np.float32)

    def put(col, v, nch):
        vecs[:, :, col:col + nch] = f(v).reshape(L, nch, 128).transpose(0, 2, 1)

    put(V_BADA, inputs["b_ada"], 48)
    put(V_LN1G, inputs["ln1_g"], 8)
    put(V_LN1B, inputs["ln1_b"], 8)
    put(V_LN2G, inputs["ln2_g"], 8)
    put(V_LN2B, inputs["ln2_b"], 8)
    put(V_GQ, inputs["g_q"], 2)
    put(V_GKV, inputs["g_kv"], 1)
    put(V_BDW, inputs["b_dw"], 4)
    put(V_GCN, inputs["g_cn"], 4)
    put(V_BCN, inputs["b_cn"], 4)
    wdw = f(inputs["w_dw"])
    vecs[:, :, V_WDW:V_WDW + 124] = wdw.reshape(L, 31, 4, 128).transpose(0, 3, 2, 1).reshape(L, 128, 124)
    shared["vecs"] = vecs
    shared["ident"] = np.eye(128, dtype=np.float32)
    t = np.arange(T)
    row = (t // 64).astype(np.float32)
    col = (t % 64).astype(np.float32)
    inv = (np.float32(10000.0) ** (-np.arange(8, dtype=np.float32) / np.float32(8))).astype(np.float32)
    ar = row[None, :] * inv[:, None]
    ac = col[None, :] * inv[:, None]
    Cc = np.ones((32, TT), np.float32)
    Ss = np.zeros((32, TT), np.float32)
    Cc[0:8, C:] = np.cos(ar); Cc[8:16, C:] = np.cos(ar); Cc[16:24, C:] = np.cos(ac); Cc[24:32, C:] = np.cos(ac)
    Ss[0:8, C:] = -np.sin(ar); Ss[8:16, C:] = np.sin(ar); Ss[16:24, C:] = -np.sin(ac); Ss[24:32, C:] = np.sin(ac)
    shared["ropeC"] = Cc
    shared["ropeS"] = Ss
    rpb = f(inputs["rpb"])
    rpb_ext = np.concatenate([rpb, np.full((L, 8, 15, 1), NEG, np.float32)], axis=-1)
    qc = np.arange(64)[:, None]
    kc = np.arange(64)[None, :]
    cstart = np.clip(qc - 8, 0, 48)
    valid = (kc >= cstart) & (kc < cstart + 16)
    idx = np.clip(kc - qc, -15, 15) + 15
    idx = np.where(valid, idx, 31)
    bt = rpb_ext[:, :, :, idx]
    bt = bt.transpose(0, 3, 1, 2, 4).reshape(L, 64, 7680)
    shared["biasT"] = np.ascontiguousarray(np.concatenate([bt, bt], axis=1))
    in_maps = []
    for i in range(NCORES):
        s0 = i * NS
        xT = np.empty((NS, D, TT), np.float32)
        for j in range(NS):
            xT[j, :, :C] = ctx[s0 + j].T
            xT[j, :, C:] = x[s0 + j].T
        cT = np.empty((128, 8, 3), np.float32)
        for j in range(NS):
            cT[:, :, j] = c[s0 + j].reshape(8, 128).T
        cT[:, :, 2] = c_ctx.reshape(8, 128).T
        m = dict(shared)
        m["xT"] = xT
        m["cT"] = cT
        in_maps.append(m)
    return in_maps


_NC_CACHE = {}


def kernel(**inputs):
    in_maps = _prep_inputs(inputs)
    if "nc" not in _NC_CACHE:
        _NC_CACHE["nc"] = build()
    nc = _NC_CACHE["nc"]
    res = run_bass_kernel_spmd(nc, in_maps, core_ids=list(range(NCORES)))
    out = np.empty((NCORES * NS, T, D), np.float32)
    for i in range(NCORES):
        o = res.results[i]["outT"]
        for j in range(NS):
            out[i * NS + j] = o[j].T
    return out
```

```python
import numpy as np
from contextlib import ExitStack
import concourse.bass as bass
import concourse.mybir as mybir
from concourse.bass_utils import run_bass_kernel_spmd

F32 = mybir.dt.float32
BF16 = mybir.dt.bfloat16
I32 = mybir.dt.int32
U32 = mybir.dt.uint32
AF = mybir.ActivationFunctionType
ALU = mybir.AluOpType

NCORES = 8
NS = 2
D = 1024
T = 2048
C = 256
TT = C + T
L = 4
NE = 16
IN_COLS = 6048
ALPHA = float((2 * L) ** 0.25)
LN_EPS = 1e-5
RMS_EPS = 1e-6
NEG = -30000.0
BLOCKS = [(0, 256), (256, 512), (768, 512), (1280, 512), (1792, 512)]
V_BADA = 0
V_LN1G = 48
V_LN1B = 56
V_LN2G = 64
V_LN2B = 72
V_GQ = 80
V_GKV = 82
V_BDW = 83
V_GCN = 87
V_BCN = 91
V_WDW = 95
NV = 95 + 124


class Buf:
    __slots__ = ("name", "w", "r")

    def __init__(self, name=""):
        self.name = name
        self.w = None
        self.r = {}


class Tile:
    def __init__(self, t, name):
        self.t = t
        self.b = Buf(name)


class KB:
    NDMA = 32
    NPDMA = 16

    def __init__(self, nc, es):
        self.nc = nc
        self.engs = {"pe": nc.tensor, "act": nc.scalar, "dve": nc.vector, "pool": nc.gpsimd, "sp": nc.sync}
        self.sems = {}
        for e in ("pe", "act", "dve", "pool"):
            self.sems[e] = es.enter_context(nc.semaphore("s_" + e))
        self.cnt = {e: 0 for e in self.sems}
        self.dsem = [es.enter_context(nc.semaphore("s_dma%d" % i)) for i in range(self.NDMA + self.NPDMA)]
        self.duse = [0] * (self.NDMA + self.NPDMA)
        self.dnext = 0
        self.pnext = 0
        self.waited = {}
        self.dbufs = {}
        self.ninst = 0

    def _sem(self, key):
        if isinstance(key, tuple):
            return self.dsem[key[1]]
        return self.sems[key]

    def _wait(self, F, deps):
        eng = self.engs[F]
        for key, val in deps.items():
            if val <= 0:
                continue
            if F == "pe" and key == "pe":
                continue
            if self.waited.get((F, key), 0) >= val:
                continue
            eng.wait_ge(self._sem(key), val)
            self.waited[(F, key)] = val
            self.ninst += 1

    def _collect(self, reads, writes):
        deps = {}

        def add(k, v):
            if deps.get(k, 0) < v:
                deps[k] = v

        for b in reads:
            if b.w is not None:
                add(*b.w)
        for b in writes:
            if b.w is not None:
                add(*b.w)
            for k, v in b.r.items():
                add(k, v)
        return deps

    def _mark(self, tok, reads, writes):
        k, v = tok
        for b in reads:
            if b.r.get(k, 0) < v:
                b.r[k] = v
        for b in writes:
            b.w = tok
            b.r = {}

    def op(self, F, fn, reads=(), writes=()):
        self._wait(F, self._collect(reads, writes))
        inst = fn(self.engs[F])
        self.cnt[F] += 1
        inst.then_inc(self.sems[F], 1)
        self._mark((F, self.cnt[F]), reads, writes)
        self.ninst += 1
        return inst

    def dma(self, Q, fn, reads=(), writes=()):
        if Q == "pool":
            i = self.NDMA + self.pnext
            self.pnext = (self.pnext + 1) % self.NPDMA
        else:
            i = self.dnext
            self.dnext = (self.dnext + 1) % self.NDMA
        key = ("d", i)
        deps = self._collect(reads, writes)
        prev = 16 * self.duse[i]
        if prev and deps.get(key, 0) < prev:
            deps[key] = prev
        self._wait(Q, deps)
        inst = fn(self.engs[Q])
        self.duse[i] += 1
        inst.then_inc(self.dsem[i], 16)
        self._mark((key, 16 * self.duse[i]), reads, writes)
        self.ninst += 1
        return inst

    def barrier(self):
        deps = {e: self.cnt[e] for e in self.cnt}
        for i in range(self.NDMA + self.NPDMA):
            if self.duse[i]:
                deps[("d", i)] = 16 * self.duse[i]
        self._wait("act", dict(deps))
        inst = self.engs["act"].nop() if False else None
        self.op("act", lambda e: e.activation(out=self.bar_t[:, 0:1], in_=self.bar_t[:, 1:2], func=AF.Copy), reads=(), writes=())
        tok = {"act": self.cnt["act"]}
        for F in self.engs:
            if F != "act":
                self._wait(F, dict(tok))
                for key, val in deps.items():
                    if self.waited.get((F, key), 0) < val:
                        self.waited[(F, key)] = val

    def dall(self, *prefix):
        return [b for key, b in self.dbufs.items() if key[:len(prefix)] == prefix]

    def dbuf(self, *key):
        b = self.dbufs.get(key)
        if b is None:
            b = Buf(str(key))
            self.dbufs[key] = b
        return b


class Ctx:
    pass


PHASE_LOG = []


def _plog(k, name):
    PHASE_LOG.append((name, dict(k.cnt)))


_UID = [0]


def _sb(nc, es, name, shape, dtype):
    _UID[0] += 1
    return Tile(es.enter_context(nc.sbuf_tensor("sb%d_%s" % (_UID[0], name), list(shape), dtype)), name)


class WStream:
    def __init__(self, k, g, units, pf=2, cast=True):
        self.k, self.g, self.units, self.pf, self.cast = k, g, units, pf, cast
        self.issued = 0
        self.slots = {}

    def _issue(self, i):
        k, g = self.k, self.g
        src, shape = self.units[i]
        st = g.wst[g.wst_i % len(g.wst)]
        g.wst_i += 1
        n = 1
        for d in shape[1:]:
            n *= d
        stv = st.t[:, 0:n]
        if len(shape) == 3:
            stv = stv.rearrange("p (a b) -> p a b", a=shape[1])
        k.dma("sp", lambda e: e.dma_start(out=stv, in_=src), reads=(), writes=[st.b])
        if not self.cast:
            self.slots[i] = (stv, st.b)
            return
        bf = g.wbf[g.wbf_i % len(g.wbf)]
        g.wbf_i += 1
        bfv = bf.t[:, 0:n]
        if len(shape) == 3:
            bfv = bfv.rearrange("p (a b) -> p a b", a=shape[1])
        ce = g.cast_engs[g.cast_i % len(g.cast_engs)]
        g.cast_i += 1
        if ce == "act":
            k.op("act", lambda e: e.copy(out=bfv, in_=stv), reads=[st.b], writes=[bf.b])
        else:
            k.op(ce, lambda e: e.tensor_copy(out=bfv, in_=stv), reads=[st.b], writes=[bf.b])
        self.slots[i] = (bfv, bf.b)

    def get(self, i):
        while self.issued < min(len(self.units), i + 1 + self.pf):
            self._issue(self.issued)
            self.issued += 1
        r = self.slots[i]
        if i - 1 in self.slots and i - 1 >= 0:
            pass
        return r


def mm(k, ps, out_ap, pairs, reads):
    n = len(pairs)
    for i, (l, r) in enumerate(pairs):
        k.op("pe", lambda e, l=l, r=r, i=i: e.matmul(out_ap, lhsT=l, rhs=r, start=(i == 0), stop=(i == n - 1)),
             reads=reads, writes=[ps.b])


def col_stats(k, g, zs, zb, N, nfeat, eps, mean_needed, sq, mean_t, rstd_t, z16=None):
    nch = len(zs)
    ps1 = g.ps[6]
    ps2 = g.ps[7]
    for c, z in enumerate(zs):
        k.op("act", lambda e, c=c, z=z: e.activation(out=sq.t[:, c, 0:N], in_=z, func=AF.Square), reads=[zb], writes=[sq.b])
    if mean_needed:
        for c, z in enumerate(zs):
            k.op("act", lambda e, c=c, z=z: e.copy(out=z16.t[:, c, 0:N], in_=z), reads=[zb], writes=[z16.b])
        mm(k, ps1, ps1.t[:, 0:N], [(g.ones_b.t[:, :], z16.t[:, c, 0:N]) for c in range(nch)], [z16.b, g.ones_b.b])
    mm(k, ps2, ps2.t[:, 0:N], [(g.ones_b.t[:, :], sq.t[:, c, 0:N]) for c in range(nch)], [sq.b, g.ones_b.b])
    inv = 1.0 / nfeat
    if mean_needed:
        k.op("act", lambda e: e.activation(out=mean_t.t[:, 0:N], in_=ps1.t[:, 0:N], func=AF.Copy, scale=inv),
             reads=[ps1.b], writes=[mean_t.b])
        k.op("dve", lambda e: e.tensor_tensor(out=rstd_t.t[:, 0:N], in0=mean_t.t[:, 0:N], in1=mean_t.t[:, 0:N], op=ALU.mult),
             reads=[mean_t.b], writes=[rstd_t.b])
        k.op("dve", lambda e: e.scalar_tensor_tensor(out=rstd_t.t[:, 0:N], in0=ps2.t[:, 0:N], scalar=inv, in1=rstd_t.t[:, 0:N],
                                                     op0=ALU.mult, op1=ALU.subtract),
             reads=[ps2.b, rstd_t.b], writes=[rstd_t.b])
        k.op("act", lambda e: e.activation(out=rstd_t.t[:, 0:N], in_=rstd_t.t[:, 0:N], func=AF.Sqrt, bias=g.eps_ln.t[:, 0:1], scale=1.0),
             reads=[rstd_t.b, g.eps_ln.b], writes=[rstd_t.b])
    else:
        k.op("act", lambda e: e.activation(out=rstd_t.t[:, 0:N], in_=ps2.t[:, 0:N], func=AF.Sqrt, bias=g.eps_rms.t[:, 0:1], scale=inv),
             reads=[ps2.b, g.eps_rms.b], writes=[rstd_t.b])
    k.op("dve", lambda e: e.reciprocal(out=rstd_t.t[:, 0:N], in_=rstd_t.t[:, 0:N]), reads=[rstd_t.b], writes=[rstd_t.b])


class AttJob:
    def __init__(self, k, g, N, q_ap, q_bufs, chunks, scale, hp, out_ap, out_buf, sbanks, obank, dbank, pts, rec):
        self.k, self.g, self.N, self.q_ap, self.q_bufs, self.chunks = k, g, N, q_ap, q_bufs, chunks
        self.scale, self.hp, self.out_ap, self.out_buf = scale, hp, out_ap, out_buf
        self.sbanks, self.obank, self.dbank, self.pts, self.rec = sbanks, obank, dbank, pts, rec
        G = max(1, 512 // N)
        self.groups = [(ci, chunks[ci:ci + G]) for ci in range(0, len(chunks), G)]
        self.ngroups = len(self.groups)
        self.sb = {}

    def qk(self, gi):
        k, g, N = self.k, self.g, self.N
        ci, grp = self.groups[gi]
        sb_ = self.sbanks[g.sb_i % len(self.sbanks)]
        g.sb_i += 1
        self.sb[gi] = sb_
        for j, ch in enumerate(grp):
            o = sb_.t[:, j * N:(j + 1) * N]
            has_b = ch.get("bias") is not None
            k.op("pe", lambda e, o=o, ch=ch, has_b=has_b: e.matmul(o, lhsT=ch["kT"], rhs=self.q_ap, start=True, stop=not has_b),
                 reads=list(ch["bufs"]) + list(self.q_bufs), writes=[sb_.b])
            if has_b:
                bl, br, bb = ch["bias"]
                k.op("pe", lambda e, o=o, bl=bl, br=br: e.matmul(o, lhsT=bl, rhs=br, start=False, stop=True), reads=bb, writes=[sb_.b])

    def rest(self, gi):
        k, g, N = self.k, self.g, self.N
        ci, grp = self.groups[gi]
        nchunks = len(self.chunks)
        sb_ = self.sb.pop(gi)
        pt = self.pts[g.pt_i % len(self.pts)]
        g.pt_i += 1
        w = len(grp) * N
        k.op("act", lambda e: e.activation(out=pt.t[:, 0:w], in_=sb_.t[:, 0:w], func=AF.Exp, scale=self.scale), reads=[sb_.b], writes=[pt.b])
        for j, ch in enumerate(grp):
            first, last = (ci + j == 0), (ci + j == nchunks - 1)
            p_ap = pt.t[:, j * N:(j + 1) * N]
            k.op("pe", lambda e, ch=ch, p_ap=p_ap, first=first, last=last: e.matmul(self.obank.t[:, 0:N], lhsT=ch["v"], rhs=p_ap, start=first, stop=last),
                 reads=list(ch["bufs"]) + [pt.b], writes=[self.obank.b])

    def fin(self):
        k, N, rec = self.k, self.N, self.rec
        r0 = self.hp * 64
        d0 = 64 - r0
        if N >= 256:
            k.op("act", lambda e: e.activation(out=rec.t[r0:r0 + 64, 0:N], in_=self.obank.t[d0:d0 + 64, 0:N], func=AF.Ln), reads=[self.obank.b], writes=[rec.b])
            k.op("act", lambda e: e.activation(out=rec.t[r0:r0 + 64, 0:N], in_=rec.t[r0:r0 + 64, 0:N], func=AF.Exp, scale=-1.0), reads=[rec.b], writes=[rec.b])
        else:
            k.op("dve", lambda e: e.reciprocal(out=rec.t[r0:r0 + 64, 0:N], in_=self.obank.t[d0:d0 + 64, 0:N]), reads=[self.obank.b], writes=[rec.b])
        k.op("dve", lambda e: e.tensor_tensor(out=self.out_ap, in0=self.obank.t[r0:r0 + 64, 0:N], in1=rec.t[r0:r0 + 64, 0:N], op=ALU.mult),
             reads=[self.obank.b, rec.b], writes=[self.out_buf])


def run_attention(jobs, side=None):
    flat = [(job, gi) for job in jobs for gi in range(job.ngroups)]
    if not flat:
        return
    LA = 3
    issued = 0
    pending = None
    for i, (job, gi) in enumerate(flat):
        while issued < min(len(flat), i + 1 + LA):
            flat[issued][0].qk(flat[issued][1])
            issued += 1
        job.rest(gi)
        if pending is not None:
            pending.fin()
            pending = None
        if side is not None:
            side()
        if gi == job.ngroups - 1:
            pending = job
    if pending is not None:
        pending.fin()


def attention(k, g, N, q_ap, q_bufs, chunks, scale, hp, out_ap, out_buf, sbanks, obank, dbank, pts, rec):
    return AttJob(k, g, N, q_ap, q_bufs, chunks, scale, hp, out_ap, out_buf, sbanks, obank, dbank, pts, rec)


def build(n_layers=L, dbg=(), stop_after=None):
    nc = bass.Bass("TRN2", target_bir_lowering=False)
    g = Ctx()

    def din(name, shape, dt=F32):
        return nc.dram_tensor(name, list(shape), dt, kind="ExternalInput").ap()

    def dscr(name, shape, dt):
        kind = "ExternalOutput" if name in dbg else "Internal"
        return nc.dram_tensor(name, list(shape), dt, kind=kind).ap()

    xT_in = din("xT", [NS, D, TT])
    cT_in = din("cT", [128, 8, 3])
    vecs_in = din("vecs", [L, 128, NV])
    ident_in = din("ident", [128, 128])
    ropeC_in = din("ropeC", [32, TT])
    ropeS_in = din("ropeS", [32, TT])
    biasT_rep = din("biasT", [L, 128, 7680])
    w_ada = din("w_ada", [L, D, 6 * D])
    w_in = din("w_in", [L, D, IN_COLS])
    w_kr = din("w_kr", [L, D, 192])
    w_uq = din("w_uq", [L, 256, 768])
    w_uqs = din("w_uq_sw", [L, 256, 768])
    w_ukv = din("w_ukv", [L, 128, 1024])
    w_pw2 = din("w_pw2", [L, 512, D])
    w_oa = din("w_oa", [L, 512, D])
    w_ob = din("w_ob", [L, 512, D])
    w_out = din("w_out", [L, D, D])
    w_router = din("w_router", [L, D, NE])
    w_gate = din("w_gate", [L, NE, D, D])
    w_up = din("w_up", [L, NE, D, D])
    w_down = din("w_down", [L, NE, D, D])
    outT = nc.dram_tensor("outT", [NS, D, T], F32, kind="ExternalOutput").ap()

    xres = dscr("xres", [NS, D, TT], F32)
    x1res = dscr("x1res", [NS, D, TT], F32)
    qaT_d = dscr("qaT_d", [NS, 512, TT], BF16)
    kaT_d = dscr("kaT_d", [NS, 512, TT], BF16)
    vae_d = dscr("vae_d", [NS, 18, 128, 768], BF16)
    vao_d = dscr("vao_d", [NS, 15, 128, 768], BF16)
    cq_d = dscr("cq_d", [NS, 256, TT], F32)
    ckv_d = dscr("ckv_d", [NS, 128, TT], F32)
    kr_d = dscr("kr_d", [NS, 2, 32, TT], F32)
    u_d = dscr("u_d", [NS, 512, TT], F32)
    gT_d = dscr("gT_d", [NS, 3072, TT], BF16)
    qbT_d = dscr("qbT_d", [NS, 8, 96, TT], BF16)
    kbT_d = dscr("kbT_d", [NS, 8, 96, TT], BF16)
    vb_d = dscr("vb_d", [NS, 18, 128, 768], BF16)
    yaT_d = dscr("yaT_d", [NS, 512, TT], BF16)
    ybT_d = dscr("ybT_d", [NS, 512, TT], BF16)
    ucT_d = dscr("ucT_d", [NS, 512, TT], BF16)
    h2tok_d = dscr("h2tok_d", [NS * TT, D], BF16)
    afftok_d = dscr("afftok_d", [NS * TT, NE], F32)
    moe_d = dscr("moe_d", [NS * TT, D], F32)
    mod_d = dscr("mod_d", [L, 128, 48, 3], F32)
    affT_d = dscr("affT_d", [NS, NE, TT], F32)

    with ExitStack() as es:
        k = KB(nc, es)
        g.k = k
        k.bar_t = es.enter_context(nc.sbuf_tensor("sb_bar", [128, 2], F32))
        nc.vector.memset(k.bar_t[:, :], 0.0)
        g.ps = [Tile(es.enter_context(nc.psum_tensor("ps%d" % i, [128, 512], F32)), "ps%d" % i) for i in range(8)]
        g.ident_f = _sb(nc, es, "ident_f", [128, 128], F32)
        g.ident_b = _sb(nc, es, "ident_b", [128, 128], BF16)
        g.ones_f = _sb(nc, es, "ones_f", [128, 128], F32)
        g.ones_b = _sb(nc, es, "ones_b", [128, 128], BF16)
        g.eps_ln = _sb(nc, es, "eps_ln", [128, 1], F32)
        g.eps_rms = _sb(nc, es, "eps_rms", [128, 1], F32)
        g.vecs = _sb(nc, es, "vecs", [128, L, NV], F32)
        g.mod = _sb(nc, es, "mod", [128, L, 48, 3], F32)
        g.wst_i = g.wbf_i = g.cast_i = 0
        g.cast_engs = ["act", "pool"]
        g.sb_i = g.pt_i = 0

        k.dma("sp", lambda e: e.dma_start(out=g.ident_f.t[:, :], in_=ident_in[:, :]), writes=[g.ident_f.b])
        k.op("dve", lambda e: e.tensor_copy(out=g.ident_b.t[:, :], in_=g.ident_f.t[:, :]), reads=[g.ident_f.b], writes=[g.ident_b.b])
        k.op("dve", lambda e: e.memset(g.ones_f.t[:, :], 1.0), writes=[g.ones_f.b])
        k.op("dve", lambda e: e.memset(g.ones_b.t[:, :], 1.0), writes=[g.ones_b.b])
        k.op("dve", lambda e: e.memset(g.eps_ln.t[:, :], LN_EPS), writes=[g.eps_ln.b])
        k.op("dve", lambda e: e.memset(g.eps_rms.t[:, :], RMS_EPS), writes=[g.eps_rms.b])
        for l in range(L):
            k.dma("sp", lambda e, l=l: e.dma_start(out=g.vecs.t[:, l, :], in_=vecs_in[l, :, :]), writes=[g.vecs.b])

        with ExitStack() as ph:
            ring(nc, g, ph, 3, 0)
            cT = _sb(nc, ph, "cT", [128, 8, 3], F32)
            k.dma("sp", lambda e: e.dma_start(out=cT.t[:, :, :], in_=cT_in[:, :, :]), writes=[cT.b])
            k.op("act", lambda e: e.activation(out=cT.t[:, :, :], in_=cT.t[:, :, :], func=AF.Silu), reads=[cT.b], writes=[cT.b])
            astg = _sb(nc, ph, "astg", [3, 6 * D], F32)
            for l in range(n_layers):
                units = [(w_ada[l, :, gi * 512:(gi + 1) * 512].rearrange("(kc p) n -> p kc n", p=128), (128, 8, 512)) for gi in range(12)]
                ws = WStream(k, g, units, pf=2, cast=False)
                for gi in range(12):
                    wv, wb = ws.get(gi)
                    ps = g.ps[gi % 4]
                    mm(k, ps, ps.t[0:3, 0:512], [(cT.t[:, kc, :], wv[:, kc, :]) for kc in range(8)], [wb, cT.b])
                    k.op("act", lambda e, ps=ps, gi=gi: e.copy(out=astg.t[:, gi * 512:(gi + 1) * 512], in_=ps.t[0:3, 0:512]), reads=[ps.b], writes=[astg.b])
                pt_ = g.ps[4 + (l % 2)]
                for oc in range(48):
                    k.op("pe", lambda e, oc=oc, pt_=pt_: e.transpose(pt_.t[:, oc * 3:(oc + 1) * 3], astg.t[0:3, oc * 128:(oc + 1) * 128], g.ident_f.t[0:3, 0:3]),
                         reads=[astg.b, g.ident_f.b], writes=[pt_.b])
                p3 = pt_.t[:, 0:144].rearrange("p (o j) -> p o j", j=3)
                for j in range(3):
                    k.op("dve", lambda e, j=j, l=l, p3=p3: e.tensor_tensor(out=g.mod.t[:, l, :, j], in0=p3[:, :, j], in1=g.vecs.t[:, l, V_BADA:V_BADA + 48], op=ALU.add),
                         reads=[pt_.b, g.vecs.b], writes=[g.mod.b])
                for r in (1, 4):
                    k.op("dve", lambda e, r=r, l=l: e.tensor_scalar(out=g.mod.t[:, l, r * 8:(r + 1) * 8, :], in0=g.mod.t[:, l, r * 8:(r + 1) * 8, :],
                                                                    scalar1=1.0, scalar2=None, op0=ALU.add), reads=[g.mod.b], writes=[g.mod.b])
            mm(k, g.ps[7], g.ps[7].t[0:64, 0:8], [(g.ones_b.t[:, 0:64], g.ones_b.t[:, 0:8])], [g.ones_b.b])
            if "mod_d" in dbg:
                for l in range(n_layers):
                    k.dma("sp", lambda e, l=l: e.dma_start(out=mod_d[l], in_=g.mod.t[:, l, :, :]), reads=[g.mod.b], writes=[k.dbuf("mod_d")])
            k.barrier()

        def modv(l, role, c, j):
            return g.mod.t[:, l, role * 8 + c, j:j + 1]

        if stop_after != "ada":
            for l in range(n_layers):
                layer(nc, k, g, l, locals())
        k.barrier()
    return nc


def ring(nc, g, ph, nst, nbf, cast_engs=("act", "pool", "dve")):
    g.cast_engs = list(cast_engs)
    g.wst = [_sb(nc, ph, "wst%d" % i, [128, 4096], F32) for i in range(nst)]
    g.wbf = [_sb(nc, ph, "wbf%d" % i, [128, 4096], BF16) for i in range(nbf)]


def load_cast(k, g, src, shape, dst_view, dst_buf, eng=None):
    st = g.wst[g.wst_i % len(g.wst)]
    g.wst_i += 1
    n = 1
    for d in shape[1:]:
        n *= d
    stv = st.t[0:shape[0], 0:n]
    if len(shape) == 3:
        stv = stv.rearrange("p (a b) -> p a b", a=shape[1])
    k.dma("sp", lambda e: e.dma_start(out=stv, in_=src), writes=[st.b])
    if eng is None:
        eng = g.cast_engs[g.cast_i % len(g.cast_engs)]
        g.cast_i += 1
    if eng == "act":
        k.op("act", lambda e: e.copy(out=dst_view, in_=stv), reads=[st.b], writes=[dst_buf])
    else:
        k.op(eng, lambda e: e.tensor_copy(out=dst_view, in_=stv), reads=[st.b], writes=[dst_buf])


def phase_proj(nc, k, g, l, s, dr):
    X = dr["xT_in"] if l == 0 else dr["xres"]
    w_in = dr["w_in"]
    with ExitStack() as ph:
        ring(nc, g, ph, 3, 4)
        hT = _sb(nc, ph, "hT", [128, 8, TT], BF16)
        hTb = [Buf("hT%d" % i) for i in range(len(BLOCKS))]
        xin = [_sb(nc, ph, "xin%d" % i, [128, 8, 512], F32) for i in range(1)]
        stg_b = [_sb(nc, ph, "stgb%d" % i, [128, 4, 512], BF16) for i in range(2)]
        stg_f = [_sb(nc, ph, "stgf%d" % i, [128, 4, 512], F32) for i in range(2)]
        sig = [_sb(nc, ph, "sig%d" % i, [128, 512], F32) for i in range(2)]
        vst = [_sb(nc, ph, "vst%d" % i, [128, 768], BF16) for i in range(2)]
        for v_ in vst:
            k.op("pool", lambda e, v_=v_: e.memset(v_.t[:, :], 1.0), writes=[v_.b])
        for bi, (c0, N) in enumerate(BLOCKS):
            xt = xin[0]
            j = 2 if bi == 0 else s
            k.dma("sp", lambda e, xt=xt, c0=c0, N=N: e.dma_start(out=xt.t[:, :, 0:N], in_=X[s, :, c0:c0 + N].rearrange("(kc p) n -> p kc n", p=128)),
                  reads=[k.dbuf("x", s, bi)], writes=[xt.b])
            for kc in range(8):
                if bi % 2 == 0:
                    k.op("dve", lambda e, xt=xt, kc=kc, c0=c0, N=N, j=j: e.tensor_scalar(
                        out=hT.t[:, kc, c0:c0 + N], in0=xt.t[:, kc, 0:N], scalar1=g.mod.t[:, l, 8 + kc, j:j + 1],
                        scalar2=g.mod.t[:, l, kc, j:j + 1], op0=ALU.mult, op1=ALU.add), reads=[xt.b, g.mod.b], writes=[hTb[bi]])
                else:
                    k.op("act", lambda e, xt=xt, kc=kc, c0=c0, N=N, j=j: e.activation(
                        out=hT.t[:, kc, c0:c0 + N], in_=xt.t[:, kc, 0:N], func=AF.Identity, scale=g.mod.t[:, l, 8 + kc, j:j + 1],
                        bias=g.mod.t[:, l, kc, j:j + 1]), reads=[xt.b, g.mod.b], writes=[hTb[bi]])

        def wu(c0, n):
            return (w_in[l, :, c0:c0 + n].rearrange("(kc p) n -> p kc n", p=128), (128, 8, n))

        units = [wu(0, 512), wu(512, 512), wu(1024, 512), wu(1536, 256), wu(1792, 128),
                 (dr["w_kr"][l, :, :].rearrange("(kc p) n -> p kc n", p=128), (128, 8, 192)),
                 wu(1952, 512), wu(2464, 512)] + [wu(2976 + 512 * i, 512) for i in range(6)]
        ws = WStream(k, g, units, pf=2)
        cnt = [0]

        def nextps():
            p = g.ps[cnt[0] % 6]
            cnt[0] += 1
            return p

        def fm_tile(wv, wb, mcol, mw, bi):
            c0, N = BLOCKS[bi]
            ps = nextps()
            mm(k, ps, ps.t[0:mw, 0:N], [(wv[:, kc, mcol:mcol + mw], hT.t[:, kc, c0:c0 + N]) for kc in range(8)], [wb, hTb[bi]])
            return ps

        it = [0]
        for u, (dst, name, scl) in enumerate([(dr["qaT_d"], "qaT", 0.125), (dr["kaT_d"], "kaT", 1.0)]):
            wv, wb = ws.get(u)
            for bi, (c0, N) in enumerate(BLOCKS):
                st = stg_b[it[0] % 2]
                it[0] += 1
                for mt in range(4):
                    ps = fm_tile(wv, wb, mt * 128, 128, bi)
                    k.op("dve", lambda e, ps=ps, st=st, mt=mt, N=N, scl=scl: e.tensor_scalar(
                        out=st.t[:, mt, 0:N], in0=ps.t[:, 0:N], scalar1=scl, scalar2=None, op0=ALU.mult), reads=[ps.b], writes=[st.b])
                k.dma("sp", lambda e, st=st, dst=dst, c0=c0, N=N: e.dma_start(
                    out=dst[s, :, c0:c0 + N].rearrange("(m p) n -> p m n", p=128), in_=st.t[:, :, 0:N]),
                    reads=[st.b], writes=[k.dbuf(name, s, bi)])
        wv, wb = ws.get(2)

        def blocks_of(t0, t1):
            return [hTb[bi] for bi, (c0, N) in enumerate(BLOCKS) if t0 < c0 + N and t1 > c0]

        for kind, ntile, base, dst in (("e", 18, 0, dr["vae_d"]), ("o", 15, C + 64, dr["vao_d"])):
            for j in range(ntile):
                t0 = base + 128 * j
                ps = nextps()
                mm(k, ps, ps.t[:, 0:512], [(hT.t[:, kc, t0:t0 + 128], wv[:, kc, 0:512]) for kc in range(8)], [wb] + blocks_of(t0, t0 + 128))
                st = vst[it[0] % 2]
                it[0] += 1
                v3 = st.t[:, :].rearrange("p (q c) -> p q c", c=192)
                p3 = ps.t[:, 0:512].rearrange("p (q c) -> p q c", c=128)
                k.op("act", lambda e, v3=v3, p3=p3: e.copy(out=v3[:, :, 0:64], in_=p3[:, :, 0:64]), reads=[ps.b], writes=[st.b])
                k.op("act", lambda e, v3=v3, p3=p3: e.copy(out=v3[:, :, 128:192], in_=p3[:, :, 64:128]), reads=[ps.b], writes=[st.b])
                k.dma("sp", lambda e, st=st, dst=dst, j=j: e.dma_start(out=dst[s, j, :, :], in_=st.t[:, :]),
                      reads=[st.b], writes=[k.dbuf("va" + kind, s, j)])
        for u, nm, dst, nmt in ((3, "cq", dr["cq_d"], 2), (4, "ckv", dr["ckv_d"], 1)):
            wv, wb = ws.get(u)
            for bi, (c0, N) in enumerate(BLOCKS):
                st = stg_f[it[0] % 2]
                it[0] += 1
                for mt in range(nmt):
                    ps = fm_tile(wv, wb, mt * 128, 128, bi)
                    k.op("dve", lambda e, ps=ps, st=st, mt=mt, N=N: e.tensor_copy(out=st.t[:, mt, 0:N], in_=ps.t[:, 0:N]), reads=[ps.b], writes=[st.b])
                k.dma("sp", lambda e, st=st, dst=dst, c0=c0, N=N, nmt=nmt: e.dma_start(
                    out=dst[s, :, c0:c0 + N].rearrange("(m p) n -> p m n", p=128), in_=st.t[:, 0:nmt, 0:N]),
                    reads=[st.b], writes=[k.dbuf(nm, s, bi)])
        wv, wb = ws.get(5)
        for bi, (c0, N) in enumerate(BLOCKS):
            st = stg_f[it[0] % 2]
            it[0] += 1
            for mt in range(2):
                ps = fm_tile(wv, wb, mt * 96, 96, bi)
                k.op("dve", lambda e, ps=ps, st=st, mt=mt, N=N: e.tensor_copy(out=st.t[64:96, mt, 0:N], in_=ps.t[64:96, 0:N]), reads=[ps.b], writes=[st.b])
            for mt in range(2):
                k.dma("sp", lambda e, st=st, c0=c0, N=N, mt=mt: e.dma_start(out=dr["kr_d"][s, mt, :, c0:c0 + N], in_=st.t[64:96, mt, 0:N]),
                      reads=[st.b], writes=[k.dbuf("kr", s, bi, mt)])
        wva, wba = ws.get(6)
        wvg, wbg = ws.get(7)
        for bi, (c0, N) in enumerate(BLOCKS):
            st = stg_f[it[0] % 2]
            it[0] += 1
            for mt in range(4):
                psa = fm_tile(wva, wba, mt * 128, 128, bi)
                psg = fm_tile(wvg, wbg, mt * 128, 128, bi)
                sg = sig[mt % 2]
                k.op("act", lambda e, psg=psg, sg=sg, N=N: e.activation(out=sg.t[:, 0:N], in_=psg.t[:, 0:N], func=AF.Sigmoid), reads=[psg.b], writes=[sg.b])
                k.op("dve", lambda e, psa=psa, sg=sg, st=st, mt=mt, N=N: e.tensor_tensor(out=st.t[:, mt, 0:N], in0=psa.t[:, 0:N], in1=sg.t[:, 0:N], op=ALU.mult),
                     reads=[psa.b, sg.b], writes=[st.b])
            k.dma("sp", lambda e, st=st, c0=c0, N=N: e.dma_start(out=dr["u_d"][s, :, c0:c0 + N].rearrange("(m p) n -> p m n", p=128), in_=st.t[:, :, 0:N]),
                  reads=[st.b], writes=[k.dbuf("u", s, bi)])
        for gu in range(6):
            wv, wb = ws.get(8 + gu)
            for bi, (c0, N) in enumerate(BLOCKS):
                st = stg_b[it[0] % 2]
                it[0] += 1
                for mt in range(4):
                    ps = fm_tile(wv, wb, mt * 128, 128, bi)
                    k.op("act", lambda e, ps=ps, st=st, mt=mt, N=N: e.activation(out=st.t[:, mt, 0:N], in_=ps.t[:, 0:N], func=AF.Sigmoid), reads=[ps.b], writes=[st.b])
                k.dma("sp", lambda e, st=st, c0=c0, N=N, gu=gu: e.dma_start(
                    out=dr["gT_d"][s, gu * 512:(gu + 1) * 512, c0:c0 + N].rearrange("(m p) n -> p m n", p=128), in_=st.t[:, :, 0:N]),
                    reads=[st.b], writes=[k.dbuf("gT", s, bi, gu)])
        k.barrier()


def phase_mla_prep(nc, k, g, l, s, dr):
    with ExitStack() as ph:
        ring(nc, g, ph, 2, 0)
        Wuq = _sb(nc, ph, "Wuq", [128, 2, 768], BF16)
        Wuqs = _sb(nc, ph, "Wuqs", [128, 2, 768], BF16)
        Wukv = _sb(nc, ph, "Wukv", [128, 1024], BF16)
        load_cast(k, g, dr["w_uq"][l].rearrange("(kc p) n -> p kc n", p=128), (128, 2, 768), Wuq.t[:, :, :], Wuq.b)
        load_cast(k, g, dr["w_uqs"][l].rearrange("(kc p) n -> p kc n", p=128), (128, 2, 768), Wuqs.t[:, :, :], Wuqs.b)
        load_cast(k, g, dr["w_ukv"][l], (128, 1024), Wukv.t[:, :], Wukv.b)
        Wukv3 = Wukv.t[:, :].rearrange("p (h c) -> p h c", h=8)
        cq = [_sb(nc, ph, "cq%d" % i, [128, 2, 512], F32) for i in range(2)]
        ckv = [_sb(nc, ph, "ckv%d" % i, [128, 512], F32) for i in range(2)]
        krt = [_sb(nc, ph, "krt%d" % i, [96, 2, 512], F32) for i in range(2)]
        sq = _sb(nc, ph, "sq", [128, 2, 512], BF16)
        rstd = _sb(nc, ph, "rstd", [128, 512], F32)
        t1 = _sb(nc, ph, "t1", [96, 512], F32)
        t2 = _sb(nc, ph, "t2", [96, 512], F32)
        qbs = [_sb(nc, ph, "qbs%d" % i, [96, 8, 512], BF16) for i in range(2)]
        kbs = [_sb(nc, ph, "kbs%d" % i, [96, 8, 512], BF16) for i in range(2)]
        vbs = [_sb(nc, ph, "vbs%d" % i, [128, 768], BF16) for i in range(2)]
        for v_ in vbs:
            k.op("pool", lambda e, v_=v_: e.memset(v_.t[:, :], 1.0), writes=[v_.b])
        vi = 0
        rCs = [_sb(nc, ph, "rC%d" % i, [96, 512], F32) for i in range(2)]
        rSs = [_sb(nc, ph, "rS%d" % i, [96, 512], F32) for i in range(2)]
        cqns = [_sb(nc, ph, "cqn%d" % i, [128, 2, 512], BF16) for i in range(2)]
        ckvns = [_sb(nc, ph, "ckvn%d" % i, [128, 512], BF16) for i in range(2)]
        kros = [_sb(nc, ph, "kro%d" % i, [96, 512], BF16) for i in range(2)]
        ta1 = _sb(nc, ph, "ta1", [96, 512], F32)
        ta2 = _sb(nc, ph, "ta2", [96, 512], F32)
        vi_ = [0]

        def stage_a(bi):
            c0, N = BLOCKS[bi]
            cqn, ckvn, kro = cqns[bi % 2], ckvns[bi % 2], kros[bi % 2]
            t1, t2 = ta1, ta2
            if True:
                cqt, ckvt, krtt = cq[bi % 2], ckv[bi % 2], krt[bi % 2]
                qb, kb = qbs[bi % 2], kbs[bi % 2]
                rC, rS = rCs[bi % 2], rSs[bi % 2]
                k.dma("sp", lambda e: e.dma_start(out=rC.t[64:96, 0:N], in_=dr["ropeC_in"][:, c0:c0 + N]), writes=[rC.b])
                k.dma("sp", lambda e: e.dma_start(out=rS.t[64:96, 0:N], in_=dr["ropeS_in"][:, c0:c0 + N]), writes=[rS.b])
                k.dma("sp", lambda e: e.dma_start(out=cqt.t[:, :, 0:N], in_=dr["cq_d"][s, :, c0:c0 + N].rearrange("(m p) n -> p m n", p=128)),
                      reads=[k.dbuf("cq", s, bi)], writes=[cqt.b])
                k.dma("sp", lambda e: e.dma_start(out=ckvt.t[:, 0:N], in_=dr["ckv_d"][s, :, c0:c0 + N]), reads=[k.dbuf("ckv", s, bi)], writes=[ckvt.b])
                for mt in range(2):
                    k.dma("sp", lambda e, mt=mt: e.dma_start(out=krtt.t[64:96, mt, 0:N], in_=dr["kr_d"][s, mt, :, c0:c0 + N]),
                          reads=[k.dbuf("kr", s, bi, mt)], writes=[krtt.b])
                col_stats(k, g, [cqt.t[:, 0, 0:N], cqt.t[:, 1, 0:N]], cqt.b, N, 256, RMS_EPS, False, sq, None, rstd)
                for c in range(2):
                    k.op("dve", lambda e, c=c: e.scalar_tensor_tensor(out=cqn.t[:, c, 0:N], in0=cqt.t[:, c, 0:N], scalar=g.vecs.t[:, l, V_GQ + c:V_GQ + c + 1],
                                                                      in1=rstd.t[:, 0:N], op0=ALU.mult, op1=ALU.mult), reads=[cqt.b, rstd.b, g.vecs.b], writes=[cqn.b])
                col_stats(k, g, [ckvt.t[:, 0:N]], ckvt.b, N, 128, RMS_EPS, False, sq, None, rstd)
                k.op("dve", lambda e: e.scalar_tensor_tensor(out=ckvn.t[:, 0:N], in0=ckvt.t[:, 0:N], scalar=g.vecs.t[:, l, V_GKV:V_GKV + 1],
                                                             in1=rstd.t[:, 0:N], op0=ALU.mult, op1=ALU.mult), reads=[ckvt.b, rstd.b, g.vecs.b], writes=[ckvn.b])
                k.op("dve", lambda e: e.tensor_tensor(out=t1.t[64:96, 0:N], in0=krtt.t[64:96, 0, 0:N], in1=rC.t[64:96, 0:N], op=ALU.mult),
                     reads=[krtt.b, rC.b], writes=[t1.b])
                k.op("dve", lambda e: e.tensor_tensor(out=t2.t[64:96, 0:N], in0=krtt.t[64:96, 1, 0:N], in1=rS.t[64:96, 0:N], op=ALU.mult),
                     reads=[krtt.b, rS.b], writes=[t2.b])
                k.op("dve", lambda e: e.tensor_tensor(out=kro.t[64:96, 0:N], in0=t1.t[64:96, 0:N], in1=t2.t[64:96, 0:N], op=ALU.add),
                     reads=[t1.b, t2.b], writes=[kro.b])

        def stage_b(bi):
            c0, N = BLOCKS[bi]
            cqn, ckvn, kro = cqns[bi % 2], ckvns[bi % 2], kros[bi % 2]
            qb, kb = qbs[bi % 2], kbs[bi % 2]
            rC, rS = rCs[bi % 2], rSs[bi % 2]
            vi = vi_[0]
            if True:
                need_q = not (bi == 0 and l == L - 1)
                for h in range(8):
                    ps = g.ps[h % 2]
                    mm(k, ps, ps.t[0:64, 0:N], [(Wukv3[:, h, 0:64], ckvn.t[:, 0:N])], [Wukv.b, ckvn.b])
                    k.op("act", lambda e, ps=ps, h=h: e.copy(out=kb.t[0:64, h, 0:N], in_=ps.t[0:64, 0:N]), reads=[ps.b], writes=[kb.b])
                    k.op("pool", lambda e, h=h: e.tensor_copy(out=kb.t[64:96, h, 0:N], in_=kro.t[64:96, 0:N]), reads=[kro.b], writes=[kb.b])
                    if need_q:
                        pa = g.ps[2 + (h % 2)]
                        pb = g.ps[4 + (h % 2)]
                        mm(k, pa, pa.t[0:96, 0:N], [(Wuq.t[:, kc, h * 96:(h + 1) * 96], cqn.t[:, kc, 0:N]) for kc in range(2)], [Wuq.b, cqn.b])
                        mm(k, pb, pb.t[0:96, 0:N], [(Wuqs.t[:, kc, h * 96:(h + 1) * 96], cqn.t[:, kc, 0:N]) for kc in range(2)], [Wuqs.b, cqn.b])
                        k.op("act", lambda e, pa=pa, h=h: e.copy(out=qb.t[0:64, h, 0:N], in_=pa.t[0:64, 0:N]), reads=[pa.b], writes=[qb.b])
                        k.op("dve", lambda e, pa=pa: e.tensor_tensor(out=t1.t[64:96, 0:N], in0=pa.t[64:96, 0:N], in1=rC.t[64:96, 0:N], op=ALU.mult),
                             reads=[pa.b, rC.b], writes=[t1.b])
                        k.op("dve", lambda e, pb=pb: e.tensor_tensor(out=t2.t[64:96, 0:N], in0=pb.t[64:96, 0:N], in1=rS.t[64:96, 0:N], op=ALU.mult),
                             reads=[pb.b, rS.b], writes=[t2.b])
                        k.op("dve", lambda e, h=h: e.tensor_tensor(out=qb.t[64:96, h, 0:N], in0=t1.t[64:96, 0:N], in1=t2.t[64:96, 0:N], op=ALU.add),
                             reads=[t1.b, t2.b], writes=[qb.b])
                k.dma("sp", lambda e: e.dma_start(out=dr["kbT_d"][s, :, :, c0:c0 + N].rearrange("h r n -> r h n"), in_=kb.t[:, :, 0:N]),
                      reads=[kb.b], writes=[k.dbuf("kbT", s, bi)])
                if need_q:
                    k.dma("sp", lambda e: e.dma_start(out=dr["qbT_d"][s, :, :, c0:c0 + N].rearrange("h r n -> r h n"), in_=qb.t[:, :, 0:N]),
                          reads=[qb.b], writes=[k.dbuf("qbT", s, bi)])
                for tk in range(N // 128):
                    ps = g.ps[tk % 2]
                    vt = vbs[vi % 2]
                    vi += 1
                    mm(k, ps, ps.t[:, 0:512].rearrange("p (h c) -> p h c", h=8), [(ckvn.t[:, tk * 128:(tk + 1) * 128], Wukv3[:, :, 64:128])], [Wukv.b, ckvn.b])
                    v3 = vt.t[:, :].rearrange("p (q c) -> p q c", c=192)
                    p3 = ps.t[:, 0:512].rearrange("p (q c) -> p q c", c=128)
                    k.op("act", lambda e, v3=v3, p3=p3: e.copy(out=v3[:, :, 0:64], in_=p3[:, :, 0:64]), reads=[ps.b], writes=[vt.b])
                    k.op("act", lambda e, v3=v3, p3=p3: e.copy(out=v3[:, :, 128:192], in_=p3[:, :, 64:128]), reads=[ps.b], writes=[vt.b])
                    j = (c0 + tk * 128) // 128
                    k.dma("sp", lambda e, vt=vt, j=j: e.dma_start(out=dr["vb_d"][s, j, :, :], in_=vt.t[:, :]), reads=[vt.b], writes=[k.dbuf("vb", s, j)])

            vi_[0] = vi

        stage_a(0)
        for bi in range(len(BLOCKS)):
            if bi + 1 < len(BLOCKS):
                stage_a(bi + 1)
            stage_b(bi)
        k.barrier()


def phase_na(nc, k, g, l, s, dr):
    with ExitStack() as ph:
        ring(nc, g, ph, 1, 0)
        kaT = _sb(nc, ph, "kaT", [128, 4, TT], BF16)
        qaT = _sb(nc, ph, "qaT", [128, 4, TT], BF16)
        Ve = _sb(nc, ph, "Ve", [128, 18, 768], BF16)
        Vo = _sb(nc, ph, "Vo", [128, 15, 768], BF16)
        yaT = _sb(nc, ph, "yaT", [128, 4, TT], BF16)
        pts = [_sb(nc, ph, "pt%d" % i, [128, 512], BF16) for i in range(4)]
        recs = [_sb(nc, ph, "rec%d" % i, [128, 256], F32) for i in range(2)]
        g.biasb = _sb(nc, ph, "biasb", [128, 7680], BF16)
        for hf in range(2):
            load_cast(k, g, dr["biasT_rep"][l, :, hf * 3840:(hf + 1) * 3840], (128, 3840), g.biasb.t[:, hf * 3840:(hf + 1) * 3840], g.biasb.b)
        k.dma("sp", lambda e: e.dma_start(out=kaT.t[:, :, :], in_=dr["kaT_d"][s].rearrange("(m p) n -> p m n", p=128)), reads=k.dall("kaT", s), writes=[kaT.b])
        k.dma("sp", lambda e: e.dma_start(out=qaT.t[:, :, :], in_=dr["qaT_d"][s].rearrange("(m p) n -> p m n", p=128)), reads=k.dall("qaT", s), writes=[qaT.b])
        k.dma("sp", lambda e: e.dma_start(out=Ve.t[:, :, :], in_=dr["vae_d"][s].rearrange("j p c -> p j c")), reads=k.dall("vae", s), writes=[Ve.b])
        k.dma("sp", lambda e: e.dma_start(out=Vo.t[:, :, :], in_=dr["vao_d"][s].rearrange("j p c -> p j c")), reads=k.dall("vao", s), writes=[Vo.b])
        sbanks = [g.ps[0], g.ps[1], g.ps[6], g.ps[7]]
        obs = [(g.ps[2], None), (g.ps[3], None), (g.ps[4], None), (g.ps[5], None)]
        it = 0
        jobs = []
        for r in range(32):
            rs = min(max(r - 4, 0), 24)
            for h in range(8):
                pair, hp = h // 2, h % 2
                rows = slice(hp * 64, hp * 64 + 64)
                q_ap = qaT.t[rows, pair, C + 64 * r:C + 64 * r + 64]
                chunks = []
                for c in range(4):
                    tok0 = C + 64 * (rs + 2 * c)
                    if rs % 2 == 0:
                        v = Ve.t[:, 2 + rs // 2 + c, pair * 192 + hp * 64:pair * 192 + hp * 64 + 128]
                        vb_ = Ve.b
                    else:
                        v = Vo.t[:, (rs - 1) // 2 + c, pair * 192 + hp * 64:pair * 192 + hp * 64 + 128]
                        vb_ = Vo.b
                    d0 = rs + 2 * c - r + 7
                    col0 = (h * 15 + d0) * 64
                    chunks.append(dict(kT=kaT.t[rows, pair, tok0:tok0 + 128], v=v, bufs=[kaT.b, vb_],
                                       bias=(g.biasb.t[rows, col0:col0 + 128], g.ident_b.t[rows, hp * 64:hp * 64 + 64], [g.biasb.b, g.ident_b.b])))
                for c in range(2):
                    chunks.append(dict(kT=kaT.t[rows, pair, 128 * c:128 * c + 128], v=Ve.t[:, c, pair * 192 + hp * 64:pair * 192 + hp * 64 + 128], bufs=[kaT.b, Ve.b], bias=None))
                ob, db = obs[it % 4]
                jobs.append(attention(k, g, 64, q_ap, [qaT.b], chunks, 1.0, hp, yaT.t[rows, pair, C + 64 * r:C + 64 * r + 64], yaT.b,
                                      sbanks, ob, db, pts, recs[it % 2]))
                it += 1
        if l < L - 1:
            for h in range(8):
                pair, hp = h // 2, h % 2
                rows = slice(hp * 64, hp * 64 + 64)
                chunks = [dict(kT=kaT.t[rows, pair, 128 * c:128 * c + 128], v=Ve.t[:, c, pair * 192 + hp * 64:pair * 192 + hp * 64 + 128], bufs=[kaT.b, Ve.b], bias=None) for c in range(2)]
                ob, db = obs[it % 4]
                jobs.append(attention(k, g, 256, qaT.t[rows, pair, 0:256], [qaT.b], chunks, 1.0, hp, yaT.t[rows, pair, 0:256], yaT.b,
                                      sbanks, ob, db, pts, recs[it % 2]))
                it += 1
        run_attention(jobs)
        c0 = 0 if l < L - 1 else C
        k.dma("sp", lambda e: e.dma_start(out=dr["yaT_d"][s, :, c0:TT].rearrange("(m p) n -> p m n", p=128), in_=yaT.t[:, :, c0:TT]),
              reads=[yaT.b], writes=[k.dbuf("yaT", s)])
        k.barrier()


def phase_mla(nc, k, g, l, s, dr):
    with ExitStack() as ph:
        kbT = _sb(nc, ph, "kbT", [96, 8, TT], BF16)
        vb = _sb(nc, ph, "vb", [128, 18, 768], BF16)
        qbs = [_sb(nc, ph, "qb%d" % i, [96, 8, 512], BF16) for i in range(2)]
        ybs = [_sb(nc, ph, "yb%d" % i, [128, 4, 512], BF16) for i in range(2)]
        pts = [_sb(nc, ph, "pt%d" % i, [128, 512], BF16) for i in range(4)]
        recs = [_sb(nc, ph, "rec%d" % i, [128, 512], F32) for i in range(2)]
        cv = conv_alloc(nc, ph)
        taps = conv_taps(k, g, l, s, dr, cv, C, T)
        tap_i = [0]
        calls = [0]

        def emit_taps(n):
            for _ in range(n):
                if tap_i[0] < len(taps):
                    taps[tap_i[0]]()
                    tap_i[0] += 1

        def side():
            calls[0] += 1
            if calls[0] % 4 == 0:
                emit_taps(1)

        emit_taps(4)
        k.dma("sp", lambda e: e.dma_start(out=kbT.t[:, :, :], in_=dr["kbT_d"][s].rearrange("h r n -> r h n")), reads=k.dall("kbT", s), writes=[kbT.b])
        k.dma("sp", lambda e: e.dma_start(out=vb.t[:, :, :], in_=dr["vb_d"][s].rearrange("j p c -> p j c")), reads=k.dall("vb", s), writes=[vb.b])
        sbanks = [g.ps[0], g.ps[1], g.ps[6], g.ps[7]]
        obs = [(g.ps[2], None), (g.ps[3], None), (g.ps[4], None), (g.ps[5], None)]
        scale = float(96 ** -0.5)
        it = 0
        for bi, (c0, N) in enumerate(BLOCKS):
            if bi == 0 and l == L - 1:
                continue
            qb, yb = qbs[bi % 2], ybs[bi % 2]
            k.dma("sp", lambda e: e.dma_start(out=qb.t[:, :, 0:N], in_=dr["qbT_d"][s, :, :, c0:c0 + N].rearrange("h r n -> r h n")),
                  reads=[k.dbuf("qbT", s, bi)], writes=[qb.b])
            nk = 2 if bi == 0 else 18
            jobs = []
            for h in range(8):
                pair, hp = h // 2, h % 2
                rows = slice(hp * 64, hp * 64 + 64)
                chunks = [dict(kT=kbT.t[0:96, h, 128 * j:128 * j + 128], v=vb.t[:, j, pair * 192 + hp * 64:pair * 192 + hp * 64 + 128], bufs=[kbT.b, vb.b], bias=None) for j in range(nk)]
                ob, db = obs[it % 4]
                jobs.append(attention(k, g, N, qb.t[0:96, h, 0:N], [qb.b], chunks, scale, hp, yb.t[rows, pair, 0:N], yb.b, sbanks, ob, db, pts, recs[it % 2]))
                it += 1
            run_attention(jobs, side)
            k.dma("sp", lambda e: e.dma_start(out=dr["ybT_d"][s, :, c0:c0 + N].rearrange("(m p) n -> p m n", p=128), in_=yb.t[:, :, 0:N]),
                  reads=[yb.b], writes=[k.dbuf("ybT", s, bi)])
        emit_taps(len(taps))
        conv_tail(k, g, l, s, dr, cv, C, T)
        if l < L - 1:
            for op_ in conv_taps(k, g, l, s, dr, cv, 0, C):
                op_()
            conv_tail(k, g, l, s, dr, cv, 0, C)
        k.barrier()


class ConvState:
    pass


def conv_alloc(nc, ph):
    cv = ConvState()
    cv.uT = _sb(nc, ph, "uT", [128, 4, T + 30], F32)
    cv.acc = _sb(nc, ph, "acc", [128, 4, T], F32)
    cv.sq = _sb(nc, ph, "sq", [128, 4, 512], BF16)
    cv.z16 = _sb(nc, ph, "z16", [128, 4, 512], BF16)
    cv.mean = _sb(nc, ph, "mean", [128, 512], F32)
    cv.rstd = _sb(nc, ph, "rstd", [128, 512], F32)
    cv.tmp = _sb(nc, ph, "tmp", [128, 512], F32)
    cv.ucs = [_sb(nc, ph, "ucs%d" % i, [128, 4, 512], BF16) for i in range(2)]
    cv.it = 0
    return cv


def conv_taps(k, g, l, s, dr, cv, c0, Ls):
    uT, acc = cv.uT, cv.acc
    ops = []
    ops.append(lambda: k.op("pool", lambda e: e.memset(uT.t[:, :, 0:15], 0.0), writes=[uT.b]))
    ops.append(lambda: k.op("pool", lambda e: e.memset(uT.t[:, :, 15 + Ls:30 + Ls], 0.0), writes=[uT.b]))
    ops.append(lambda: k.dma("sp", lambda e: e.dma_start(out=uT.t[:, :, 15:15 + Ls], in_=dr["u_d"][s, :, c0:c0 + Ls].rearrange("(m p) n -> p m n", p=128)),
                             reads=k.dall("u", s), writes=[uT.b]))
    for ch in range(4):
        wcol = V_WDW + ch * 31
        ops.append(lambda ch=ch, wcol=wcol: k.op("dve", lambda e: e.tensor_scalar(
            out=acc.t[:, ch, 0:Ls], in0=uT.t[:, ch, 0:Ls], scalar1=g.vecs.t[:, l, wcol:wcol + 1],
            scalar2=g.vecs.t[:, l, V_BDW + ch:V_BDW + ch + 1], op0=ALU.mult, op1=ALU.add), reads=[uT.b, g.vecs.b], writes=[acc.b]))
        for kk in range(1, 31):
            ops.append(lambda ch=ch, wcol=wcol, kk=kk: k.op("dve", lambda e: e.scalar_tensor_tensor(
                out=acc.t[:, ch, 0:Ls], in0=uT.t[:, ch, kk:kk + Ls], scalar=g.vecs.t[:, l, wcol + kk:wcol + kk + 1], in1=acc.t[:, ch, 0:Ls],
                op0=ALU.mult, op1=ALU.add), reads=[uT.b, g.vecs.b, acc.b], writes=[acc.b]))
    return ops


def conv_tail(k, g, l, s, dr, cv, c0, Ls):
    acc, sq, z16, mean, rstd, tmp = cv.acc, cv.sq, cv.z16, cv.mean, cv.rstd, cv.tmp
    for b0 in range(0, Ls, 512):
        N = min(512, Ls - b0)
        col_stats(k, g, [acc.t[:, ch, b0:b0 + N] for ch in range(4)], acc.b, N, 512, LN_EPS, True, sq, mean, rstd, z16)
        uc = cv.ucs[cv.it % 2]
        cv.it += 1
        for ch in range(4):
            k.op("dve", lambda e, ch=ch: e.tensor_tensor(out=tmp.t[:, 0:N], in0=acc.t[:, ch, b0:b0 + N], in1=mean.t[:, 0:N], op=ALU.subtract),
                 reads=[acc.b, mean.b], writes=[tmp.b])
            k.op("dve", lambda e, ch=ch: e.scalar_tensor_tensor(out=tmp.t[:, 0:N], in0=tmp.t[:, 0:N], scalar=g.vecs.t[:, l, V_GCN + ch:V_GCN + ch + 1],
                                                                 in1=rstd.t[:, 0:N], op0=ALU.mult, op1=ALU.mult), reads=[tmp.b, rstd.b, g.vecs.b], writes=[tmp.b])
            k.op("act", lambda e, ch=ch, uc=uc: e.activation(out=uc.t[:, ch, 0:N], in_=tmp.t[:, 0:N], func=AF.Silu,
                                                             bias=g.vecs.t[:, l, V_BCN + ch:V_BCN + ch + 1], scale=1.0), reads=[tmp.b, g.vecs.b], writes=[uc.b])
        k.dma("sp", lambda e, uc=uc, b0=b0, N=N: e.dma_start(out=dr["ucT_d"][s, :, c0 + b0:c0 + b0 + N].rearrange("(m p) n -> p m n", p=128), in_=uc.t[:, :, 0:N]),
              reads=[uc.b], writes=[k.dbuf("ucT", s, c0 + b0)])


MBLK = [(c0, 256) for c0 in range(0, TT, 256)]


def phase_merge(nc, k, g, l, s, dr):
    X = dr["xT_in"] if l == 0 else dr["xres"]
    NM = 512
    with ExitStack() as ph:
        ring(nc, g, ph, 2, 0)
        Woa = _sb(nc, ph, "Woa", [128, 4, 1024], BF16)
        Wob = _sb(nc, ph, "Wob", [128, 4, 1024], BF16)
        Wpw = _sb(nc, ph, "Wpw", [128, 4, 1024], BF16)
        Wout = _sb(nc, ph, "Wout", [128, 8, 1024], BF16)
        Wrf = _sb(nc, ph, "Wrf", [128, 8, 16], F32)
        Wrh = _sb(nc, ph, "Wrh", [128, 8, 16], BF16)
        Wrl = _sb(nc, ph, "Wrl", [128, 8, 16], BF16)
        for W, src in ((Woa, dr["w_oa"]), (Wob, dr["w_ob"]), (Wpw, dr["w_pw2"])):
            load_cast(k, g, src[l].rearrange("(kc p) n -> p kc n", p=128), (128, 4, 1024), W.t[:, :, :], W.b)
        for hf in range(2):
            load_cast(k, g, dr["w_out"][l, :, hf * 512:(hf + 1) * 512].rearrange("(kc p) n -> p kc n", p=128), (128, 8, 512),
                      Wout.t[:, :, hf * 512:(hf + 1) * 512], Wout.b)
        k.dma("sp", lambda e: e.dma_start(out=Wrf.t[:, :, :], in_=dr["w_router"][l].rearrange("(kc p) n -> p kc n", p=128)), writes=[Wrf.b])
        k.op("dve", lambda e: e.tensor_copy(out=Wrh.t[:, :, :], in_=Wrf.t[:, :, :]), reads=[Wrf.b], writes=[Wrh.b])
        k.op("dve", lambda e: e.tensor_tensor(out=Wrl.t[:, :, :], in0=Wrf.t[:, :, :], in1=Wrh.t[:, :, :], op=ALU.subtract), reads=[Wrf.b, Wrh.b], writes=[Wrl.b])
        yas = [_sb(nc, ph, "ya%d" % i, [128, 4, NM], BF16) for i in range(2)]
        ybs_ = [_sb(nc, ph, "yb%d" % i, [128, 4, NM], BF16) for i in range(2)]
        ucs_ = [_sb(nc, ph, "uc%d" % i, [128, 4, NM], BF16) for i in range(2)]
        gts = [_sb(nc, ph, "gt%d" % i, [128, 3, NM], BF16) for i in range(3)]
        ms_ = [_sb(nc, ph, "m%d" % i, [128, 8, NM], BF16) for i in range(2)]
        mfs = [_sb(nc, ph, "mf%d" % i, [128, NM], F32) for i in range(2)]
        t2s = [_sb(nc, ph, "t2_%d" % i, [128, NM], F32) for i in range(2)]
        tts = [_sb(nc, ph, "tt%d" % i, [128, NM], F32) for i in range(2)]
        z = _sb(nc, ph, "z", [128, 8, NM], F32)
        sq = _sb(nc, ph, "sq", [128, 8, NM], BF16)
        z16 = _sb(nc, ph, "z16", [128, 8, NM], BF16)
        mean = _sb(nc, ph, "mean", [128, NM], F32)
        rstd = _sb(nc, ph, "rstd", [128, NM], F32)
        h2fs = [_sb(nc, ph, "h2f%d" % i, [128, NM], F32) for i in range(2)]
        h2b = _sb(nc, ph, "h2b", [128, 8, NM], BF16)
        h2l = _sb(nc, ph, "h2l", [128, 8, NM], BF16)
        h2s = [_sb(nc, ph, "h2s%d" % i, [128, 1024], BF16) for i in range(2)]
        Et = _sb(nc, ph, "Et", [16, NM], F32)
        afs = [_sb(nc, ph, "afs%d" % i, [128, 16], F32) for i in range(4)]
        ssum = _sb(nc, ph, "ssum", [128, 4], F32)
        affs = _sb(nc, ph, "affs", [16, NM], F32)
        cnt = [0]
        gti = [0]
        tti = [0]
        hi_ = [0]

        def nextps():
            p = g.ps[cnt[0] % 4]
            cnt[0] += 1
            return p

        blocks = [(bi, c0, N) for bi, (c0, N) in enumerate(BLOCKS) if not (bi == 0 and l == L - 1)]

        def stage1(ix):
            bi, c0, N = blocks[ix]
            ya, yb, uc, m = yas[ix % 2], ybs_[ix % 2], ucs_[ix % 2], ms_[ix % 2]
            for T_, nm in ((ya, "yaT"), (yb, "ybT"), (uc, "ucT")):
                k.dma("sp", lambda e, T_=T_, nm=nm: e.dma_start(out=T_.t[:, :, 0:N], in_=dr[nm + "_d"][s, :, c0:c0 + N].rearrange("(m p) n -> p m n", p=128)),
                      reads=k.dall(nm, s), writes=[T_.b])
            for oc in range(8):
                osl = slice(oc * 128, (oc + 1) * 128)
                gt = gts[gti[0] % 3]
                gti[0] += 1
                k.dma("sp", lambda e, gt=gt, oc=oc: e.dma_start(
                    out=gt.t[:, :, 0:N], in_=dr["gT_d"][s, :, c0:c0 + N].rearrange("(b m p) n -> b m p n", b=3, p=128)[:, oc].rearrange("b p n -> p b n")),
                    reads=k.dall("gT", s), writes=[gt.b])
                pa = nextps()
                mm(k, pa, pa.t[:, 0:N], [(Woa.t[:, kc, osl], ya.t[:, kc, 0:N]) for kc in range(4)], [Woa.b, ya.b])
                pb = nextps()
                mm(k, pb, pb.t[:, 0:N], [(Wob.t[:, kc, osl], yb.t[:, kc, 0:N]) for kc in range(4)], [Wob.b, yb.b])
                pc = nextps()
                mm(k, pc, pc.t[:, 0:N], [(Wpw.t[:, kc, osl], uc.t[:, kc, 0:N]) for kc in range(4)], [Wpw.b, uc.b])
                mf, t2 = mfs[oc % 2], t2s[oc % 2]
                t1 = tts[tti[0] % 2]
                tti[0] += 1
                k.op("dve", lambda e, pa=pa, gt=gt, mf=mf: e.tensor_tensor(out=mf.t[:, 0:N], in0=pa.t[:, 0:N], in1=gt.t[:, 0, 0:N], op=ALU.mult), reads=[pa.b, gt.b], writes=[mf.b])
                k.op("dve", lambda e, pb=pb, gt=gt, t1=t1: e.tensor_tensor(out=t1.t[:, 0:N], in0=pb.t[:, 0:N], in1=gt.t[:, 1, 0:N], op=ALU.mult), reads=[pb.b, gt.b], writes=[t1.b])
                k.op("dve", lambda e, mf=mf, t1=t1: e.tensor_tensor(out=mf.t[:, 0:N], in0=mf.t[:, 0:N], in1=t1.t[:, 0:N], op=ALU.add), reads=[mf.b, t1.b], writes=[mf.b])
                k.op("dve", lambda e, pc=pc, gt=gt, t2=t2: e.tensor_tensor(out=t2.t[:, 0:N], in0=pc.t[:, 0:N], in1=gt.t[:, 2, 0:N], op=ALU.mult), reads=[pc.b, gt.b], writes=[t2.b])
                k.op("pool", lambda e, oc=oc, m=m, mf=mf, t2=t2: e.tensor_tensor(out=m.t[:, oc, 0:N], in0=mf.t[:, 0:N], in1=t2.t[:, 0:N], op=ALU.add), reads=[mf.b, t2.b], writes=[m.b])

        def stage2(ix):
            bi, c0, N = blocks[ix]
            m = ms_[ix % 2]
            j = 2 if bi == 0 else s
            k.dma("sp", lambda e: e.dma_start(out=z.t[:, :, 0:N], in_=X[s, :, c0:c0 + N].rearrange("(kc p) n -> p kc n", p=128)),
                  reads=k.dall("x", s), writes=[z.b])
            for oc in range(8):
                osl = slice(oc * 128, (oc + 1) * 128)
                py = nextps()
                mm(k, py, py.t[:, 0:N], [(Wout.t[:, kc, osl], m.t[:, kc, 0:N]) for kc in range(8)], [Wout.b, m.b])
                t1 = tts[tti[0] % 2]
                tti[0] += 1
                k.op("act", lambda e, py=py, t1=t1, oc=oc: e.activation(out=t1.t[:, 0:N], in_=py.t[:, 0:N], func=AF.Copy, scale=g.mod.t[:, l, 16 + oc, j:j + 1]),
                     reads=[py.b, g.mod.b], writes=[t1.b])
                k.op("dve", lambda e, t1=t1, oc=oc: e.scalar_tensor_tensor(out=z.t[:, oc, 0:N], in0=z.t[:, oc, 0:N], scalar=ALPHA, in1=t1.t[:, 0:N], op0=ALU.mult, op1=ALU.add),
                     reads=[z.b, t1.b], writes=[z.b])

        def stage3(ix):
            bi, c0, N = blocks[ix]
            j = 2 if bi == 0 else s
            col_stats(k, g, [z.t[:, oc, 0:N] for oc in range(8)], z.b, N, 1024, LN_EPS, True, sq, mean, rstd, z16)
            for oc in range(8):
                t1 = tts[tti[0] % 2]
                tti[0] += 1
                k.op("dve", lambda e, t1=t1, oc=oc: e.tensor_tensor(out=t1.t[:, 0:N], in0=z.t[:, oc, 0:N], in1=mean.t[:, 0:N], op=ALU.subtract), reads=[z.b, mean.b], writes=[t1.b])
                k.op("dve", lambda e, t1=t1, oc=oc: e.scalar_tensor_tensor(out=t1.t[:, 0:N], in0=t1.t[:, 0:N], scalar=g.vecs.t[:, l, V_LN1G + oc:V_LN1G + oc + 1], in1=rstd.t[:, 0:N],
                                                                          op0=ALU.mult, op1=ALU.mult), reads=[t1.b, rstd.b, g.vecs.b], writes=[t1.b])
                k.op("act", lambda e, t1=t1, oc=oc: e.activation(out=z.t[:, oc, 0:N], in_=t1.t[:, 0:N], func=AF.Identity, bias=g.vecs.t[:, l, V_LN1B + oc:V_LN1B + oc + 1], scale=1.0),
                     reads=[t1.b, g.vecs.b], writes=[z.b])
                h2f = h2fs[oc % 2]
                k.op("act", lambda e, oc=oc, h2f=h2f: e.activation(out=h2f.t[:, 0:N], in_=z.t[:, oc, 0:N], func=AF.Identity, scale=g.mod.t[:, l, 32 + oc, j:j + 1],
                                                                   bias=g.mod.t[:, l, 24 + oc, j:j + 1]), reads=[z.b, g.mod.b], writes=[h2f.b])
                k.op("act", lambda e, oc=oc, h2f=h2f: e.copy(out=h2b.t[:, oc, 0:N], in_=h2f.t[:, 0:N]), reads=[h2f.b], writes=[h2b.b])
                k.op("pool", lambda e, oc=oc, h2f=h2f: e.tensor_tensor(out=h2l.t[:, oc, 0:N], in0=h2f.t[:, 0:N], in1=h2b.t[:, oc, 0:N], op=ALU.subtract),
                     reads=[h2f.b, h2b.b], writes=[h2l.b])
            k.dma("sp", lambda e: e.dma_start(out=dr["x1res"][s, :, c0:c0 + N].rearrange("(kc p) n -> p kc n", p=128), in_=z.t[:, :, 0:N]),
                  reads=[z.b], writes=[k.dbuf("x1", s, bi)])

        def stage4(ix):
            bi, c0, N = blocks[ix]
            ntk = N // 128
            pr = g.ps[4]
            pairs = []
            for kc in range(8):
                pairs += [(Wrh.t[:, kc, :], h2b.t[:, kc, 0:N]), (Wrl.t[:, kc, :], h2b.t[:, kc, 0:N]), (Wrh.t[:, kc, :], h2l.t[:, kc, 0:N])]
            mm(k, pr, pr.t[0:16, 0:N], pairs, [Wrh.b, Wrl.b, h2b.b, h2l.b])
            k.op("act", lambda e: e.activation(out=Et.t[:, 0:N], in_=pr.t[0:16, 0:N], func=AF.Exp), reads=[pr.b], writes=[Et.b])
            pq = g.ps[5]
            for tk in range(ntk):
                af = afs[tk]
                k.op("pe", lambda e, tk=tk: e.transpose(pq.t[:, tk * 16:tk * 16 + 16], Et.t[0:16, tk * 128:(tk + 1) * 128], g.ident_f.t[0:16, 0:16]),
                     reads=[Et.b, g.ident_f.b], writes=[pq.b])
                k.op("dve", lambda e, tk=tk: e.tensor_reduce(out=ssum.t[:, tk:tk + 1], in_=pq.t[:, tk * 16:tk * 16 + 16], op=ALU.add, axis=mybir.AxisListType.X),
                     reads=[pq.b], writes=[ssum.b])
                k.op("dve", lambda e, tk=tk: e.reciprocal(out=ssum.t[:, tk:tk + 1], in_=ssum.t[:, tk:tk + 1]), reads=[ssum.b], writes=[ssum.b])
                k.op("dve", lambda e, tk=tk, af=af: e.tensor_scalar(out=af.t[:, :], in0=pq.t[:, tk * 16:tk * 16 + 16], scalar1=ssum.t[:, tk:tk + 1], scalar2=None, op0=ALU.mult),
                     reads=[pq.b, ssum.b], writes=[af.b])
                r0 = s * TT + c0 + tk * 128
                k.dma("sp", lambda e, af=af, r0=r0: e.dma_start(out=dr["afftok_d"][r0:r0 + 128, :], in_=af.t[:, :]), reads=[af.b], writes=[k.dbuf("afftok", s, bi, tk)])
                k.op("pe", lambda e, tk=tk, af=af: e.transpose(pr.t[0:16, tk * 128:(tk + 1) * 128], af.t[:, :], g.ident_f.t[:, :]),
                     reads=[af.b, g.ident_f.b], writes=[pr.b])
            k.op("dve", lambda e: e.tensor_copy(out=affs.t[:, 0:N], in_=pr.t[0:16, 0:N]), reads=[pr.b], writes=[affs.b])
            k.dma("sp", lambda e: e.dma_start(out=dr["affT_d"][s, :, c0:c0 + N], in_=affs.t[:, 0:N]), reads=[affs.b], writes=[k.dbuf("affT", s, bi)])
            for tk in range(ntk):
                pt_ = g.ps[4 + (tk % 2)]
                ptb = pt_.t[:, :].bitcast(BF16)
                hs = h2s[hi_[0] % 2]
                hi_[0] += 1
                for kc in range(8):
                    k.op("pe", lambda e, kc=kc, tk=tk, ptb=ptb: e.transpose(ptb[:, kc * 128:(kc + 1) * 128], h2b.t[:, kc, tk * 128:(tk + 1) * 128], g.ident_b.t[:, :]),
                         reads=[h2b.b, g.ident_b.b], writes=[pt_.b])
                k.op("act", lambda e, ptb=ptb, hs=hs: e.copy(out=hs.t[:, :], in_=ptb[:, 0:1024]), reads=[pt_.b], writes=[hs.b])
                r0 = s * TT + c0 + tk * 128
                k.dma("sp", lambda e, hs=hs, r0=r0: e.dma_start(out=dr["h2tok_d"][r0:r0 + 128, :], in_=hs.t[:, :]), reads=[hs.b], writes=[k.dbuf("h2tok", s, bi, tk)])

        stage1(0)
        for ix in range(len(blocks)):
            stage2(ix)
            if ix + 1 < len(blocks):
                stage1(ix + 1)
            stage3(ix)
            stage4(ix)
        k.barrier()


def phase_moe(nc, k, g, l, dr):
    with_ctx = l < L - 1
    nch = 5 if with_ctx else 4
    moe_d, h2tok_d, afftok_d = dr["moe_d"], dr["h2tok_d"], dr["afftok_d"]
    with ExitStack() as ph:
        ring(nc, g, ph, 3, 4, cast_engs=("dve", "act", "dve"))
        gl = _sb(nc, ph, "gidx_l", [128, 2, 32], I32)
        gc = _sb(nc, ph, "gidx_c", [64, 16], I32)
        with ExitStack() as ph2:
            zt = _sb(nc, ph2, "zt", [128, 2048], F32)
            k.op("pool", lambda e: e.memset(zt.t[:, :], 0.0), writes=[zt.b])
            moe_v = moe_d.rearrange("(i p a) d -> i p (a d)", p=128, a=2)
            for i in range(18):
                k.dma("sp", lambda e, i=i: e.dma_start(out=moe_v[i], in_=zt.t[:, :]), reads=[zt.b], writes=[k.dbuf("moez", i)])
            affT = _sb(nc, ph2, "affT", [32, TT], F32)
            for s in range(NS):
                k.dma("sp", lambda e, s=s: e.dma_start(out=affT.t[16 * s:16 * s + 16, :], in_=dr["affT_d"][s, :, :]), reads=k.dall("affT", s), writes=[affT.b])
            wk = _sb(nc, ph2, "wk", [32, T], F32)
            mx = _sb(nc, ph2, "mx", [32, 8], F32)
            idxu = _sb(nc, ph2, "idxu", [32, 256], U32)
            idxf = _sb(nc, ph2, "idxf", [32, 256], F32)
            offl = _sb(nc, ph2, "offl", [32, 1], F32)
            offc = _sb(nc, ph2, "offc", [32, 1], F32)
            k.op("dve", lambda e: e.memset(offl.t[:, :], float(TT + C)), writes=[offl.b])
            k.op("dve", lambda e: e.memset(offl.t[0:16, :], float(C)), writes=[offl.b])
            k.op("dve", lambda e: e.memset(offc.t[:, :], float(TT)), writes=[offc.b])
            k.op("dve", lambda e: e.memset(offc.t[0:16, :], 0.0), writes=[offc.b])
            pst = g.ps[7]

            def topk(n_tok, n_it):
                for it in range(n_it):
                    k.op("dve", lambda e: e.max(out=mx.t[:, :], in_=wk.t[:, 0:n_tok]), reads=[wk.b], writes=[mx.b])
                    k.op("dve", lambda e, it=it: e.max_index(out=idxu.t[:, it * 8:(it + 1) * 8], in_max=mx.t[:, :], in_values=wk.t[:, 0:n_tok]),
                         reads=[wk.b, mx.b], writes=[idxu.b])
                    k.op("dve", lambda e: e.match_replace(out=wk.t[:, 0:n_tok], in_to_replace=mx.t[:, :], in_values=wk.t[:, 0:n_tok], imm_value=-1.0),
                         reads=[wk.b, mx.b], writes=[wk.b])

            k.op("dve", lambda e: e.tensor_copy(out=wk.t[:, 0:T], in_=affT.t[:, C:TT]), reads=[affT.b], writes=[wk.b])
            topk(T, 32)
            k.op("dve", lambda e: e.tensor_scalar(out=idxf.t[:, :], in0=idxu.t[:, :], scalar1=offl.t[:, 0:1], scalar2=None, op0=ALU.add),
                 reads=[idxu.b, offl.b], writes=[idxf.b])
            for ch in range(2):
                k.op("pe", lambda e, ch=ch: e.transpose(pst.t[:, ch * 32:(ch + 1) * 32], idxf.t[0:32, ch * 128:(ch + 1) * 128], g.ident_f.t[0:32, 0:32]),
                     reads=[idxf.b, g.ident_f.b], writes=[pst.b])
                k.op("dve", lambda e, ch=ch: e.tensor_copy(out=gl.t[:, ch, :], in_=pst.t[:, ch * 32:(ch + 1) * 32]), reads=[pst.b], writes=[gl.b])
            if with_ctx:
                k.op("dve", lambda e: e.tensor_copy(out=wk.t[:, 0:C], in_=affT.t[:, 0:C]), reads=[affT.b], writes=[wk.b])
                topk(C, 4)
                k.op("dve", lambda e: e.tensor_scalar(out=idxf.t[:, 0:32], in0=idxu.t[:, 0:32], scalar1=offc.t[:, 0:1], scalar2=None, op0=ALU.add),
                     reads=[idxu.b, offc.b], writes=[idxf.b])
                k.op("pe", lambda e: e.transpose(pst.t[0:32, 64:96], idxf.t[0:32, 0:32], g.ident_f.t[0:32, 0:32]), reads=[idxf.b, g.ident_f.b], writes=[pst.b])
                k.op("dve", lambda e: e.tensor_copy(out=gc.t[0:32, :], in_=pst.t[0:32, 64:80]), reads=[pst.b], writes=[gc.b])
                k.op("dve", lambda e: e.tensor_copy(out=gc.t[32:64, :], in_=pst.t[0:32, 80:96]), reads=[pst.b], writes=[gc.b])
            k.barrier()
        units = []
        for ex in range(NE):
            for W, c0 in ((dr["w_gate"], 0), (dr["w_up"], 0), (dr["w_gate"], 512), (dr["w_up"], 512), (dr["w_down"], 0), (dr["w_down"], 512)):
                units.append((W[l, ex, :, c0:c0 + 512].rearrange("(kc p) n -> p kc n", p=128), (128, 8, 512)))
        ws = WStream(k, g, units, pf=2)
        xg = [[_sb(nc, ph, "xg%d_%d" % (b, i), [128, 1024], BF16) for i in range(nch)] for b in range(2)]
        ag = [[_sb(nc, ph, "ag%d_%d" % (b, i), [128, 16], F32) for i in range(nch)] for b in range(2)]
        xgT = _sb(nc, ph, "xgT", [128, 8, 576], BF16)
        hid = _sb(nc, ph, "hid", [128, 8, 576], BF16)
        sgs = [_sb(nc, ph, "sg%d" % i, [128, 576], F32) for i in range(2)]
        ysb = [_sb(nc, ph, "ysb%d" % i, [128, 1024], F32) for i in range(nch)]
        h2reads = k.dall("h2tok")
        afreads = k.dall("afftok")
        zreads = k.dall("moez")
        ti = [0]

        def idx_ap(ch, ex):
            if ch < 4:
                s_, c_ = ch // 2, ch % 2
                return gl.t[:, c_, s_ * 16 + ex:s_ * 16 + ex + 1], gl.b, 128
            return gc.t[0:64, ex:ex + 1], gc.b, 64

        def gathers(ex):
            b = ex % 2
            for ch in range(nch):
                ia, ib, P = idx_ap(ch, ex)
                k.dma("pool", lambda e, ch=ch, P=P, ia=ia: e.indirect_dma_start(
                    out=xg[b][ch].t[0:P, :], out_offset=None, in_=h2tok_d[:, :], in_offset=bass.IndirectOffsetOnAxis(ap=ia, axis=0)),
                    reads=h2reads + [ib], writes=[xg[b][ch].b])
                k.dma("pool", lambda e, ch=ch, P=P, ia=ia: e.indirect_dma_start(
                    out=ag[b][ch].t[0:P, :], out_offset=None, in_=afftok_d[:, :], in_offset=bass.IndirectOffsetOnAxis(ap=ia, axis=0)),
                    reads=afreads + [ib], writes=[ag[b][ch].b])

        def transposes(ex):
            b = ex % 2
            for ch in range(nch):
                P = 128 if ch < 4 else 64
                pt_ = g.ps[6 + (ti[0] % 2)]
                ti[0] += 1
                ptb = pt_.t[:, :].bitcast(BF16)
                for kc in range(8):
                    k.op("pe", lambda e, kc=kc, ch=ch, P=P, ptb=ptb: e.transpose(ptb[:, kc * 128:kc * 128 + P], xg[b][ch].t[0:P, kc * 128:(kc + 1) * 128], g.ident_b.t[0:P, 0:P]),
                         reads=[xg[b][ch].b, g.ident_b.b], writes=[pt_.b])
                k.op("act", lambda e, ch=ch, P=P, ptb=ptb: e.copy(out=xgT.t[:, :, ch * 128:ch * 128 + P], in_=ptb[:, 0:1024].rearrange("p (a b) -> p a b", a=8)[:, :, 0:P]),
                     reads=[pt_.b], writes=[xgT.b])

        gathers(0)
        transposes(0)
        for ex in range(NE):
            b = ex % 2
            if ex + 1 < NE:
                gathers(ex + 1)
            for half in range(2):
                wg, wgb = ws.get(ex * 6 + 2 * half)
                wu, wub = ws.get(ex * 6 + 2 * half + 1)
                for jj in range(4):
                    j = half * 4 + jj
                    b0 = 3 * (j % 2)
                    pg, pu, pc = g.ps[b0], g.ps[b0 + 1], g.ps[b0 + 2]
                    csl = slice(jj * 128, (jj + 1) * 128)
                    mm(k, pg, pg.t[:, 0:512], [(wg[:, kc, csl], xgT.t[:, kc, 0:512]) for kc in range(8)], [wgb, xgT.b])
                    mm(k, pu, pu.t[:, 0:512], [(wu[:, kc, csl], xgT.t[:, kc, 0:512]) for kc in range(8)], [wub, xgT.b])
                    sg = sgs[j % 2]
                    k.op("act", lambda e, pg=pg, sg=sg: e.activation(out=sg.t[:, 0:512], in_=pg.t[:, 0:512], func=AF.Silu), reads=[pg.b], writes=[sg.b])
                    if with_ctx:
                        mm(k, pc, pc.t[:, 0:64], [(wg[:, kc, csl], xgT.t[:, kc, 512:576]) for kc in range(8)], [wgb, xgT.b])
                        mm(k, pc, pc.t[:, 64:128], [(wu[:, kc, csl], xgT.t[:, kc, 512:576]) for kc in range(8)], [wub, xgT.b])
                        k.op("act", lambda e, pc=pc, sg=sg: e.activation(out=sg.t[:, 512:576], in_=pc.t[:, 0:64], func=AF.Silu), reads=[pc.b], writes=[sg.b])
                    k.op("dve", lambda e, pu=pu, sg=sg, j=j: e.tensor_tensor(out=hid.t[:, j, 0:512], in0=pu.t[:, 0:512], in1=sg.t[:, 0:512], op=ALU.mult),
                         reads=[pu.b, sg.b], writes=[hid.b])
                    if with_ctx:
                        k.op("dve", lambda e, pc=pc, sg=sg, j=j: e.tensor_tensor(out=hid.t[:, j, 512:576], in0=pc.t[:, 64:128], in1=sg.t[:, 512:576], op=ALU.mult),
                             reads=[pc.b, sg.b], writes=[hid.b])
            if ex + 1 < NE:
                transposes(ex + 1)
            wdl, wdlb = ws.get(ex * 6 + 4)
            wdh, wdhb = ws.get(ex * 6 + 5)
            prev = k.dall("moeacc", ex - 1) if ex > 0 else []
            for ch in range(nch):
                ia, ib, P = idx_ap(ch, ex)
                for oh, (wd, wdb) in enumerate(((wdl, wdlb), (wdh, wdhb))):
                    py = g.ps[6 + (ti[0] % 2)]
                    ti[0] += 1
                    mm(k, py, py.t[0:P, 0:512], [(hid.t[:, kc, ch * 128:ch * 128 + P], wd[:, kc, 0:512]) for kc in range(8)], [hid.b, wdb])
                    k.op("act", lambda e, py=py, ch=ch, P=P, oh=oh, ex=ex, b=b: e.activation(out=ysb[ch].t[0:P, oh * 512:(oh + 1) * 512], in_=py.t[0:P, 0:512], func=AF.Copy,
                                                                                           scale=ag[b][ch].t[0:P, ex:ex + 1]), reads=[py.b, ag[b][ch].b], writes=[ysb[ch].b])
                k.dma("pool", lambda e, ch=ch, P=P, ia=ia: e.indirect_dma_start(
                    out=moe_d[:, :], out_offset=bass.IndirectOffsetOnAxis(ap=ia, axis=0), in_=ysb[ch].t[0:P, :], in_offset=None, compute_op=ALU.add),
                    reads=[ysb[ch].b, ib] + zreads + prev, writes=[k.dbuf("moeacc", ex, ch)])
        k.barrier()


def phase_ln2(nc, k, g, l, s, dr):
    last = (l == L - 1)
    with ExitStack() as ph:
        zs = [_sb(nc, ph, "z%d" % i, [128, 8, 512], F32) for i in range(2)]
        sq = _sb(nc, ph, "sq", [128, 8, 512], BF16)
        z16 = _sb(nc, ph, "z16", [128, 8, 512], BF16)
        mean = _sb(nc, ph, "mean", [128, 512], F32)
        rstd = _sb(nc, ph, "rstd", [128, 512], F32)
        tts = [_sb(nc, ph, "tt%d" % i, [128, 512], F32) for i in range(2)]
        mrows = [[_sb(nc, ph, "mrow%d_%d" % (b, i), [128, 1024], F32) for i in range(4)] for b in range(2)]
        tti = [0]
        blocks = [(bi, c0, N) for bi, (c0, N) in enumerate(BLOCKS) if not (bi == 0 and last)]

        def stage_t(ix):
            bi, c0, N = blocks[ix]
            z, mrow = zs[ix % 2], mrows[ix % 2]
            j = 2 if bi == 0 else s
            k.dma("sp", lambda e: e.dma_start(out=z.t[:, :, 0:N], in_=dr["x1res"][s, :, c0:c0 + N].rearrange("(kc p) n -> p kc n", p=128)),
                  reads=k.dall("x1", s), writes=[z.b])
            ntk = N // 128
            for tk in range(ntk):
                r0 = s * TT + c0 + tk * 128
                k.dma("sp", lambda e, tk=tk, r0=r0: e.dma_start(out=mrow[tk].t[:, :], in_=dr["moe_d"][r0:r0 + 128, :]),
                      reads=k.dall("moez") + k.dall("moeacc"), writes=[mrow[tk].b])
            for oc in range(8):
                pm = g.ps[oc % 4]
                for tk in range(ntk):
                    k.op("pe", lambda e, tk=tk, oc=oc, pm=pm: e.transpose(pm.t[:, tk * 128:(tk + 1) * 128], mrow[tk].t[:, oc * 128:(oc + 1) * 128], g.ident_f.t[:, :]),
                         reads=[mrow[tk].b, g.ident_f.b], writes=[pm.b])
                t1 = tts[tti[0] % 2]
                tti[0] += 1
                k.op("act", lambda e, pm=pm, t1=t1, oc=oc: e.activation(out=t1.t[:, 0:N], in_=pm.t[:, 0:N], func=AF.Copy, scale=g.mod.t[:, l, 40 + oc, j:j + 1]),
                     reads=[pm.b, g.mod.b], writes=[t1.b])
                k.op("dve", lambda e, t1=t1, oc=oc: e.scalar_tensor_tensor(out=z.t[:, oc, 0:N], in0=z.t[:, oc, 0:N], scalar=ALPHA, in1=t1.t[:, 0:N], op0=ALU.mult, op1=ALU.add),
                     reads=[z.b, t1.b], writes=[z.b])

        def stage_n(ix):
            bi, c0, N = blocks[ix]
            z = zs[ix % 2]
            col_stats(k, g, [z.t[:, oc, 0:N] for oc in range(8)], z.b, N, 1024, LN_EPS, True, sq, mean, rstd, z16)
            for oc in range(8):
                t1 = tts[tti[0] % 2]
                tti[0] += 1
                k.op("dve", lambda e, t1=t1, oc=oc: e.tensor_tensor(out=t1.t[:, 0:N], in0=z.t[:, oc, 0:N], in1=mean.t[:, 0:N], op=ALU.subtract), reads=[z.b, mean.b], writes=[t1.b])
                k.op("dve", lambda e, t1=t1, oc=oc: e.scalar_tensor_tensor(out=t1.t[:, 0:N], in0=t1.t[:, 0:N], scalar=g.vecs.t[:, l, V_LN2G + oc:V_LN2G + oc + 1], in1=rstd.t[:, 0:N],
                                                                          op0=ALU.mult, op1=ALU.mult), reads=[t1.b, rstd.b, g.vecs.b], writes=[t1.b])
                k.op("act", lambda e, t1=t1, oc=oc: e.activation(out=z.t[:, oc, 0:N], in_=t1.t[:, 0:N], func=AF.Identity, bias=g.vecs.t[:, l, V_LN2B + oc:V_LN2B + oc + 1], scale=1.0),
                     reads=[t1.b, g.vecs.b], writes=[z.b])
            if last:
                k.dma("sp", lambda e: e.dma_start(out=dr["outT"][s, :, c0 - C:c0 - C + N].rearrange("(kc p) n -> p kc n", p=128), in_=z.t[:, :, 0:N]),
                      reads=[z.b], writes=[k.dbuf("out", s, bi)])
            else:
                k.dma("sp", lambda e: e.dma_start(out=dr["xres"][s, :, c0:c0 + N].rearrange("(kc p) n -> p kc n", p=128), in_=z.t[:, :, 0:N]),
                      reads=[z.b], writes=[k.dbuf("x", s, bi)])

        stage_t(0)
        for ix in range(len(blocks)):
            if ix + 1 < len(blocks):
                stage_t(ix + 1)
            stage_n(ix)
        k.barrier()


def layer(nc, k, g, l, dr):
    stop = dr.get("stop_after")
    for s in range(NS):
        _plog(k, "L%d s%d proj" % (l, s))
        phase_proj(nc, k, g, l, s, dr)
        if stop == "proj":
            continue
        _plog(k, "L%d s%d mla_prep" % (l, s))
        phase_mla_prep(nc, k, g, l, s, dr)
        if stop == "mla_prep":
            return
        _plog(k, "L%d s%d na" % (l, s))
        phase_na(nc, k, g, l, s, dr)
        if stop == "na":
            return
        _plog(k, "L%d s%d mla" % (l, s))
        phase_mla(nc, k, g, l, s, dr)
        if stop == "mla":
            return
    if stop in ("proj", "mix"):
        return
    for s in range(NS):
        _plog(k, "L%d s%d merge" % (l, s))
        phase_merge(nc, k, g, l, s, dr)
    if stop == "merge":
        return
    _plog(k, "L%d moe" % l)
    phase_moe(nc, k, g, l, dr)
    if stop == "moe":
        return
    for s in range(NS):
        _plog(k, "L%d s%d ln2" % (l, s))
        phase_ln2(nc, k, g, l, s, dr)
    _plog(k, "L%d end" % l)


def _prep_inputs(inputs):
    f = lambda a: np.ascontiguousarray(np.asarray(a, dtype=np.float32))
    x, c, ctx, c_ctx = f(inputs["x"]), f(inputs["c"]), f(inputs["ctx"]), f(inputs["c_ctx"])
    shared = {}
    for nm in ("w_ada", "w_in", "w_uq", "w_ukv", "w_pw2", "w_oa", "w_ob", "w_out", "w_router", "w_gate", "w_up", "w_down"):
        shared[nm] = f(inputs[nm])
    w_in = shared["w_in"]
    kr = w_in[:, :, 1920:1952]
    sw = np.concatenate([np.arange(8, 16), np.arange(0, 8), np.arange(24, 32), np.arange(16, 24)])
    w_kr = np.zeros((L, D, 192), np.float32)
    w_kr[:, :, 64:96] = kr
    w_kr[:, :, 160:192] = kr[:, :, sw]
    shared["w_kr"] = w_kr
    wq = shared["w_uq"].reshape(L, 256, 8, 96)
    wqs = wq.copy()
    wqs[:, :, :, 64:96] = wq[:, :, :, 64:96][:, :, :, sw]
    shared["w_uq_sw"] = np.ascontiguousarray(wqs.reshape(L, 256, 768))
    vecs = np.zeros((L, 128, NV), np.float32)

    def put(col, v, nch):
        vecs[:, :, col:col + nch] = f(v).reshape(L, nch, 128).transpose(0, 2, 1)

    put(V_BADA, inputs["b_ada"], 48)
    put(V_LN1G, inputs["ln1_g"], 8)
    put(V_LN1B, inputs["ln1_b"], 8)
    put(V_LN2G, inputs["ln2_g"], 8)
    put(V_LN2B, inputs["ln2_b"], 8)
    put(V_GQ, inputs["g_q"], 2)
    put(V_GKV, inputs["g_kv"], 1)
    put(V_BDW, inputs["b_dw"], 4)
    put(V_GCN, inputs["g_cn"], 4)
    put(V_BCN, inputs["b_cn"], 4)
    wdw = f(inputs["w_dw"])
    vecs[:, :, V_WDW:V_WDW + 124] = wdw.reshape(L, 31, 4, 128).transpose(0, 3, 2, 1).reshape(L, 128, 124)
    shared["vecs"] = vecs
    shared["ident"] = np.eye(128, dtype=np.float32)
    t = np.arange(T)
    row = (t // 64).astype(np.float32)
    col = (t % 64).astype(np.float32)
    inv = (np.float32(10000.0) ** (-np.arange(8, dtype=np.float32) / np.float32(8))).astype(np.float32)
    ar = row[None, :] * inv[:, None]
    ac = col[None, :] * inv[:, None]
    Cc = np.ones((32, TT), np.float32)
    Ss = np.zeros((32, TT), np.float32)
    Cc[0:8, C:] = np.cos(ar); Cc[8:16, C:] = np.cos(ar); Cc[16:24, C:] = np.cos(ac); Cc[24:32, C:] = np.cos(ac)
    Ss[0:8, C:] = -np.sin(ar); Ss[8:16, C:] = np.sin(ar); Ss[16:24, C:] = -np.sin(ac); Ss[24:32, C:] = np.sin(ac)
    shared["ropeC"] = Cc
    shared["ropeS"] = Ss
    rpb = f(inputs["rpb"])
    rpb_ext = np.concatenate([rpb, np.full((L, 8, 15, 1), NEG, np.float32)], axis=-1)
    qc = np.arange(64)[:, None]
    kc = np.arange(64)[None, :]
    cstart = np.clip(qc - 8, 0, 48)
    valid = (kc >= cstart) & (kc < cstart + 16)
    idx = np.clip(kc - qc, -15, 15) + 15
    idx = np.where(valid, idx, 31)
    bt = rpb_ext[:, :, :, idx]
    bt = bt.transpose(0, 3, 1, 2, 4).reshape(L, 64, 7680)
    shared["biasT"] = np.ascontiguousarray(np.concatenate([bt, bt], axis=1))
    in_maps = []
    for i in range(NCORES):
        s0 = i * NS
        xT = np.empty((NS, D, TT), np.float32)
        for j in range(NS):
            xT[j, :, :C] = ctx[s0 + j].T
            xT[j, :, C:] = x[s0 + j].T
        cT = np.empty((128, 8, 3), np.float32)
        for j in range(NS):
            cT[:, :, j] = c[s0 + j].reshape(8, 128).T
        cT[:, :, 2] = c_ctx.reshape(8, 128).T
        m = dict(shared)
        m["xT"] = xT
        m["cT"] = cT
        in_maps.append(m)
    return in_maps


_NC_CACHE = {}


def kernel(**inputs):
    in_maps = _prep_inputs(inputs)
    if "nc" not in _NC_CACHE:
        _NC_CACHE["nc"] = build()
    nc = _NC_CACHE["nc"]
    res = run_bass_kernel_spmd(nc, in_maps, core_ids=list(range(NCORES)))
    out = np.empty((NCORES * NS, T, D), np.float32)
    for i in range(NCORES):
        o = res.results[i]["outT"]
        for j in range(NS):
            out[i * NS + j] = o[j].T
    return out
```

```python
import numpy as np
from contextlib import ExitStack
import concourse.bass as bass
import concourse.mybir as mybir
from concourse.bass_utils import run_bass_kernel_spmd

F32 = mybir.dt.float32
BF16 = mybir.dt.bfloat16
I32 = mybir.dt.int32
U32 = mybir.dt.uint32
AF = mybir.ActivationFunctionType
ALU = mybir.AluOpType

NCORES = 8
NS = 2
D = 1024
T = 2048
C = 256
TT = C + T
L = 4
NE = 16
IN_COLS = 6048
ALPHA = float((2 * L) ** 0.25)
LN_EPS = 1e-5
RMS_EPS = 1e-6
NEG = -30000.0
BLOCKS = [(0, 256), (256, 512), (768, 512), (1280, 512), (1792, 512)]
V_BADA = 0
V_LN1G = 48
V_LN1B = 56
V_LN2G = 64
V_LN2B = 72
V_GQ = 80
V_GKV = 82
V_BDW = 83
V_GCN = 87
V_BCN = 91
V_WDW = 95
NV = 95 + 124


class Buf:
    __slots__ = ("name", "w", "r")

    def __init__(self, name=""):
        self.name = name
        self.w = None
        self.r = {}


class Tile:
    def __init__(self, t, name):
        self.t = t
        self.b = Buf(name)


class KB:
    NDMA = 32
    NPDMA = 16

    def __init__(self, nc, es):
        self.nc = nc
        self.engs = {"pe": nc.tensor, "act": nc.scalar, "dve": nc.vector, "pool": nc.gpsimd, "sp": nc.sync}
        self.sems = {}
        for e in ("pe", "act", "dve", "pool"):
            self.sems[e] = es.enter_context(nc.semaphore("s_" + e))
        self.cnt = {e: 0 for e in self.sems}
        self.dsem = [es.enter_context(nc.semaphore("s_dma%d" % i)) for i in range(self.NDMA + self.NPDMA)]
        self.duse = [0] * (self.NDMA + self.NPDMA)
        self.dnext = 0
        self.pnext = 0
        self.waited = {}
        self.dbufs = {}
        self.ninst = 0

    def _sem(self, key):
        if isinstance(key, tuple):
            return self.dsem[key[1]]
        return self.sems[key]

    def _wait(self, F, deps):
        eng = self.engs[F]
        for key, val in deps.items():
            if val <= 0:
                continue
            if F == "pe" and key == "pe":
                continue
            if self.waited.get((F, key), 0) >= val:
                continue
            eng.wait_ge(self._sem(key), val)
            self.waited[(F, key)] = val
            self.ninst += 1

    def _collect(self, reads, writes):
        deps = {}

        def add(k, v):
            if deps.get(k, 0) < v:
                deps[k] = v

        for b in reads:
            if b.w is not None:
                add(*b.w)
        for b in writes:
            if b.w is not None:
                add(*b.w)
            for k, v in b.r.items():
                add(k, v)
        return deps

    def _mark(self, tok, reads, writes):
        k, v = tok
        for b in reads:
            if b.r.get(k, 0) < v:
                b.r[k] = v
        for b in writes:
            b.w = tok
            b.r = {}

    def op(self, F, fn, reads=(), writes=()):
        self._wait(F, self._collect(reads, writes))
        inst = fn(self.engs[F])
        self.cnt[F] += 1
        inst.then_inc(self.sems[F], 1)
        self._mark((F, self.cnt[F]), reads, writes)
        self.ninst += 1
        return inst

    def dma(self, Q, fn, reads=(), writes=()):
        if Q == "pool":
            i = self.NDMA + self.pnext
            self.pnext = (self.pnext + 1) % self.NPDMA
        else:
            i = self.dnext
            self.dnext = (self.dnext + 1) % self.NDMA
        key = ("d", i)
        deps = self._collect(reads, writes)
        prev = 16 * self.duse[i]
        if prev and deps.get(key, 0) < prev:
            deps[key] = prev
        self._wait(Q, deps)
        inst = fn(self.engs[Q])
        self.duse[i] += 1
        inst.then_inc(self.dsem[i], 16)
        self._mark((key, 16 * self.duse[i]), reads, writes)
        self.ninst += 1
        return inst

    def barrier(self):
        deps = {e: self.cnt[e] for e in self.cnt}
        for i in range(self.NDMA + self.NPDMA):
            if self.duse[i]:
                deps[("d", i)] = 16 * self.duse[i]
        for F in self.engs:
            self._wait(F, dict(deps))

    def dall(self, *prefix):
        return [b for key, b in self.dbufs.items() if key[:len(prefix)] == prefix]

    def dbuf(self, *key):
        b = self.dbufs.get(key)
        if b is None:
            b = Buf(str(key))
            self.dbufs[key] = b
        return b


class Ctx:
    pass


PHASE_LOG = []


def _plog(k, name):
    PHASE_LOG.append((name, dict(k.cnt)))


_UID = [0]


def _sb(nc, es, name, shape, dtype):
    _UID[0] += 1
    return Tile(es.enter_context(nc.sbuf_tensor("sb%d_%s" % (_UID[0], name), list(shape), dtype)), name)


class WStream:
    def __init__(self, k, g, units, pf=2, cast=True):
        self.k, self.g, self.units, self.pf, self.cast = k, g, units, pf, cast
        self.issued = 0
        self.slots = {}

    def _issue(self, i):
        k, g = self.k, self.g
        src, shape = self.units[i]
        st = g.wst[g.wst_i % len(g.wst)]
        g.wst_i += 1
        n = 1
        for d in shape[1:]:
            n *= d
        stv = st.t[:, 0:n]
        if len(shape) == 3:
            stv = stv.rearrange("p (a b) -> p a b", a=shape[1])
        k.dma("sp", lambda e: e.dma_start(out=stv, in_=src), reads=(), writes=[st.b])
        if not self.cast:
            self.slots[i] = (stv, st.b)
            return
        bf = g.wbf[g.wbf_i % len(g.wbf)]
        g.wbf_i += 1
        bfv = bf.t[:, 0:n]
        if len(shape) == 3:
            bfv = bfv.rearrange("p (a b) -> p a b", a=shape[1])
        ce = g.cast_engs[g.cast_i % len(g.cast_engs)]
        g.cast_i += 1
        if ce == "act":
            k.op("act", lambda e: e.copy(out=bfv, in_=stv), reads=[st.b], writes=[bf.b])
        else:
            k.op(ce, lambda e: e.tensor_copy(out=bfv, in_=stv), reads=[st.b], writes=[bf.b])
        self.slots[i] = (bfv, bf.b)

    def get(self, i):
        while self.issued < min(len(self.units), i + 1 + self.pf):
            self._issue(self.issued)
            self.issued += 1
        r = self.slots[i]
        if i - 1 in self.slots and i - 1 >= 0:
            pass
        return r


def mm(k, ps, out_ap, pairs, reads):
    n = len(pairs)
    for i, (l, r) in enumerate(pairs):
        k.op("pe", lambda e, l=l, r=r, i=i: e.matmul(out_ap, lhsT=l, rhs=r, start=(i == 0), stop=(i == n - 1)),
             reads=reads, writes=[ps.b])


def col_stats(k, g, zs, zb, N, nfeat, eps, mean_needed, sq, mean_t, rstd_t, z16=None):
    nch = len(zs)
    ps1 = g.ps[6]
    ps2 = g.ps[7]
    for c, z in enumerate(zs):
        k.op("act", lambda e, c=c, z=z: e.activation(out=sq.t[:, c, 0:N], in_=z, func=AF.Square), reads=[zb], writes=[sq.b])
    if mean_needed:
        for c, z in enumerate(zs):
            k.op("act", lambda e, c=c, z=z: e.copy(out=z16.t[:, c, 0:N], in_=z), reads=[zb], writes=[z16.b])
        mm(k, ps1, ps1.t[:, 0:N], [(g.ones_b.t[:, :], z16.t[:, c, 0:N]) for c in range(nch)], [z16.b, g.ones_b.b])
    mm(k, ps2, ps2.t[:, 0:N], [(g.ones_b.t[:, :], sq.t[:, c, 0:N]) for c in range(nch)], [sq.b, g.ones_b.b])
    inv = 1.0 / nfeat
    if mean_needed:
        k.op("act", lambda e: e.activation(out=mean_t.t[:, 0:N], in_=ps1.t[:, 0:N], func=AF.Copy, scale=inv),
             reads=[ps1.b], writes=[mean_t.b])
        k.op("dve", lambda e: e.tensor_tensor(out=rstd_t.t[:, 0:N], in0=mean_t.t[:, 0:N], in1=mean_t.t[:, 0:N], op=ALU.mult),
             reads=[mean_t.b], writes=[rstd_t.b])
        k.op("dve", lambda e: e.scalar_tensor_tensor(out=rstd_t.t[:, 0:N], in0=ps2.t[:, 0:N], scalar=inv, in1=rstd_t.t[:, 0:N],
                                                     op0=ALU.mult, op1=ALU.subtract),
             reads=[ps2.b, rstd_t.b], writes=[rstd_t.b])
        k.op("act", lambda e: e.activation(out=rstd_t.t[:, 0:N], in_=rstd_t.t[:, 0:N], func=AF.Ln, bias=g.eps_ln.t[:, 0:1], scale=1.0),
             reads=[rstd_t.b, g.eps_ln.b], writes=[rstd_t.b])
    else:
        k.op("act", lambda e: e.activation(out=rstd_t.t[:, 0:N], in_=ps2.t[:, 0:N], func=AF.Ln, bias=g.eps_rms.t[:, 0:1], scale=inv),
             reads=[ps2.b, g.eps_rms.b], writes=[rstd_t.b])
    k.op("act", lambda e: e.activation(out=rstd_t.t[:, 0:N], in_=rstd_t.t[:, 0:N], func=AF.Exp, scale=-0.5), reads=[rstd_t.b], writes=[rstd_t.b])


class AttJob:
    def __init__(self, k, g, N, q_ap, q_bufs, chunks, scale, hp, out_ap, out_buf, sbanks, obank, dbank, pts, rec):
        self.k, self.g, self.N, self.q_ap, self.q_bufs, self.chunks = k, g, N, q_ap, q_bufs, chunks
        self.scale, self.hp, self.out_ap, self.out_buf = scale, hp, out_ap, out_buf
        self.sbanks, self.obank, self.dbank, self.pts, self.rec = sbanks, obank, dbank, pts, rec
        G = max(1, 512 // N)
        self.groups = [(ci, chunks[ci:ci + G]) for ci in range(0, len(chunks), G)]
        self.ngroups = len(self.groups)
        self.sb = {}

    def qk(self, gi):
        k, g, N = self.k, self.g, self.N
        ci, grp = self.groups[gi]
        sb_ = self.sbanks[g.sb_i % len(self.sbanks)]
        g.sb_i += 1
        self.sb[gi] = sb_
        for j, ch in enumerate(grp):
            o = sb_.t[:, j * N:(j + 1) * N]
            has_b = ch.get("bias") is not None
            k.op("pe", lambda e, o=o, ch=ch, has_b=has_b: e.matmul(o, lhsT=ch["kT"], rhs=self.q_ap, start=True, stop=not has_b),
                 reads=list(ch["bufs"]) + list(self.q_bufs), writes=[sb_.b])
            if has_b:
                bl, br, bb = ch["bias"]
                k.op("pe", lambda e, o=o, bl=bl, br=br: e.matmul(o, lhsT=bl, rhs=br, start=False, stop=True), reads=bb, writes=[sb_.b])

    def rest(self, gi):
        k, g, N = self.k, self.g, self.N
        ci, grp = self.groups[gi]
        nchunks = len(self.chunks)
        sb_ = self.sb.pop(gi)
        pt = self.pts[g.pt_i % len(self.pts)]
        g.pt_i += 1
        w = len(grp) * N
        k.op("act", lambda e: e.activation(out=pt.t[:, 0:w], in_=sb_.t[:, 0:w], func=AF.Exp, scale=self.scale), reads=[sb_.b], writes=[pt.b])
        for j, ch in enumerate(grp):
            first, last = (ci + j == 0), (ci + j == nchunks - 1)
            p_ap = pt.t[:, j * N:(j + 1) * N]
            k.op("pe", lambda e, ch=ch, p_ap=p_ap, first=first, last=last: e.matmul(self.obank.t[:, 0:N], lhsT=ch["v"], rhs=p_ap, start=first, stop=last),
                 reads=list(ch["bufs"]) + [pt.b], writes=[self.obank.b])

    def fin(self):
        k, N, rec = self.k, self.N, self.rec
        r0 = self.hp * 64
        d0 = 64 - r0
        if N >= 256:
            k.op("act", lambda e: e.activation(out=rec.t[r0:r0 + 64, 0:N], in_=self.obank.t[d0:d0 + 64, 0:N], func=AF.Ln), reads=[self.obank.b], writes=[rec.b])
            k.op("act", lambda e: e.activation(out=rec.t[r0:r0 + 64, 0:N], in_=rec.t[r0:r0 + 64, 0:N], func=AF.Exp, scale=-1.0), reads=[rec.b], writes=[rec.b])
        else:
            k.op("dve", lambda e: e.reciprocal(out=rec.t[r0:r0 + 64, 0:N], in_=self.obank.t[d0:d0 + 64, 0:N]), reads=[self.obank.b], writes=[rec.b])
        k.op("dve", lambda e: e.tensor_tensor(out=self.out_ap, in0=self.obank.t[r0:r0 + 64, 0:N], in1=rec.t[r0:r0 + 64, 0:N], op=ALU.mult),
             reads=[self.obank.b, rec.b], writes=[self.out_buf])


def run_attention(jobs, side=None):
    flat = [(job, gi) for job in jobs for gi in range(job.ngroups)]
    if not flat:
        return
    LA = 3
    issued = 0
    pending = None
    for i, (job, gi) in enumerate(flat):
        while issued < min(len(flat), i + 1 + LA):
            flat[issued][0].qk(flat[issued][1])
            issued += 1
        job.rest(gi)
        if pending is not None:
            pending.fin()
            pending = None
        if side is not None:
            side()
        if gi == job.ngroups - 1:
            pending = job
    if pending is not None:
        pending.fin()


def attention(k, g, N, q_ap, q_bufs, chunks, scale, hp, out_ap, out_buf, sbanks, obank, dbank, pts, rec):
    return AttJob(k, g, N, q_ap, q_bufs, chunks, scale, hp, out_ap, out_buf, sbanks, obank, dbank, pts, rec)


def build(n_layers=L, dbg=(), stop_after=None):
    nc = bass.Bass("TRN2", target_bir_lowering=False)
    g = Ctx()

    def din(name, shape, dt=F32):
        return nc.dram_tensor(name, list(shape), dt, kind="ExternalInput").ap()

    def dscr(name, shape, dt):
        kind = "ExternalOutput" if name in dbg else "Internal"
        return nc.dram_tensor(name, list(shape), dt, kind=kind).ap()

    xT_in = din("xT", [NS, D, TT])
    cT_in = din("cT", [128, 8, 3])
    vecs_in = din("vecs", [L, 128, NV])
    ident_in = din("ident", [128, 128])
    ropeC_in = din("ropeC", [32, TT])
    ropeS_in = din("ropeS", [32, TT])
    biasT_rep = din("biasT", [L, 128, 7680])
    w_ada = din("w_ada", [L, D, 6 * D])
    w_in = din("w_in", [L, D, IN_COLS])
    w_kr = din("w_kr", [L, D, 192])
    w_uq = din("w_uq", [L, 256, 768])
    w_uqs = din("w_uq_sw", [L, 256, 768])
    w_ukv = din("w_ukv", [L, 128, 1024])
    w_pw2 = din("w_pw2", [L, 512, D])
    w_oa = din("w_oa", [L, 512, D])
    w_ob = din("w_ob", [L, 512, D])
    w_out = din("w_out", [L, D, D])
    w_router = din("w_router", [L, D, NE])
    w_gate = din("w_gate", [L, NE, D, D])
    w_up = din("w_up", [L, NE, D, D])
    w_down = din("w_down", [L, NE, D, D])
    outT = nc.dram_tensor("outT", [NS, D, T], F32, kind="ExternalOutput").ap()

    xres = dscr("xres", [NS, D, TT], F32)
    x1res = dscr("x1res", [NS, D, TT], F32)
    qaT_d = dscr("qaT_d", [NS, 512, TT], BF16)
    kaT_d = dscr("kaT_d", [NS, 512, TT], BF16)
    vae_d = dscr("vae_d", [NS, 18, 128, 768], BF16)
    vao_d = dscr("vao_d", [NS, 15, 128, 768], BF16)
    cq_d = dscr("cq_d", [NS, 256, TT], F32)
    ckv_d = dscr("ckv_d", [NS, 128, TT], F32)
    kr_d = dscr("kr_d", [NS, 2, 32, TT], F32)
    u_d = dscr("u_d", [NS, 512, TT], F32)
    gT_d = dscr("gT_d", [NS, 3072, TT], BF16)
    qbT_d = dscr("qbT_d", [NS, 8, 96, TT], BF16)
    kbT_d = dscr("kbT_d", [NS, 8, 96, TT], BF16)
    vb_d = dscr("vb_d", [NS, 18, 128, 768], BF16)
    yaT_d = dscr("yaT_d", [NS, 512, TT], BF16)
    ybT_d = dscr("ybT_d", [NS, 512, TT], BF16)
    ucT_d = dscr("ucT_d", [NS, 512, TT], BF16)
    h2tok_d = dscr("h2tok_d", [NS * TT, D], BF16)
    afftok_d = dscr("afftok_d", [NS * TT, NE], F32)
    moe_d = dscr("moe_d", [NS * TT, D], F32)
    mod_d = dscr("mod_d", [L, 128, 48, 3], F32)
    affT_d = dscr("affT_d", [NS, NE, TT], F32)

    with ExitStack() as es:
        k = KB(nc, es)
        g.k = k
        g.ps = [Tile(es.enter_context(nc.psum_tensor("ps%d" % i, [128, 512], F32)), "ps%d" % i) for i in range(8)]
        g.ident_f = _sb(nc, es, "ident_f", [128, 128], F32)
        g.ident_b = _sb(nc, es, "ident_b", [128, 128], BF16)
        g.ones_f = _sb(nc, es, "ones_f", [128, 128], F32)
        g.ones_b = _sb(nc, es, "ones_b", [128, 128], BF16)
        g.eps_ln = _sb(nc, es, "eps_ln", [128, 1], F32)
        g.eps_rms = _sb(nc, es, "eps_rms", [128, 1], F32)
        g.vecs = _sb(nc, es, "vecs", [128, L, NV], F32)
        g.mod = _sb(nc, es, "mod", [128, L, 48, 3], F32)
        g.wst_i = g.wbf_i = g.cast_i = 0
        g.cast_engs = ["act", "pool"]
        g.sb_i = g.pt_i = 0

        k.dma("sp", lambda e: e.dma_start(out=g.ident_f.t[:, :], in_=ident_in[:, :]), writes=[g.ident_f.b])
        k.op("dve", lambda e: e.tensor_copy(out=g.ident_b.t[:, :], in_=g.ident_f.t[:, :]), reads=[g.ident_f.b], writes=[g.ident_b.b])
        k.op("dve", lambda e: e.memset(g.ones_f.t[:, :], 1.0), writes=[g.ones_f.b])
        k.op("dve", lambda e: e.memset(g.ones_b.t[:, :], 1.0), writes=[g.ones_b.b])
        k.op("dve", lambda e: e.memset(g.eps_ln.t[:, :], LN_EPS), writes=[g.eps_ln.b])
        k.op("dve", lambda e: e.memset(g.eps_rms.t[:, :], RMS_EPS), writes=[g.eps_rms.b])
        for l in range(L):
            k.dma("sp", lambda e, l=l: e.dma_start(out=g.vecs.t[:, l, :], in_=vecs_in[l, :, :]), writes=[g.vecs.b])

        with ExitStack() as ph:
            ring(nc, g, ph, 3, 0)
            cT = _sb(nc, ph, "cT", [128, 8, 3], F32)
            k.dma("sp", lambda e: e.dma_start(out=cT.t[:, :, :], in_=cT_in[:, :, :]), writes=[cT.b])
            k.op("act", lambda e: e.activation(out=cT.t[:, :, :], in_=cT.t[:, :, :], func=AF.Silu), reads=[cT.b], writes=[cT.b])
            astg = _sb(nc, ph, "astg", [3, 6 * D], F32)
            for l in range(n_layers):
                units = [(w_ada[l, :, gi * 512:(gi + 1) * 512].rearrange("(kc p) n -> p kc n", p=128), (128, 8, 512)) for gi in range(12)]
                ws = WStream(k, g, units, pf=2, cast=False)
                for gi in range(12):
                    wv, wb = ws.get(gi)
                    ps = g.ps[gi % 4]
                    mm(k, ps, ps.t[0:3, 0:512], [(cT.t[:, kc, :], wv[:, kc, :]) for kc in range(8)], [wb, cT.b])
                    k.op("act", lambda e, ps=ps, gi=gi: e.copy(out=astg.t[:, gi * 512:(gi + 1) * 512], in_=ps.t[0:3, 0:512]), reads=[ps.b], writes=[astg.b])
                pt_ = g.ps[4 + (l % 2)]
                for oc in range(48):
                    k.op("pe", lambda e, oc=oc, pt_=pt_: e.transpose(pt_.t[:, oc * 3:(oc + 1) * 3], astg.t[0:3, oc * 128:(oc + 1) * 128], g.ident_f.t[0:3, 0:3]),
                         reads=[astg.b, g.ident_f.b], writes=[pt_.b])
                p3 = pt_.t[:, 0:144].rearrange("p (o j) -> p o j", j=3)
                for j in range(3):
                    k.op("dve", lambda e, j=j, l=l, p3=p3: e.tensor_tensor(out=g.mod.t[:, l, :, j], in0=p3[:, :, j], in1=g.vecs.t[:, l, V_BADA:V_BADA + 48], op=ALU.add),
                         reads=[pt_.b, g.vecs.b], writes=[g.mod.b])
                for r in (1, 4):
                    k.op("dve", lambda e, r=r, l=l: e.tensor_scalar(out=g.mod.t[:, l, r * 8:(r + 1) * 8, :], in0=g.mod.t[:, l, r * 8:(r + 1) * 8, :],
                                                                    scalar1=1.0, scalar2=None, op0=ALU.add), reads=[g.mod.b], writes=[g.mod.b])
            mm(k, g.ps[7], g.ps[7].t[0:64, 0:8], [(g.ones_b.t[:, 0:64], g.ones_b.t[:, 0:8])], [g.ones_b.b])
            if "mod_d" in dbg:
                for l in range(n_layers):
                    k.dma("sp", lambda e, l=l: e.dma_start(out=mod_d[l], in_=g.mod.t[:, l, :, :]), reads=[g.mod.b], writes=[k.dbuf("mod_d")])
            k.barrier()

        def modv(l, role, c, j):
            return g.mod.t[:, l, role * 8 + c, j:j + 1]

        if stop_after != "ada":
            for l in range(n_layers):
                layer(nc, k, g, l, locals())
        k.barrier()
    return nc


def ring(nc, g, ph, nst, nbf, cast_engs=("act", "pool", "dve")):
    g.cast_engs = list(cast_engs)
    g.wst = [_sb(nc, ph, "wst%d" % i, [128, 4096], F32) for i in range(nst)]
    g.wbf = [_sb(nc, ph, "wbf%d" % i, [128, 4096], BF16) for i in range(nbf)]


def load_cast(k, g, src, shape, dst_view, dst_buf, eng=None):
    st = g.wst[g.wst_i % len(g.wst)]
    g.wst_i += 1
    n = 1
    for d in shape[1:]:
        n *= d
    stv = st.t[0:shape[0], 0:n]
    if len(shape) == 3:
        stv = stv.rearrange("p (a b) -> p a b", a=shape[1])
    k.dma("sp", lambda e: e.dma_start(out=stv, in_=src), writes=[st.b])
    if eng is None:
        eng = g.cast_engs[g.cast_i % len(g.cast_engs)]
        g.cast_i += 1
    if eng == "act":
        k.op("act", lambda e: e.copy(out=dst_view, in_=stv), reads=[st.b], writes=[dst_buf])
    else:
        k.op(eng, lambda e: e.tensor_copy(out=dst_view, in_=stv), reads=[st.b], writes=[dst_buf])


def phase_proj(nc, k, g, l, s, dr):
    X = dr["xT_in"] if l == 0 else dr["xres"]
    w_in = dr["w_in"]
    with ExitStack() as ph:
        ring(nc, g, ph, 3, 4)
        hT = _sb(nc, ph, "hT", [128, 8, TT], BF16)
        hTb = [Buf("hT%d" % i) for i in range(len(BLOCKS))]
        xin = [_sb(nc, ph, "xin%d" % i, [128, 8, 512], F32) for i in range(1)]
        stg_b = [_sb(nc, ph, "stgb%d" % i, [128, 4, 512], BF16) for i in range(2)]
        stg_f = [_sb(nc, ph, "stgf%d" % i, [128, 4, 512], F32) for i in range(2)]
        sig = [_sb(nc, ph, "sig%d" % i, [128, 512], F32) for i in range(2)]
        vst = [_sb(nc, ph, "vst%d" % i, [128, 768], BF16) for i in range(2)]
        for v_ in vst:
            k.op("pool", lambda e, v_=v_: e.memset(v_.t[:, :], 1.0), writes=[v_.b])
        for bi, (c0, N) in enumerate(BLOCKS):
            xt = xin[0]
            j = 2 if bi == 0 else s
            k.dma("sp", lambda e, xt=xt, c0=c0, N=N: e.dma_start(out=xt.t[:, :, 0:N], in_=X[s, :, c0:c0 + N].rearrange("(kc p) n -> p kc n", p=128)),
                  reads=[k.dbuf("x", s, bi)], writes=[xt.b])
            for kc in range(8):
                if bi % 2 == 0:
                    k.op("dve", lambda e, xt=xt, kc=kc, c0=c0, N=N, j=j: e.tensor_scalar(
                        out=hT.t[:, kc, c0:c0 + N], in0=xt.t[:, kc, 0:N], scalar1=g.mod.t[:, l, 8 + kc, j:j + 1],
                        scalar2=g.mod.t[:, l, kc, j:j + 1], op0=ALU.mult, op1=ALU.add), reads=[xt.b, g.mod.b], writes=[hTb[bi]])
                else:
                    k.op("act", lambda e, xt=xt, kc=kc, c0=c0, N=N, j=j: e.activation(
                        out=hT.t[:, kc, c0:c0 + N], in_=xt.t[:, kc, 0:N], func=AF.Identity, scale=g.mod.t[:, l, 8 + kc, j:j + 1],
                        bias=g.mod.t[:, l, kc, j:j + 1]), reads=[xt.b, g.mod.b], writes=[hTb[bi]])

        def wu(c0, n):
            return (w_in[l, :, c0:c0 + n].rearrange("(kc p) n -> p kc n", p=128), (128, 8, n))

        units = [wu(0, 512), wu(512, 512), wu(1024, 512), wu(1536, 256), wu(1792, 128),
                 (dr["w_kr"][l, :, :].rearrange("(kc p) n -> p kc n", p=128), (128, 8, 192)),
                 wu(1952, 512), wu(2464, 512)] + [wu(2976 + 512 * i, 512) for i in range(6)]
        ws = WStream(k, g, units, pf=2)
        cnt = [0]

        def nextps():
            p = g.ps[cnt[0] % 6]
            cnt[0] += 1
            return p

        def fm_tile(wv, wb, mcol, mw, bi):
            c0, N = BLOCKS[bi]
            ps = nextps()
            mm(k, ps, ps.t[0:mw, 0:N], [(wv[:, kc, mcol:mcol + mw], hT.t[:, kc, c0:c0 + N]) for kc in range(8)], [wb, hTb[bi]])
            return ps

        it = [0]
        for u, (dst, name, scl) in enumerate([(dr["qaT_d"], "qaT", 0.125), (dr["kaT_d"], "kaT", 1.0)]):
            wv, wb = ws.get(u)
            for bi, (c0, N) in enumerate(BLOCKS):
                st = stg_b[it[0] % 2]
                it[0] += 1
                for mt in range(4):
                    ps = fm_tile(wv, wb, mt * 128, 128, bi)
                    k.op("dve", lambda e, ps=ps, st=st, mt=mt, N=N, scl=scl: e.tensor_scalar(
                        out=st.t[:, mt, 0:N], in0=ps.t[:, 0:N], scalar1=scl, scalar2=None, op0=ALU.mult), reads=[ps.b], writes=[st.b])
                k.dma("sp", lambda e, st=st, dst=dst, c0=c0, N=N: e.dma_start(
                    out=dst[s, :, c0:c0 + N].rearrange("(m p) n -> p m n", p=128), in_=st.t[:, :, 0:N]),
                    reads=[st.b], writes=[k.dbuf(name, s, bi)])
        wv, wb = ws.get(2)

        def blocks_of(t0, t1):
            return [hTb[bi] for bi, (c0, N) in enumerate(BLOCKS) if t0 < c0 + N and t1 > c0]

        for kind, ntile, base, dst in (("e", 18, 0, dr["vae_d"]), ("o", 15, C + 64, dr["vao_d"])):
            for j in range(ntile):
                t0 = base + 128 * j
                ps = nextps()
                mm(k, ps, ps.t[:, 0:512], [(hT.t[:, kc, t0:t0 + 128], wv[:, kc, 0:512]) for kc in range(8)], [wb] + blocks_of(t0, t0 + 128))
                st = vst[it[0] % 2]
                it[0] += 1
                v3 = st.t[:, :].rearrange("p (q c) -> p q c", c=192)
                p3 = ps.t[:, 0:512].rearrange("p (q c) -> p q c", c=128)
                k.op("act", lambda e, v3=v3, p3=p3: e.copy(out=v3[:, :, 0:64], in_=p3[:, :, 0:64]), reads=[ps.b], writes=[st.b])
                k.op("act", lambda e, v3=v3, p3=p3: e.copy(out=v3[:, :, 128:192], in_=p3[:, :, 64:128]), reads=[ps.b], writes=[st.b])
                k.dma("sp", lambda e, st=st, dst=dst, j=j: e.dma_start(out=dst[s, j, :, :], in_=st.t[:, :]),
                      reads=[st.b], writes=[k.dbuf("va" + kind, s, j)])
        for u, nm, dst, nmt in ((3, "cq", dr["cq_d"], 2), (4, "ckv", dr["ckv_d"], 1)):
            wv, wb = ws.get(u)
            for bi, (c0, N) in enumerate(BLOCKS):
                st = stg_f[it[0] % 2]
                it[0] += 1
                for mt in range(nmt):
                    ps = fm_tile(wv, wb, mt * 128, 128, bi)
                    k.op("dve", lambda e, ps=ps, st=st, mt=mt, N=N: e.tensor_copy(out=st.t[:, mt, 0:N], in_=ps.t[:, 0:N]), reads=[ps.b], writes=[st.b])
                k.dma("sp", lambda e, st=st, dst=dst, c0=c0, N=N, nmt=nmt: e.dma_start(
                    out=dst[s, :, c0:c0 + N].rearrange("(m p) n -> p m n", p=128), in_=st.t[:, 0:nmt, 0:N]),
                    reads=[st.b], writes=[k.dbuf(nm, s, bi)])
        wv, wb = ws.get(5)
        for bi, (c0, N) in enumerate(BLOCKS):
            st = stg_f[it[0] % 2]
            it[0] += 1
            for mt in range(2):
                ps = fm_tile(wv, wb, mt * 96, 96, bi)
                k.op("dve", lambda e, ps=ps, st=st, mt=mt, N=N: e.tensor_copy(out=st.t[64:96, mt, 0:N], in_=ps.t[64:96, 0:N]), reads=[ps.b], writes=[st.b])
            for mt in range(2):
                k.dma("sp", lambda e, st=st, c0=c0, N=N, mt=mt: e.dma_start(out=dr["kr_d"][s, mt, :, c0:c0 + N], in_=st.t[64:96, mt, 0:N]),
                      reads=[st.b], writes=[k.dbuf("kr", s, bi, mt)])
        wva, wba = ws.get(6)
        wvg, wbg = ws.get(7)
        for bi, (c0, N) in enumerate(BLOCKS):
            st = stg_f[it[0] % 2]
            it[0] += 1
            for mt in range(4):
                psa = fm_tile(wva, wba, mt * 128, 128, bi)
                psg = fm_tile(wvg, wbg, mt * 128, 128, bi)
                sg = sig[mt % 2]
                k.op("act", lambda e, psg=psg, sg=sg, N=N: e.activation(out=sg.t[:, 0:N], in_=psg.t[:, 0:N], func=AF.Sigmoid), reads=[psg.b], writes=[sg.b])
                k.op("dve", lambda e, psa=psa, sg=sg, st=st, mt=mt, N=N: e.tensor_tensor(out=st.t[:, mt, 0:N], in0=psa.t[:, 0:N], in1=sg.t[:, 0:N], op=ALU.mult),
                     reads=[psa.b, sg.b], writes=[st.b])
            k.dma("sp", lambda e, st=st, c0=c0, N=N: e.dma_start(out=dr["u_d"][s, :, c0:c0 + N].rearrange("(m p) n -> p m n", p=128), in_=st.t[:, :, 0:N]),
                  reads=[st.b], writes=[k.dbuf("u", s, bi)])
        for gu in range(6):
            wv, wb = ws.get(8 + gu)
            for bi, (c0, N) in enumerate(BLOCKS):
                st = stg_b[it[0] % 2]
                it[0] += 1
                for mt in range(4):
                    ps = fm_tile(wv, wb, mt * 128, 128, bi)
                    k.op("act", lambda e, ps=ps, st=st, mt=mt, N=N: e.activation(out=st.t[:, mt, 0:N], in_=ps.t[:, 0:N], func=AF.Sigmoid), reads=[ps.b], writes=[st.b])
                k.dma("sp", lambda e, st=st, c0=c0, N=N, gu=gu: e.dma_start(
                    out=dr["gT_d"][s, gu * 512:(gu + 1) * 512, c0:c0 + N].rearrange("(m p) n -> p m n", p=128), in_=st.t[:, :, 0:N]),
                    reads=[st.b], writes=[k.dbuf("gT", s, bi, gu)])
        k.barrier()


def phase_mla_prep(nc, k, g, l, s, dr):
    with ExitStack() as ph:
        ring(nc, g, ph, 2, 0)
        Wuq = _sb(nc, ph, "Wuq", [128, 2, 768], BF16)
        Wuqs = _sb(nc, ph, "Wuqs", [128, 2, 768], BF16)
        Wukv = _sb(nc, ph, "Wukv", [128, 1024], BF16)
        load_cast(k, g, dr["w_uq"][l].rearrange("(kc p) n -> p kc n", p=128), (128, 2, 768), Wuq.t[:, :, :], Wuq.b)
        load_cast(k, g, dr["w_uqs"][l].rearrange("(kc p) n -> p kc n", p=128), (128, 2, 768), Wuqs.t[:, :, :], Wuqs.b)
        load_cast(k, g, dr["w_ukv"][l], (128, 1024), Wukv.t[:, :], Wukv.b)
        Wukv3 = Wukv.t[:, :].rearrange("p (h c) -> p h c", h=8)
        cq = [_sb(nc, ph, "cq%d" % i, [128, 2, 512], F32) for i in range(2)]
        ckv = [_sb(nc, ph, "ckv%d" % i, [128, 512], F32) for i in range(2)]
        krt = [_sb(nc, ph, "krt%d" % i, [96, 2, 512], F32) for i in range(2)]
        sq = _sb(nc, ph, "sq", [128, 2, 512], BF16)
        rstd = _sb(nc, ph, "rstd", [128, 512], F32)
        t1 = _sb(nc, ph, "t1", [96, 512], F32)
        t2 = _sb(nc, ph, "t2", [96, 512], F32)
        qbs = [_sb(nc, ph, "qbs%d" % i, [96, 8, 512], BF16) for i in range(2)]
        kbs = [_sb(nc, ph, "kbs%d" % i, [96, 8, 512], BF16) for i in range(2)]
        vbs = [_sb(nc, ph, "vbs%d" % i, [128, 768], BF16) for i in range(2)]
        for v_ in vbs:
            k.op("pool", lambda e, v_=v_: e.memset(v_.t[:, :], 1.0), writes=[v_.b])
        vi = 0
        rCs = [_sb(nc, ph, "rC%d" % i, [96, 512], F32) for i in range(2)]
        rSs = [_sb(nc, ph, "rS%d" % i, [96, 512], F32) for i in range(2)]
        cqns = [_sb(nc, ph, "cqn%d" % i, [128, 2, 512], BF16) for i in range(2)]
        ckvns = [_sb(nc, ph, "ckvn%d" % i, [128, 512], BF16) for i in range(2)]
        kros = [_sb(nc, ph, "kro%d" % i, [96, 512], BF16) for i in range(2)]
        ta1 = _sb(nc, ph, "ta1", [96, 512], F32)
        ta2 = _sb(nc, ph, "ta2", [96, 512], F32)
        vi_ = [0]

        def stage_a(bi):
            c0, N = BLOCKS[bi]
            cqn, ckvn, kro = cqns[bi % 2], ckvns[bi % 2], kros[bi % 2]
            t1, t2 = ta1, ta2
            if True:
                cqt, ckvt, krtt = cq[bi % 2], ckv[bi % 2], krt[bi % 2]
                qb, kb = qbs[bi % 2], kbs[bi % 2]
                rC, rS = rCs[bi % 2], rSs[bi % 2]
                k.dma("sp", lambda e: e.dma_start(out=rC.t[64:96, 0:N], in_=dr["ropeC_in"][:, c0:c0 + N]), writes=[rC.b])
                k.dma("sp", lambda e: e.dma_start(out=rS.t[64:96, 0:N], in_=dr["ropeS_in"][:, c0:c0 + N]), writes=[rS.b])
                k.dma("sp", lambda e: e.dma_start(out=cqt.t[:, :, 0:N], in_=dr["cq_d"][s, :, c0:c0 + N].rearrange("(m p) n -> p m n", p=128)),
                      reads=[k.dbuf("cq", s, bi)], writes=[cqt.b])
                k.dma("sp", lambda e: e.dma_start(out=ckvt.t[:, 0:N], in_=dr["ckv_d"][s, :, c0:c0 + N]), reads=[k.dbuf("ckv", s, bi)], writes=[ckvt.b])
                for mt in range(2):
                    k.dma("sp", lambda e, mt=mt: e.dma_start(out=krtt.t[64:96, mt, 0:N], in_=dr["kr_d"][s, mt, :, c0:c0 + N]),
                          reads=[k.dbuf("kr", s, bi, mt)], writes=[krtt.b])
                col_stats(k, g, [cqt.t[:, 0, 0:N], cqt.t[:, 1, 0:N]], cqt.b, N, 256, RMS_EPS, False, sq, None, rstd)
                for c in range(2):
                    k.op("dve", lambda e, c=c: e.scalar_tensor_tensor(out=cqn.t[:, c, 0:N], in0=cqt.t[:, c, 0:N], scalar=g.vecs.t[:, l, V_GQ + c:V_GQ + c + 1],
                                                                      in1=rstd.t[:, 0:N], op0=ALU.mult, op1=ALU.mult), reads=[cqt.b, rstd.b, g.vecs.b], writes=[cqn.b])
                col_stats(k, g, [ckvt.t[:, 0:N]], ckvt.b, N, 128, RMS_EPS, False, sq, None, rstd)
                k.op("dve", lambda e: e.scalar_tensor_tensor(out=ckvn.t[:, 0:N], in0=ckvt.t[:, 0:N], scalar=g.vecs.t[:, l, V_GKV:V_GKV + 1],
                                                             in1=rstd.t[:, 0:N], op0=ALU.mult, op1=ALU.mult), reads=[ckvt.b, rstd.b, g.vecs.b], writes=[ckvn.b])
                k.op("dve", lambda e: e.tensor_tensor(out=t1.t[64:96, 0:N], in0=krtt.t[64:96, 0, 0:N], in1=rC.t[64:96, 0:N], op=ALU.mult),
                     reads=[krtt.b, rC.b], writes=[t1.b])
                k.op("dve", lambda e: e.tensor_tensor(out=t2.t[64:96, 0:N], in0=krtt.t[64:96, 1, 0:N], in1=rS.t[64:96, 0:N], op=ALU.mult),
                     reads=[krtt.b, rS.b], writes=[t2.b])
                k.op("dve", lambda e: e.tensor_tensor(out=kro.t[64:96, 0:N], in0=t1.t[64:96, 0:N], in1=t2.t[64:96, 0:N], op=ALU.add),
                     reads=[t1.b, t2.b], writes=[kro.b])

        def stage_b(bi):
            c0, N = BLOCKS[bi]
            cqn, ckvn, kro = cqns[bi % 2], ckvns[bi % 2], kros[bi % 2]
            qb, kb = qbs[bi % 2], kbs[bi % 2]
            rC, rS = rCs[bi % 2], rSs[bi % 2]
            vi = vi_[0]
            if True:
                need_q = not (bi == 0 and l == L - 1)
                for h in range(8):
                    ps = g.ps[h % 2]
                    mm(k, ps, ps.t[0:64, 0:N], [(Wukv3[:, h, 0:64], ckvn.t[:, 0:N])], [Wukv.b, ckvn.b])
                    k.op("act", lambda e, ps=ps, h=h: e.copy(out=kb.t[0:64, h, 0:N], in_=ps.t[0:64, 0:N]), reads=[ps.b], writes=[kb.b])
                    k.op("pool", lambda e, h=h: e.tensor_copy(out=kb.t[64:96, h, 0:N], in_=kro.t[64:96, 0:N]), reads=[kro.b], writes=[kb.b])
                    if need_q:
                        pa = g.ps[2 + (h % 2)]
                        pb = g.ps[4 + (h % 2)]
                        mm(k, pa, pa.t[0:96, 0:N], [(Wuq.t[:, kc, h * 96:(h + 1) * 96], cqn.t[:, kc, 0:N]) for kc in range(2)], [Wuq.b, cqn.b])
                        mm(k, pb, pb.t[0:96, 0:N], [(Wuqs.t[:, kc, h * 96:(h + 1) * 96], cqn.t[:, kc, 0:N]) for kc in range(2)], [Wuqs.b, cqn.b])
                        k.op("act", lambda e, pa=pa, h=h: e.copy(out=qb.t[0:64, h, 0:N], in_=pa.t[0:64, 0:N]), reads=[pa.b], writes=[qb.b])
                        k.op("dve", lambda e, pa=pa: e.tensor_tensor(out=t1.t[64:96, 0:N], in0=pa.t[64:96, 0:N], in1=rC.t[64:96, 0:N], op=ALU.mult),
                             reads=[pa.b, rC.b], writes=[t1.b])
                        k.op("dve", lambda e, pb=pb: e.tensor_tensor(out=t2.t[64:96, 0:N], in0=pb.t[64:96, 0:N], in1=rS.t[64:96, 0:N], op=ALU.mult),
                             reads=[pb.b, rS.b], writes=[t2.b])
                        k.op("dve", lambda e, h=h: e.tensor_tensor(out=qb.t[64:96, h, 0:N], in0=t1.t[64:96, 0:N], in1=t2.t[64:96, 0:N], op=ALU.add),
                             reads=[t1.b, t2.b], writes=[qb.b])
                k.dma("sp", lambda e: e.dma_start(out=dr["kbT_d"][s, :, :, c0:c0 + N].rearrange("h r n -> r h n"), in_=kb.t[:, :, 0:N]),
                      reads=[kb.b], writes=[k.dbuf("kbT", s, bi)])
                if need_q:
                    k.dma("sp", lambda e: e.dma_start(out=dr["qbT_d"][s, :, :, c0:c0 + N].rearrange("h r n -> r h n"), in_=qb.t[:, :, 0:N]),
                          reads=[qb.b], writes=[k.dbuf("qbT", s, bi)])
                for tk in range(N // 128):
                    ps = g.ps[tk % 2]
                    vt = vbs[vi % 2]
                    vi += 1
                    mm(k, ps, ps.t[:, 0:512].rearrange("p (h c) -> p h c", h=8), [(ckvn.t[:, tk * 128:(tk + 1) * 128], Wukv3[:, :, 64:128])], [Wukv.b, ckvn.b])
                    v3 = vt.t[:, :].rearrange("p (q c) -> p q c", c=192)
                    p3 = ps.t[:, 0:512].rearrange("p (q c) -> p q c", c=128)
                    k.op("act", lambda e, v3=v3, p3=p3: e.copy(out=v3[:, :, 0:64], in_=p3[:, :, 0:64]), reads=[ps.b], writes=[vt.b])
                    k.op("act", lambda e, v3=v3, p3=p3: e.copy(out=v3[:, :, 128:192], in_=p3[:, :, 64:128]), reads=[ps.b], writes=[vt.b])
                    j = (c0 + tk * 128) // 128
                    k.dma("sp", lambda e, vt=vt, j=j: e.dma_start(out=dr["vb_d"][s, j, :, :], in_=vt.t[:, :]), reads=[vt.b], writes=[k.dbuf("vb", s, j)])

            vi_[0] = vi

        stage_a(0)
        for bi in range(len(BLOCKS)):
            if bi + 1 < len(BLOCKS):
                stage_a(bi + 1)
            stage_b(bi)
        k.barrier()


def phase_na(nc, k, g, l, s, dr):
    with ExitStack() as ph:
        ring(nc, g, ph, 1, 0)
        kaT = _sb(nc, ph, "kaT", [128, 4, TT], BF16)
        qaT = _sb(nc, ph, "qaT", [128, 4, TT], BF16)
        Ve = _sb(nc, ph, "Ve", [128, 18, 768], BF16)
        Vo = _sb(nc, ph, "Vo", [128, 15, 768], BF16)
        yaT = _sb(nc, ph, "yaT", [128, 4, TT], BF16)
        pts = [_sb(nc, ph, "pt%d" % i, [128, 512], BF16) for i in range(4)]
        recs = [_sb(nc, ph, "rec%d" % i, [128, 256], F32) for i in range(2)]
        g.biasb = _sb(nc, ph, "biasb", [128, 7680], BF16)
        for hf in range(2):
            load_cast(k, g, dr["biasT_rep"][l, :, hf * 3840:(hf + 1) * 3840], (128, 3840), g.biasb.t[:, hf * 3840:(hf + 1) * 3840], g.biasb.b)
        k.dma("sp", lambda e: e.dma_start(out=kaT.t[:, :, :], in_=dr["kaT_d"][s].rearrange("(m p) n -> p m n", p=128)), reads=k.dall("kaT", s), writes=[kaT.b])
        k.dma("sp", lambda e: e.dma_start(out=qaT.t[:, :, :], in_=dr["qaT_d"][s].rearrange("(m p) n -> p m n", p=128)), reads=k.dall("qaT", s), writes=[qaT.b])
        k.dma("sp", lambda e: e.dma_start(out=Ve.t[:, :, :], in_=dr["vae_d"][s].rearrange("j p c -> p j c")), reads=k.dall("vae", s), writes=[Ve.b])
        k.dma("sp", lambda e: e.dma_start(out=Vo.t[:, :, :], in_=dr["vao_d"][s].rearrange("j p c -> p j c")), reads=k.dall("vao", s), writes=[Vo.b])
        sbanks = [g.ps[0], g.ps[1], g.ps[6], g.ps[7]]
        obs = [(g.ps[2], None), (g.ps[3], None), (g.ps[4], None), (g.ps[5], None)]
        it = 0
        jobs = []
        for r in range(32):
            rs = min(max(r - 4, 0), 24)
            for h in range(8):
                pair, hp = h // 2, h % 2
                rows = slice(hp * 64, hp * 64 + 64)
                q_ap = qaT.t[rows, pair, C + 64 * r:C + 64 * r + 64]
                chunks = []
                for c in range(4):
                    tok0 = C + 64 * (rs + 2 * c)
                    if rs % 2 == 0:
                        v = Ve.t[:, 2 + rs // 2 + c, pair * 192 + hp * 64:pair * 192 + hp * 64 + 128]
                        vb_ = Ve.b
                    else:
                        v = Vo.t[:, (rs - 1) // 2 + c, pair * 192 + hp * 64:pair * 192 + hp * 64 + 128]
                        vb_ = Vo.b
                    d0 = rs + 2 * c - r + 7
                    col0 = (h * 15 + d0) * 64
                    chunks.append(dict(kT=kaT.t[rows, pair, tok0:tok0 + 128], v=v, bufs=[kaT.b, vb_],
                                       bias=(g.biasb.t[rows, col0:col0 + 128], g.ident_b.t[rows, hp * 64:hp * 64 + 64], [g.biasb.b, g.ident_b.b])))
                for c in range(2):
                    chunks.append(dict(kT=kaT.t[rows, pair, 128 * c:128 * c + 128], v=Ve.t[:, c, pair * 192 + hp * 64:pair * 192 + hp * 64 + 128], bufs=[kaT.b, Ve.b], bias=None))
                ob, db = obs[it % 4]
                jobs.append(attention(k, g, 64, q_ap, [qaT.b], chunks, 1.0, hp, yaT.t[rows, pair, C + 64 * r:C + 64 * r + 64], yaT.b,
                                      sbanks, ob, db, pts, recs[it % 2]))
                it += 1
        if l < L - 1:
            for h in range(8):
                pair, hp = h // 2, h % 2
                rows = slice(hp * 64, hp * 64 + 64)
                chunks = [dict(kT=kaT.t[rows, pair, 128 * c:128 * c + 128], v=Ve.t[:, c, pair * 192 + hp * 64:pair * 192 + hp * 64 + 128], bufs=[kaT.b, Ve.b], bias=None) for c in range(2)]
                ob, db = obs[it % 4]
                jobs.append(attention(k, g, 256, qaT.t[rows, pair, 0:256], [qaT.b], chunks, 1.0, hp, yaT.t[rows, pair, 0:256], yaT.b,
                                      sbanks, ob, db, pts, recs[it % 2]))
                it += 1
        run_attention(jobs)
        c0 = 0 if l < L - 1 else C
        k.dma("sp", lambda e: e.dma_start(out=dr["yaT_d"][s, :, c0:TT].rearrange("(m p) n -> p m n", p=128), in_=yaT.t[:, :, c0:TT]),
              reads=[yaT.b], writes=[k.dbuf("yaT", s)])
        k.barrier()


def phase_mla(nc, k, g, l, s, dr):
    with ExitStack() as ph:
        kbT = _sb(nc, ph, "kbT", [96, 8, TT], BF16)
        vb = _sb(nc, ph, "vb", [128, 18, 768], BF16)
        qbs = [_sb(nc, ph, "qb%d" % i, [96, 8, 512], BF16) for i in range(2)]
        ybs = [_sb(nc, ph, "yb%d" % i, [128, 4, 512], BF16) for i in range(2)]
        pts = [_sb(nc, ph, "pt%d" % i, [128, 512], BF16) for i in range(4)]
        recs = [_sb(nc, ph, "rec%d" % i, [128, 512], F32) for i in range(2)]
        cv = conv_alloc(nc, ph)
        taps = conv_taps(k, g, l, s, dr, cv, C, T)
        tap_i = [0]
        calls = [0]

        def emit_taps(n):
            for _ in range(n):
                if tap_i[0] < len(taps):
                    taps[tap_i[0]]()
                    tap_i[0] += 1

        def side():
            calls[0] += 1
            if calls[0] % 4 == 0:
                emit_taps(1)

        emit_taps(4)
        k.dma("sp", lambda e: e.dma_start(out=kbT.t[:, :, :], in_=dr["kbT_d"][s].rearrange("h r n -> r h n")), reads=k.dall("kbT", s), writes=[kbT.b])
        k.dma("sp", lambda e: e.dma_start(out=vb.t[:, :, :], in_=dr["vb_d"][s].rearrange("j p c -> p j c")), reads=k.dall("vb", s), writes=[vb.b])
        sbanks = [g.ps[0], g.ps[1], g.ps[6], g.ps[7]]
        obs = [(g.ps[2], None), (g.ps[3], None), (g.ps[4], None), (g.ps[5], None)]
        scale = float(96 ** -0.5)
        it = 0
        for bi, (c0, N) in enumerate(BLOCKS):
            if bi == 0 and l == L - 1:
                continue
            qb, yb = qbs[bi % 2], ybs[bi % 2]
            k.dma("sp", lambda e: e.dma_start(out=qb.t[:, :, 0:N], in_=dr["qbT_d"][s, :, :, c0:c0 + N].rearrange("h r n -> r h n")),
                  reads=[k.dbuf("qbT", s, bi)], writes=[qb.b])
            nk = 2 if bi == 0 else 18
            jobs = []
            for h in range(8):
                pair, hp = h // 2, h % 2
                rows = slice(hp * 64, hp * 64 + 64)
                chunks = [dict(kT=kbT.t[0:96, h, 128 * j:128 * j + 128], v=vb.t[:, j, pair * 192 + hp * 64:pair * 192 + hp * 64 + 128], bufs=[kbT.b, vb.b], bias=None) for j in range(nk)]
                ob, db = obs[it % 4]
                jobs.append(attention(k, g, N, qb.t[0:96, h, 0:N], [qb.b], chunks, scale, hp, yb.t[rows, pair, 0:N], yb.b, sbanks, ob, db, pts, recs[it % 2]))
                it += 1
            run_attention(jobs, side)
            k.dma("sp", lambda e: e.dma_start(out=dr["ybT_d"][s, :, c0:c0 + N].rearrange("(m p) n -> p m n", p=128), in_=yb.t[:, :, 0:N]),
                  reads=[yb.b], writes=[k.dbuf("ybT", s, bi)])
        emit_taps(len(taps))
        conv_tail(k, g, l, s, dr, cv, C, T)
        if l < L - 1:
            for op_ in conv_taps(k, g, l, s, dr, cv, 0, C):
                op_()
            conv_tail(k, g, l, s, dr, cv, 0, C)
        k.barrier()


class ConvState:
    pass


def conv_alloc(nc, ph):
    cv = ConvState()
    cv.uT = _sb(nc, ph, "uT", [128, 4, T + 30], F32)
    cv.acc = _sb(nc, ph, "acc", [128, 4, T], F32)
    cv.sq = _sb(nc, ph, "sq", [128, 4, 512], BF16)
    cv.z16 = _sb(nc, ph, "z16", [128, 4, 512], BF16)
    cv.mean = _sb(nc, ph, "mean", [128, 512], F32)
    cv.rstd = _sb(nc, ph, "rstd", [128, 512], F32)
    cv.tmp = _sb(nc, ph, "tmp", [128, 512], F32)
    cv.ucs = [_sb(nc, ph, "ucs%d" % i, [128, 4, 512], BF16) for i in range(2)]
    cv.it = 0
    return cv


def conv_taps(k, g, l, s, dr, cv, c0, Ls):
    uT, acc = cv.uT, cv.acc
    ops = []
    ops.append(lambda: k.op("pool", lambda e: e.memset(uT.t[:, :, 0:15], 0.0), writes=[uT.b]))
    ops.append(lambda: k.op("pool", lambda e: e.memset(uT.t[:, :, 15 + Ls:30 + Ls], 0.0), writes=[uT.b]))
    ops.append(lambda: k.dma("sp", lambda e: e.dma_start(out=uT.t[:, :, 15:15 + Ls], in_=dr["u_d"][s, :, c0:c0 + Ls].rearrange("(m p) n -> p m n", p=128)),
                             reads=k.dall("u", s), writes=[uT.b]))
    for ch in range(4):
        wcol = V_WDW + ch * 31
        ops.append(lambda ch=ch, wcol=wcol: k.op("dve", lambda e: e.tensor_scalar(
            out=acc.t[:, ch, 0:Ls], in0=uT.t[:, ch, 0:Ls], scalar1=g.vecs.t[:, l, wcol:wcol + 1],
            scalar2=g.vecs.t[:, l, V_BDW + ch:V_BDW + ch + 1], op0=ALU.mult, op1=ALU.add), reads=[uT.b, g.vecs.b], writes=[acc.b]))
        for kk in range(1, 31):
            ops.append(lambda ch=ch, wcol=wcol, kk=kk: k.op("dve", lambda e: e.scalar_tensor_tensor(
                out=acc.t[:, ch, 0:Ls], in0=uT.t[:, ch, kk:kk + Ls], scalar=g.vecs.t[:, l, wcol + kk:wcol + kk + 1], in1=acc.t[:, ch, 0:Ls],
                op0=ALU.mult, op1=ALU.add), reads=[uT.b, g.vecs.b, acc.b], writes=[acc.b]))
    return ops


def conv_tail(k, g, l, s, dr, cv, c0, Ls):
    acc, sq, z16, mean, rstd, tmp = cv.acc, cv.sq, cv.z16, cv.mean, cv.rstd, cv.tmp
    for b0 in range(0, Ls, 512):
        N = min(512, Ls - b0)
        col_stats(k, g, [acc.t[:, ch, b0:b0 + N] for ch in range(4)], acc.b, N, 512, LN_EPS, True, sq, mean, rstd, z16)
        uc = cv.ucs[cv.it % 2]
        cv.it += 1
        for ch in range(4):
            k.op("dve", lambda e, ch=ch: e.tensor_tensor(out=tmp.t[:, 0:N], in0=acc.t[:, ch, b0:b0 + N], in1=mean.t[:, 0:N], op=ALU.subtract),
                 reads=[acc.b, mean.b], writes=[tmp.b])
            k.op("dve", lambda e, ch=ch: e.scalar_tensor_tensor(out=tmp.t[:, 0:N], in0=tmp.t[:, 0:N], scalar=g.vecs.t[:, l, V_GCN + ch:V_GCN + ch + 1],
                                                                 in1=rstd.t[:, 0:N], op0=ALU.mult, op1=ALU.mult), reads=[tmp.b, rstd.b, g.vecs.b], writes=[tmp.b])
            k.op("act", lambda e, ch=ch, uc=uc: e.activation(out=uc.t[:, ch, 0:N], in_=tmp.t[:, 0:N], func=AF.Silu,
                                                             bias=g.vecs.t[:, l, V_BCN + ch:V_BCN + ch + 1], scale=1.0), reads=[tmp.b, g.vecs.b], writes=[uc.b])
        k.dma("sp", lambda e, uc=uc, b0=b0, N=N: e.dma_start(out=dr["ucT_d"][s, :, c0 + b0:c0 + b0 + N].rearrange("(m p) n -> p m n", p=128), in_=uc.t[:, :, 0:N]),
              reads=[uc.b], writes=[k.dbuf("ucT", s, c0 + b0)])


MBLK = [(c0, 256) for c0 in range(0, TT, 256)]


def phase_merge(nc, k, g, l, s, dr):
    X = dr["xT_in"] if l == 0 else dr["xres"]
    NM = 512
    with ExitStack() as ph:
        ring(nc, g, ph, 2, 0)
        Woa = _sb(nc, ph, "Woa", [128, 4, 1024], BF16)
        Wob = _sb(nc, ph, "Wob", [128, 4, 1024], BF16)
        Wpw = _sb(nc, ph, "Wpw", [128, 4, 1024], BF16)
        Wout = _sb(nc, ph, "Wout", [128, 8, 1024], BF16)
        Wrf = _sb(nc, ph, "Wrf", [128, 8, 16], F32)
        Wrh = _sb(nc, ph, "Wrh", [128, 8, 16], BF16)
        Wrl = _sb(nc, ph, "Wrl", [128, 8, 16], BF16)
        for W, src in ((Woa, dr["w_oa"]), (Wob, dr["w_ob"]), (Wpw, dr["w_pw2"])):
            load_cast(k, g, src[l].rearrange("(kc p) n -> p kc n", p=128), (128, 4, 1024), W.t[:, :, :], W.b)
        for hf in range(2):
            load_cast(k, g, dr["w_out"][l, :, hf * 512:(hf + 1) * 512].rearrange("(kc p) n -> p kc n", p=128), (128, 8, 512),
                      Wout.t[:, :, hf * 512:(hf + 1) * 512], Wout.b)
        k.dma("sp", lambda e: e.dma_start(out=Wrf.t[:, :, :], in_=dr["w_router"][l].rearrange("(kc p) n -> p kc n", p=128)), writes=[Wrf.b])
        k.op("dve", lambda e: e.tensor_copy(out=Wrh.t[:, :, :], in_=Wrf.t[:, :, :]), reads=[Wrf.b], writes=[Wrh.b])
        k.op("dve", lambda e: e.tensor_tensor(out=Wrl.t[:, :, :], in0=Wrf.t[:, :, :], in1=Wrh.t[:, :, :], op=ALU.subtract), reads=[Wrf.b, Wrh.b], writes=[Wrl.b])
        yas = [_sb(nc, ph, "ya%d" % i, [128, 4, NM], BF16) for i in range(2)]
        ybs_ = [_sb(nc, ph, "yb%d" % i, [128, 4, NM], BF16) for i in range(2)]
        ucs_ = [_sb(nc, ph, "uc%d" % i, [128, 4, NM], BF16) for i in range(2)]
        gts = [_sb(nc, ph, "gt%d" % i, [128, 3, NM], BF16) for i in range(3)]
        ms_ = [_sb(nc, ph, "m%d" % i, [128, 8, NM], BF16) for i in range(2)]
        mfs = [_sb(nc, ph, "mf%d" % i, [128, NM], F32) for i in range(2)]
        t2s = [_sb(nc, ph, "t2_%d" % i, [128, NM], F32) for i in range(2)]
        tts = [_sb(nc, ph, "tt%d" % i, [128, NM], F32) for i in range(2)]
        z = _sb(nc, ph, "z", [128, 8, NM], F32)
        sq = _sb(nc, ph, "sq", [128, 8, NM], BF16)
        z16 = _sb(nc, ph, "z16", [128, 8, NM], BF16)
        mean = _sb(nc, ph, "mean", [128, NM], F32)
        rstd = _sb(nc, ph, "rstd", [128, NM], F32)
        h2fs = [_sb(nc, ph, "h2f%d" % i, [128, NM], F32) for i in range(2)]
        h2b = _sb(nc, ph, "h2b", [128, 8, NM], BF16)
        h2l = _sb(nc, ph, "h2l", [128, 8, NM], BF16)
        h2s = [_sb(nc, ph, "h2s%d" % i, [128, 1024], BF16) for i in range(2)]
        Et = _sb(nc, ph, "Et", [16, NM], F32)
        afs = [_sb(nc, ph, "afs%d" % i, [128, 16], F32) for i in range(4)]
        ssum = _sb(nc, ph, "ssum", [128, 4], F32)
        affs = _sb(nc, ph, "affs", [16, NM], F32)
        cnt = [0]
        gti = [0]
        tti = [0]
        hi_ = [0]

        def nextps():
            p = g.ps[cnt[0] % 4]
            cnt[0] += 1
            return p

        blocks = [(bi, c0, N) for bi, (c0, N) in enumerate(BLOCKS) if not (bi == 0 and l == L - 1)]

        def stage1(ix):
            bi, c0, N = blocks[ix]
            ya, yb, uc, m = yas[ix % 2], ybs_[ix % 2], ucs_[ix % 2], ms_[ix % 2]
            for T_, nm in ((ya, "yaT"), (yb, "ybT"), (uc, "ucT")):
                k.dma("sp", lambda e, T_=T_, nm=nm: e.dma_start(out=T_.t[:, :, 0:N], in_=dr[nm + "_d"][s, :, c0:c0 + N].rearrange("(m p) n -> p m n", p=128)),
                      reads=k.dall(nm, s), writes=[T_.b])
            for oc in range(8):
                osl = slice(oc * 128, (oc + 1) * 128)
                gt = gts[gti[0] % 3]
                gti[0] += 1
                k.dma("sp", lambda e, gt=gt, oc=oc: e.dma_start(
                    out=gt.t[:, :, 0:N], in_=dr["gT_d"][s, :, c0:c0 + N].rearrange("(b m p) n -> b m p n", b=3, p=128)[:, oc].rearrange("b p n -> p b n")),
                    reads=k.dall("gT", s), writes=[gt.b])
                pa = nextps()
                mm(k, pa, pa.t[:, 0:N], [(Woa.t[:, kc, osl], ya.t[:, kc, 0:N]) for kc in range(4)], [Woa.b, ya.b])
                pb = nextps()
                mm(k, pb, pb.t[:, 0:N], [(Wob.t[:, kc, osl], yb.t[:, kc, 0:N]) for kc in range(4)], [Wob.b, yb.b])
                pc = nextps()
                mm(k, pc, pc.t[:, 0:N], [(Wpw.t[:, kc, osl], uc.t[:, kc, 0:N]) for kc in range(4)], [Wpw.b, uc.b])
                mf, t2 = mfs[oc % 2], t2s[oc % 2]
                t1 = tts[tti[0] % 2]
                tti[0] += 1
                k.op("dve", lambda e, pa=pa, gt=gt, mf=mf: e.tensor_tensor(out=mf.t[:, 0:N], in0=pa.t[:, 0:N], in1=gt.t[:, 0, 0:N], op=ALU.mult), reads=[pa.b, gt.b], writes=[mf.b])
                k.op("dve", lambda e, pb=pb, gt=gt, t1=t1: e.tensor_tensor(out=t1.t[:, 0:N], in0=pb.t[:, 0:N], in1=gt.t[:, 1, 0:N], op=ALU.mult), reads=[pb.b, gt.b], writes=[t1.b])
                k.op("dve", lambda e, mf=mf, t1=t1: e.tensor_tensor(out=mf.t[:, 0:N], in0=mf.t[:, 0:N], in1=t1.t[:, 0:N], op=ALU.add), reads=[mf.b, t1.b], writes=[mf.b])
                k.op("dve", lambda e, pc=pc, gt=gt, t2=t2: e.tensor_tensor(out=t2.t[:, 0:N], in0=pc.t[:, 0:N], in1=gt.t[:, 2, 0:N], op=ALU.mult), reads=[pc.b, gt.b], writes=[t2.b])
                k.op("pool", lambda e, oc=oc, m=m, mf=mf, t2=t2: e.tensor_tensor(out=m.t[:, oc, 0:N], in0=mf.t[:, 0:N], in1=t2.t[:, 0:N], op=ALU.add), reads=[mf.b, t2.b], writes=[m.b])

        def stage2(ix):
            bi, c0, N = blocks[ix]
            m = ms_[ix % 2]
            j = 2 if bi == 0 else s
            k.dma("sp", lambda e: e.dma_start(out=z.t[:, :, 0:N], in_=X[s, :, c0:c0 + N].rearrange("(kc p) n -> p kc n", p=128)),
                  reads=k.dall("x", s), writes=[z.b])
            for oc in range(8):
                osl = slice(oc * 128, (oc + 1) * 128)
                py = nextps()
                mm(k, py, py.t[:, 0:N], [(Wout.t[:, kc, osl], m.t[:, kc, 0:N]) for kc in range(8)], [Wout.b, m.b])
                t1 = tts[tti[0] % 2]
                tti[0] += 1
                k.op("act", lambda e, py=py, t1=t1, oc=oc: e.activation(out=t1.t[:, 0:N], in_=py.t[:, 0:N], func=AF.Copy, scale=g.mod.t[:, l, 16 + oc, j:j + 1]),
                     reads=[py.b, g.mod.b], writes=[t1.b])
                k.op("dve", lambda e, t1=t1, oc=oc: e.scalar_tensor_tensor(out=z.t[:, oc, 0:N], in0=z.t[:, oc, 0:N], scalar=ALPHA, in1=t1.t[:, 0:N], op0=ALU.mult, op1=ALU.add),
                     reads=[z.b, t1.b], writes=[z.b])

        def stage3(ix):
            bi, c0, N = blocks[ix]
            j = 2 if bi == 0 else s
            col_stats(k, g, [z.t[:, oc, 0:N] for oc in range(8)], z.b, N, 1024, LN_EPS, True, sq, mean, rstd, z16)
            for oc in range(8):
                t1 = tts[tti[0] % 2]
                tti[0] += 1
                k.op("dve", lambda e, t1=t1, oc=oc: e.tensor_tensor(out=t1.t[:, 0:N], in0=z.t[:, oc, 0:N], in1=mean.t[:, 0:N], op=ALU.subtract), reads=[z.b, mean.b], writes=[t1.b])
                k.op("dve", lambda e, t1=t1, oc=oc: e.scalar_tensor_tensor(out=t1.t[:, 0:N], in0=t1.t[:, 0:N], scalar=g.vecs.t[:, l, V_LN1G + oc:V_LN1G + oc + 1], in1=rstd.t[:, 0:N],
                                                                          op0=ALU.mult, op1=ALU.mult), reads=[t1.b, rstd.b, g.vecs.b], writes=[t1.b])
                k.op("act", lambda e, t1=t1, oc=oc: e.activation(out=z.t[:, oc, 0:N], in_=t1.t[:, 0:N], func=AF.Identity, bias=g.vecs.t[:, l, V_LN1B + oc:V_LN1B + oc + 1], scale=1.0),
                     reads=[t1.b, g.vecs.b], writes=[z.b])
                h2f = h2fs[oc % 2]
                k.op("act", lambda e, oc=oc, h2f=h2f: e.activation(out=h2f.t[:, 0:N], in_=z.t[:, oc, 0:N], func=AF.Identity, scale=g.mod.t[:, l, 32 + oc, j:j + 1],
                                                                   bias=g.mod.t[:, l, 24 + oc, j:j + 1]), reads=[z.b, g.mod.b], writes=[h2f.b])
                k.op("act", lambda e, oc=oc, h2f=h2f: e.copy(out=h2b.t[:, oc, 0:N], in_=h2f.t[:, 0:N]), reads=[h2f.b], writes=[h2b.b])
                k.op("pool", lambda e, oc=oc, h2f=h2f: e.tensor_tensor(out=h2l.t[:, oc, 0:N], in0=h2f.t[:, 0:N], in1=h2b.t[:, oc, 0:N], op=ALU.subtract),
                     reads=[h2f.b, h2b.b], writes=[h2l.b])
            k.dma("sp", lambda e: e.dma_start(out=dr["x1res"][s, :, c0:c0 + N].rearrange("(kc p) n -> p kc n", p=128), in_=z.t[:, :, 0:N]),
                  reads=[z.b], writes=[k.dbuf("x1", s, bi)])

        def stage4(ix):
            bi, c0, N = blocks[ix]
            ntk = N // 128
            pr = g.ps[4]
            pairs = []
            for kc in range(8):
                pairs += [(Wrh.t[:, kc, :], h2b.t[:, kc, 0:N]), (Wrl.t[:, kc, :], h2b.t[:, kc, 0:N]), (Wrh.t[:, kc, :], h2l.t[:, kc, 0:N])]
            mm(k, pr, pr.t[0:16, 0:N], pairs, [Wrh.b, Wrl.b, h2b.b, h2l.b])
            k.op("act", lambda e: e.activation(out=Et.t[:, 0:N], in_=pr.t[0:16, 0:N], func=AF.Exp), reads=[pr.b], writes=[Et.b])
            pq = g.ps[5]
            for tk in range(ntk):
                af = afs[tk]
                k.op("pe", lambda e, tk=tk: e.transpose(pq.t[:, tk * 16:tk * 16 + 16], Et.t[0:16, tk * 128:(tk + 1) * 128], g.ident_f.t[0:16, 0:16]),
                     reads=[Et.b, g.ident_f.b], writes=[pq.b])
                k.op("dve", lambda e, tk=tk: e.tensor_reduce(out=ssum.t[:, tk:tk + 1], in_=pq.t[:, tk * 16:tk * 16 + 16], op=ALU.add, axis=mybir.AxisListType.X),
                     reads=[pq.b], writes=[ssum.b])
                k.op("dve", lambda e, tk=tk: e.reciprocal(out=ssum.t[:, tk:tk + 1], in_=ssum.t[:, tk:tk + 1]), reads=[ssum.b], writes=[ssum.b])
                k.op("dve", lambda e, tk=tk, af=af: e.tensor_scalar(out=af.t[:, :], in0=pq.t[:, tk * 16:tk * 16 + 16], scalar1=ssum.t[:, tk:tk + 1], scalar2=None, op0=ALU.mult),
                     reads=[pq.b, ssum.b], writes=[af.b])
                r0 = s * TT + c0 + tk * 128
                k.dma("sp", lambda e, af=af, r0=r0: e.dma_start(out=dr["afftok_d"][r0:r0 + 128, :], in_=af.t[:, :]), reads=[af.b], writes=[k.dbuf("afftok", s, bi, tk)])
                k.op("pe", lambda e, tk=tk, af=af: e.transpose(pr.t[0:16, tk * 128:(tk + 1) * 128], af.t[:, :], g.ident_f.t[:, :]),
                     reads=[af.b, g.ident_f.b], writes=[pr.b])
            k.op("dve", lambda e: e.tensor_copy(out=affs.t[:, 0:N], in_=pr.t[0:16, 0:N]), reads=[pr.b], writes=[affs.b])
            k.dma("sp", lambda e: e.dma_start(out=dr["affT_d"][s, :, c0:c0 + N], in_=affs.t[:, 0:N]), reads=[affs.b], writes=[k.dbuf("affT", s, bi)])
            for tk in range(ntk):
                pt_ = g.ps[4 + (tk % 2)]
                ptb = pt_.t[:, :].bitcast(BF16)
                hs = h2s[hi_[0] % 2]
                hi_[0] += 1
                for kc in range(8):
                    k.op("pe", lambda e, kc=kc, tk=tk, ptb=ptb: e.transpose(ptb[:, kc * 128:(kc + 1) * 128], h2b.t[:, kc, tk * 128:(tk + 1) * 128], g.ident_b.t[:, :]),
                         reads=[h2b.b, g.ident_b.b], writes=[pt_.b])
                k.op("act", lambda e, ptb=ptb, hs=hs: e.copy(out=hs.t[:, :], in_=ptb[:, 0:1024]), reads=[pt_.b], writes=[hs.b])
                r0 = s * TT + c0 + tk * 128
                k.dma("sp", lambda e, hs=hs, r0=r0: e.dma_start(out=dr["h2tok_d"][r0:r0 + 128, :], in_=hs.t[:, :]), reads=[hs.b], writes=[k.dbuf("h2tok", s, bi, tk)])

        stage1(0)
        for ix in range(len(blocks)):
            stage2(ix)
            if ix + 1 < len(blocks):
                stage1(ix + 1)
            stage3(ix)
            stage4(ix)
        k.barrier()


def phase_moe(nc, k, g, l, dr):
    with_ctx = l < L - 1
    nch = 5 if with_ctx else 4
    moe_d, h2tok_d, afftok_d = dr["moe_d"], dr["h2tok_d"], dr["afftok_d"]
    with ExitStack() as ph:
        ring(nc, g, ph, 3, 4, cast_engs=("dve", "act", "dve"))
        gl = _sb(nc, ph, "gidx_l", [128, 2, 32], I32)
        gc = _sb(nc, ph, "gidx_c", [64, 16], I32)
        with ExitStack() as ph2:
            zt = _sb(nc, ph2, "zt", [128, 2048], F32)
            k.op("pool", lambda e: e.memset(zt.t[:, :], 0.0), writes=[zt.b])
            moe_v = moe_d.rearrange("(i p a) d -> i p (a d)", p=128, a=2)
            for i in range(18):
                k.dma("sp", lambda e, i=i: e.dma_start(out=moe_v[i], in_=zt.t[:, :]), reads=[zt.b], writes=[k.dbuf("moez", i)])
            affT = _sb(nc, ph2, "affT", [32, TT], F32)
            for s in range(NS):
                k.dma("sp", lambda e, s=s: e.dma_start(out=affT.t[16 * s:16 * s + 16, :], in_=dr["affT_d"][s, :, :]), reads=k.dall("affT", s), writes=[affT.b])
            wk = _sb(nc, ph2, "wk", [32, T], F32)
            mx = _sb(nc, ph2, "mx", [32, 8], F32)
            idxu = _sb(nc, ph2, "idxu", [32, 256], U32)
            idxf = _sb(nc, ph2, "idxf", [32, 256], F32)
            offl = _sb(nc, ph2, "offl", [32, 1], F32)
            offc = _sb(nc, ph2, "offc", [32, 1], F32)
            k.op("dve", lambda e: e.memset(offl.t[:, :], float(TT + C)), writes=[offl.b])
            k.op("dve", lambda e: e.memset(offl.t[0:16, :], float(C)), writes=[offl.b])
            k.op("dve", lambda e: e.memset(offc.t[:, :], float(TT)), writes=[offc.b])
            k.op("dve", lambda e: e.memset(offc.t[0:16, :], 0.0), writes=[offc.b])
            pst = g.ps[7]

            def topk(n_tok, n_it):
                for it in range(n_it):
                    k.op("dve", lambda e: e.max(out=mx.t[:, :], in_=wk.t[:, 0:n_tok]), reads=[wk.b], writes=[mx.b])
                    k.op("dve", lambda e, it=it: e.max_index(out=idxu.t[:, it * 8:(it + 1) * 8], in_max=mx.t[:, :], in_values=wk.t[:, 0:n_tok]),
                         reads=[wk.b, mx.b], writes=[idxu.b])
                    k.op("dve", lambda e: e.match_replace(out=wk.t[:, 0:n_tok], in_to_replace=mx.t[:, :], in_values=wk.t[:, 0:n_tok], imm_value=-1.0),
                         reads=[wk.b, mx.b], writes=[wk.b])

            k.op("dve", lambda e: e.tensor_copy(out=wk.t[:, 0:T], in_=affT.t[:, C:TT]), reads=[affT.b], writes=[wk.b])
            topk(T, 32)
            k.op("dve", lambda e: e.tensor_scalar(out=idxf.t[:, :], in0=idxu.t[:, :], scalar1=offl.t[:, 0:1], scalar2=None, op0=ALU.add),
                 reads=[idxu.b, offl.b], writes=[idxf.b])
            for ch in range(2):
                k.op("pe", lambda e, ch=ch: e.transpose(pst.t[:, ch * 32:(ch + 1) * 32], idxf.t[0:32, ch * 128:(ch + 1) * 128], g.ident_f.t[0:32, 0:32]),
                     reads=[idxf.b, g.ident_f.b], writes=[pst.b])
                k.op("dve", lambda e, ch=ch: e.tensor_copy(out=gl.t[:, ch, :], in_=pst.t[:, ch * 32:(ch + 1) * 32]), reads=[pst.b], writes=[gl.b])
            if with_ctx:
                k.op("dve", lambda e: e.tensor_copy(out=wk.t[:, 0:C], in_=affT.t[:, 0:C]), reads=[affT.b], writes=[wk.b])
                topk(C, 4)
                k.op("dve", lambda e: e.tensor_scalar(out=idxf.t[:, 0:32], in0=idxu.t[:, 0:32], scalar1=offc.t[:, 0:1], scalar2=None, op0=ALU.add),
                     reads=[idxu.b, offc.b], writes=[idxf.b])
                k.op("pe", lambda e: e.transpose(pst.t[0:32, 64:96], idxf.t[0:32, 0:32], g.ident_f.t[0:32, 0:32]), reads=[idxf.b, g.ident_f.b], writes=[pst.b])
                k.op("dve", lambda e: e.tensor_copy(out=gc.t[0:32, :], in_=pst.t[0:32, 64:80]), reads=[pst.b], writes=[gc.b])
                k.op("dve", lambda e: e.tensor_copy(out=gc.t[32:64, :], in_=pst.t[0:32, 80:96]), reads=[pst.b], writes=[gc.b])
            k.barrier()
        units = []
        for ex in range(NE):
            for W, c0 in ((dr["w_gate"], 0), (dr["w_up"], 0), (dr["w_gate"], 512), (dr["w_up"], 512), (dr["w_down"], 0), (dr["w_down"], 512)):
                units.append((W[l, ex, :, c0:c0 + 512].rearrange("(kc p) n -> p kc n", p=128), (128, 8, 512)))
        ws = WStream(k, g, units, pf=2)
        xg = [[_sb(nc, ph, "xg%d_%d" % (b, i), [128, 1024], BF16) for i in range(nch)] for b in range(2)]
        ag = [[_sb(nc, ph, "ag%d_%d" % (b, i), [128, 16], F32) for i in range(nch)] for b in range(2)]
        xgT = _sb(nc, ph, "xgT", [128, 8, 576], BF16)
        hid = _sb(nc, ph, "hid", [128, 8, 576], BF16)
        sgs = [_sb(nc, ph, "sg%d" % i, [128, 576], F32) for i in range(2)]
        ysb = [_sb(nc, ph, "ysb%d" % i, [128, 1024], F32) for i in range(nch)]
        h2reads = k.dall("h2tok")
        afreads = k.dall("afftok")
        zreads = k.dall("moez")
        ti = [0]

        def idx_ap(ch, ex):
            if ch < 4:
                s_, c_ = ch // 2, ch % 2
                return gl.t[:, c_, s_ * 16 + ex:s_ * 16 + ex + 1], gl.b, 128
            return gc.t[0:64, ex:ex + 1], gc.b, 64

        def gathers(ex):
            b = ex % 2
            for ch in range(nch):
                ia, ib, P = idx_ap(ch, ex)
                k.dma("pool", lambda e, ch=ch, P=P, ia=ia: e.indirect_dma_start(
                    out=xg[b][ch].t[0:P, :], out_offset=None, in_=h2tok_d[:, :], in_offset=bass.IndirectOffsetOnAxis(ap=ia, axis=0)),
                    reads=h2reads + [ib], writes=[xg[b][ch].b])
                k.dma("pool", lambda e, ch=ch, P=P, ia=ia: e.indirect_dma_start(
                    out=ag[b][ch].t[0:P, :], out_offset=None, in_=afftok_d[:, :], in_offset=bass.IndirectOffsetOnAxis(ap=ia, axis=0)),
                    reads=afreads + [ib], writes=[ag[b][ch].b])

        def transposes(ex):
            b = ex % 2
            for ch in range(nch):
                P = 128 if ch < 4 else 64
                pt_ = g.ps[6 + (ti[0] % 2)]
                ti[0] += 1
                ptb = pt_.t[:, :].bitcast(BF16)
                for kc in range(8):
                    k.op("pe", lambda e, kc=kc, ch=ch, P=P, ptb=ptb: e.transpose(ptb[:, kc * 128:kc * 128 + P], xg[b][ch].t[0:P, kc * 128:(kc + 1) * 128], g.ident_b.t[0:P, 0:P]),
                         reads=[xg[b][ch].b, g.ident_b.b], writes=[pt_.b])
                k.op("act", lambda e, ch=ch, P=P, ptb=ptb: e.copy(out=xgT.t[:, :, ch * 128:ch * 128 + P], in_=ptb[:, 0:1024].rearrange("p (a b) -> p a b", a=8)[:, :, 0:P]),
                     reads=[pt_.b], writes=[xgT.b])

        gathers(0)
        transposes(0)
        for ex in range(NE):
            b = ex % 2
            if ex + 1 < NE:
                gathers(ex + 1)
            for half in range(2):
                wg, wgb = ws.get(ex * 6 + 2 * half)
                wu, wub = ws.get(ex * 6 + 2 * half + 1)
                for jj in range(4):
                    j = half * 4 + jj
                    b0 = 3 * (j % 2)
                    pg, pu, pc = g.ps[b0], g.ps[b0 + 1], g.ps[b0 + 2]
                    csl = slice(jj * 128, (jj + 1) * 128)
                    mm(k, pg, pg.t[:, 0:512], [(wg[:, kc, csl], xgT.t[:, kc, 0:512]) for kc in range(8)], [wgb, xgT.b])
                    mm(k, pu, pu.t[:, 0:512], [(wu[:, kc, csl], xgT.t[:, kc, 0:512]) for kc in range(8)], [wub, xgT.b])
                    sg = sgs[j % 2]
                    k.op("act", lambda e, pg=pg, sg=sg: e.activation(out=sg.t[:, 0:512], in_=pg.t[:, 0:512], func=AF.Silu), reads=[pg.b], writes=[sg.b])
                    if with_ctx:
                        mm(k, pc, pc.t[:, 0:64], [(wg[:, kc, csl], xgT.t[:, kc, 512:576]) for kc in range(8)], [wgb, xgT.b])
                        mm(k, pc, pc.t[:, 64:128], [(wu[:, kc, csl], xgT.t[:, kc, 512:576]) for kc in range(8)], [wub, xgT.b])
                        k.op("act", lambda e, pc=pc, sg=sg: e.activation(out=sg.t[:, 512:576], in_=pc.t[:, 0:64], func=AF.Silu), reads=[pc.b], writes=[sg.b])
                    k.op("dve", lambda e, pu=pu, sg=sg, j=j: e.tensor_tensor(out=hid.t[:, j, 0:512], in0=pu.t[:, 0:512], in1=sg.t[:, 0:512], op=ALU.mult),
                         reads=[pu.b, sg.b], writes=[hid.b])
                    if with_ctx:
                        k.op("dve", lambda e, pc=pc, sg=sg, j=j: e.tensor_tensor(out=hid.t[:, j, 512:576], in0=pc.t[:, 64:128], in1=sg.t[:, 512:576], op=ALU.mult),
                             reads=[pc.b, sg.b], writes=[hid.b])
            if ex + 1 < NE:
                transposes(ex + 1)
            wdl, wdlb = ws.get(ex * 6 + 4)
            wdh, wdhb = ws.get(ex * 6 + 5)
            prev = k.dall("moeacc", ex - 1) if ex > 0 else []
            for ch in range(nch):
                ia, ib, P = idx_ap(ch, ex)
                for oh, (wd, wdb) in enumerate(((wdl, wdlb), (wdh, wdhb))):
                    py = g.ps[6 + (ti[0] % 2)]
                    ti[0] += 1
                    mm(k, py, py.t[0:P, 0:512], [(hid.t[:, kc, ch * 128:ch * 128 + P], wd[:, kc, 0:512]) for kc in range(8)], [hid.b, wdb])
                    k.op("act", lambda e, py=py, ch=ch, P=P, oh=oh, ex=ex, b=b: e.activation(out=ysb[ch].t[0:P, oh * 512:(oh + 1) * 512], in_=py.t[0:P, 0:512], func=AF.Copy,
                                                                                           scale=ag[b][ch].t[0:P, ex:ex + 1]), reads=[py.b, ag[b][ch].b], writes=[ysb[ch].b])
                k.dma("pool", lambda e, ch=ch, P=P, ia=ia: e.indirect_dma_start(
                    out=moe_d[:, :], out_offset=bass.IndirectOffsetOnAxis(ap=ia, axis=0), in_=ysb[ch].t[0:P, :], in_offset=None, compute_op=ALU.add),
                    reads=[ysb[ch].b, ib] + zreads + prev, writes=[k.dbuf("moeacc", ex, ch)])
        k.barrier()


def phase_ln2(nc, k, g, l, s, dr):
    last = (l == L - 1)
    with ExitStack() as ph:
        zs = [_sb(nc, ph, "z%d" % i, [128, 8, 512], F32) for i in range(2)]
        sq = _sb(nc, ph, "sq", [128, 8, 512], BF16)
        z16 = _sb(nc, ph, "z16", [128, 8, 512], BF16)
        mean = _sb(nc, ph, "mean", [128, 512], F32)
        rstd = _sb(nc, ph, "rstd", [128, 512], F32)
        tts = [_sb(nc, ph, "tt%d" % i, [128, 512], F32) for i in range(2)]
        mrows = [[_sb(nc, ph, "mrow%d_%d" % (b, i), [128, 1024], F32) for i in range(4)] for b in range(2)]
        tti = [0]
        blocks = [(bi, c0, N) for bi, (c0, N) in enumerate(BLOCKS) if not (bi == 0 and last)]

        def stage_t(ix):
            bi, c0, N = blocks[ix]
            z, mrow = zs[ix % 2], mrows[ix % 2]
            j = 2 if bi == 0 else s
            k.dma("sp", lambda e: e.dma_start(out=z.t[:, :, 0:N], in_=dr["x1res"][s, :, c0:c0 + N].rearrange("(kc p) n -> p kc n", p=128)),
                  reads=k.dall("x1", s), writes=[z.b])
            ntk = N // 128
            for tk in range(ntk):
                r0 = s * TT + c0 + tk * 128
                k.dma("sp", lambda e, tk=tk, r0=r0: e.dma_start(out=mrow[tk].t[:, :], in_=dr["moe_d"][r0:r0 + 128, :]),
                      reads=k.dall("moez") + k.dall("moeacc"), writes=[mrow[tk].b])
            for oc in range(8):
                pm = g.ps[oc % 4]
                for tk in range(ntk):
                    k.op("pe", lambda e, tk=tk, oc=oc, pm=pm: e.transpose(pm.t[:, tk * 128:(tk + 1) * 128], mrow[tk].t[:, oc * 128:(oc + 1) * 128], g.ident_f.t[:, :]),
                         reads=[mrow[tk].b, g.ident_f.b], writes=[pm.b])
                t1 = tts[tti[0] % 2]
                tti[0] += 1
                k.op("act", lambda e, pm=pm, t1=t1, oc=oc: e.activation(out=t1.t[:, 0:N], in_=pm.t[:, 0:N], func=AF.Copy, scale=g.mod.t[:, l, 40 + oc, j:j + 1]),
                     reads=[pm.b, g.mod.b], writes=[t1.b])
                k.op("dve", lambda e, t1=t1, oc=oc: e.scalar_tensor_tensor(out=z.t[:, oc, 0:N], in0=z.t[:, oc, 0:N], scalar=ALPHA, in1=t1.t[:, 0:N], op0=ALU.mult, op1=ALU.add),
                     reads=[z.b, t1.b], writes=[z.b])

        def stage_n(ix):
            bi, c0, N = blocks[ix]
            z = zs[ix % 2]
            col_stats(k, g, [z.t[:, oc, 0:N] for oc in range(8)], z.b, N, 1024, LN_EPS, True, sq, mean, rstd, z16)
            for oc in range(8):
                t1 = tts[tti[0] % 2]
                tti[0] += 1
                k.op("dve", lambda e, t1=t1, oc=oc: e.tensor_tensor(out=t1.t[:, 0:N], in0=z.t[:, oc, 0:N], in1=mean.t[:, 0:N], op=ALU.subtract), reads=[z.b, mean.b], writes=[t1.b])
                k.op("dve", lambda e, t1=t1, oc=oc: e.scalar_tensor_tensor(out=t1.t[:, 0:N], in0=t1.t[:, 0:N], scalar=g.vecs.t[:, l, V_LN2G + oc:V_LN2G + oc + 1], in1=rstd.t[:, 0:N],
                                                                          op0=ALU.mult, op1=ALU.mult), reads=[t1.b, rstd.b, g.vecs.b], writes=[t1.b])
                k.op("act", lambda e, t1=t1, oc=oc: e.activation(out=z.t[:, oc, 0:N], in_=t1.t[:, 0:N], func=AF.Identity, bias=g.vecs.t[:, l, V_LN2B + oc:V_LN2B + oc + 1], scale=1.0),
                     reads=[t1.b, g.vecs.b], writes=[z.b])
            if last:
                k.dma("sp", lambda e: e.dma_start(out=dr["outT"][s, :, c0 - C:c0 - C + N].rearrange("(kc p) n -> p kc n", p=128), in_=z.t[:, :, 0:N]),
                      reads=[z.b], writes=[k.dbuf("out", s, bi)])
            else:
                k.dma("sp", lambda e: e.dma_start(out=dr["xres"][s, :, c0:c0 + N].rearrange("(kc p) n -> p kc n", p=128), in_=z.t[:, :, 0:N]),
                      reads=[z.b], writes=[k.dbuf("x", s, bi)])

        stage_t(0)
        for ix in range(len(blocks)):
            if ix + 1 < len(blocks):
                stage_t(ix + 1)
            stage_n(ix)
        k.barrier()


def layer(nc, k, g, l, dr):
    stop = dr.get("stop_after")
    for s in range(NS):
        _plog(k, "L%d s%d proj" % (l, s))
        phase_proj(nc, k, g, l, s, dr)
        if stop == "proj":
            continue
        _plog(k, "L%d s%d mla_prep" % (l, s))
        phase_mla_prep(nc, k, g, l, s, dr)
        if stop == "mla_prep":
            return
        _plog(k, "L%d s%d na" % (l, s))
        phase_na(nc, k, g, l, s, dr)
        if stop == "na":
            return
        _plog(k, "L%d s%d mla" % (l, s))
        phase_mla(nc, k, g, l, s, dr)
        if stop == "mla":
            return
    if stop in ("proj", "mix"):
        return
    for s in range(NS):
        _plog(k, "L%d s%d merge" % (l, s))
        phase_merge(nc, k, g, l, s, dr)
    if stop == "merge":
        return
    _plog(k, "L%d moe" % l)
    phase_moe(nc, k, g, l, dr)
    if stop == "moe":
        return
    for s in range(NS):
        _plog(k, "L%d s%d ln2" % (l, s))
        phase_ln2(nc, k, g, l, s, dr)
    _plog(k, "L%d end" % l)


def _prep_inputs(inputs):
    f = lambda a: np.ascontiguousarray(np.asarray(a, dtype=np.float32))
    x, c, ctx, c_ctx = f(inputs["x"]), f(inputs["c"]), f(inputs["ctx"]), f(inputs["c_ctx"])
    shared = {}
    for nm in ("w_ada", "w_in", "w_uq", "w_ukv", "w_pw2", "w_oa", "w_ob", "w_out", "w_router", "w_gate", "w_up", "w_down"):
        shared[nm] = f(inputs[nm])
    w_in = shared["w_in"]
    kr = w_in[:, :, 1920:1952]
    sw = np.concatenate([np.arange(8, 16), np.arange(0, 8), np.arange(24, 32), np.arange(16, 24)])
    w_kr = np.zeros((L, D, 192), np.float32)
    w_kr[:, :, 64:96] = kr
    w_kr[:, :, 160:192] = kr[:, :, sw]
    shared["w_kr"] = w_kr
    wq = shared["w_uq"].reshape(L, 256, 8, 96)
    wqs = wq.copy()
    wqs[:, :, :, 64:96] = wq[:, :, :, 64:96][:, :, :, sw]
    shared["w_uq_sw"] = np.ascontiguousarray(wqs.reshape(L, 256, 768))
    vecs = np.zeros((L, 128, NV), np.float32)

    def put(col, v, nch):
        vecs[:, :, col:col + nch] = f(v).reshape(L, nch, 128).transpose(0, 2, 1)

    put(V_BADA, inputs["b_ada"], 48)
    put(V_LN1G, inputs["ln1_g"], 8)
    put(V_LN1B, inputs["ln1_b"], 8)
    put(V_LN2G, inputs["ln2_g"], 8)
    put(V_LN2B, inputs["ln2_b"], 8)
    put(V_GQ, inputs["g_q"], 2)
    put(V_GKV, inputs["g_kv"], 1)
    put(V_BDW, inputs["b_dw"], 4)
    put(V_GCN, inputs["g_cn"], 4)
    put(V_BCN, inputs["b_cn"], 4)
    wdw = f(inputs["w_dw"])
    vecs[:, :, V_WDW:V_WDW + 124] = wdw.reshape(L, 31, 4, 128).transpose(0, 3, 2, 1).reshape(L, 128, 124)
    shared["vecs"] = vecs
    shared["ident"] = np.eye(128, dtype=np.float32)
    t = np.arange(T)
    row = (t // 64).astype(np.float32)
    col = (t % 64).astype(np.float32)
    inv = (np.float32(10000.0) ** (-np.arange(8, dtype=np.float32) / np.float32(8))).astype(np.float32)
    ar = row[None, :] * inv[:, None]
    ac = col[None, :] * inv[:, None]
    Cc = np.ones((32, TT), np.float32)
    Ss = np.zeros((32, TT), np.float32)
    Cc[0:8, C:] = np.cos(ar); Cc[8:16, C:] = np.cos(ar); Cc[16:24, C:] = np.cos(ac); Cc[24:32, C:] = np.cos(ac)
    Ss[0:8, C:] = -np.sin(ar); Ss[8:16, C:] = np.sin(ar); Ss[16:24, C:] = -np.sin(ac); Ss[24:32, C:] = np.sin(ac)
    shared["ropeC"] = Cc
    shared["ropeS"] = Ss
    rpb = f(inputs["rpb"])
    rpb_ext = np.concatenate([rpb, np.full((L, 8, 15, 1), NEG, np.float32)], axis=-1)
    qc = np.arange(64)[:, None]
    kc = np.arange(64)[None, :]
    cstart = np.clip(qc - 8, 0, 48)
    valid = (kc >= cstart) & (kc < cstart + 16)
    idx = np.clip(kc - qc, -15, 15) + 15
    idx = np.where(valid, idx, 31)
    bt = rpb_ext[:, :, :, idx]
    bt = bt.transpose(0, 3, 1, 2, 4).reshape(L, 64, 7680)
    shared["biasT"] = np.ascontiguousarray(np.concatenate([bt, bt], axis=1))
    in_maps = []
    for i in range(NCORES):
        s0 = i * NS
        xT = np.empty((NS, D, TT), np.float32)
        for j in range(NS):
            xT[j, :, :C] = ctx[s0 + j].T
            xT[j, :, C:] = x[s0 + j].T
        cT = np.empty((128, 8, 3), np.float32)
        for j in range(NS):
            cT[:, :, j] = c[s0 + j].reshape(8, 128).T
        cT[:, :, 2] = c_ctx.reshape(8, 128).T
        m = dict(shared)
        m["xT"] = xT
        m["cT"] = cT
        in_maps.append(m)
    return in_maps


_NC_CACHE = {}


def kernel(**inputs):
    in_maps = _prep_inputs(inputs)
    if "nc" not in _NC_CACHE:
        _NC_CACHE["nc"] = build()
    nc = _NC_CACHE["nc"]
    res = run_bass_kernel_spmd(nc, in_maps, core_ids=list(range(NCORES)))
    out = np.empty((NCORES * NS, T, D), np.float32)
    for i in range(NCORES):
        o = res.results[i]["outT"]
        for j in range(NS):
            out[i * NS + j] = o[j].T
    return out
```
